# Optimizing a Trainium2 kernel written in Bass

```python
import math
import jax, jax.numpy as jnp
from jax import lax
import numpy as np

D_MODEL = 4096
BATCH = 4
SEQ = 2048
DEPTH = 2
DEC_BATCH = 128
DEC_SEQ = 8
PAST_LEN = 16384
PAGE_SIZE = 128

GLA_HEADS = 4
GLA_DK = D_MODEL // 4
GLA_DV = D_MODEL // 2
GLA_HK = GLA_DK // GLA_HEADS
GLA_HV = GLA_DV // GLA_HEADS
GLA_RANK = 16
GLA_TAU = 16.0
GLA_CHUNK = 64
RMS_EPS = 1e-6

S5_WIDTH = D_MODEL // 2
S5_GROUP = 16
S5_GROUPS = S5_WIDTH // S5_GROUP
S5_STATE = 64
S5_DT_MIN = 1e-3
S5_DT_MAX = 1e-1

PEER_KEYS = 128
PEER_EXPERTS = PEER_KEYS * PEER_KEYS
PEER_HEADS = 8
PEER_TOPK = 16
PEER_QDIM = 256
PEER_HALF = PEER_QDIM // 2
PEER_BLOCK = 128

DN_ALPHA = (2 * DEPTH) ** 0.25
DN_BETA = (8 * DEPTH) ** -0.25
LN_EPS = 1e-5

IN_WIDTHS = (GLA_DK, GLA_DK, GLA_DV, GLA_DV, GLA_RANK, S5_WIDTH, D_MODEL, D_MODEL)
W_IN_COLS = sum(IN_WIDTHS)
IN_OFFSETS = tuple(int(v) for v in np.cumsum(IN_WIDTHS)[:-1])

kernel_name = "gla_s5_peer_deepnorm_decoder_step"


def layer_norm(x, g, b):
    xf = x.astype(jnp.float32)
    mu = jnp.mean(xf, -1, keepdims=True)
    var = jnp.mean(jnp.square(xf - mu), -1, keepdims=True)
    y = (xf - mu) * lax.rsqrt(var + LN_EPS) * g.astype(jnp.float32) + b.astype(jnp.float32)
    return y.astype(x.dtype)


def gla_recurrence(q, k, v, log_a, s0):
    bsz, L, H = q.shape[:3]
    dv = v.shape[-1]
    c = min(GLA_CHUNK, L)
    n = -(-L // c)
    pad = n * c - L

    def prep(t):
        t = jnp.pad(t.astype(jnp.float32), ((0, 0), (0, pad), (0, 0), (0, 0)))
        return t.reshape(bsz, n, c, H, t.shape[-1]).transpose(1, 0, 3, 2, 4)

    qc, kc, vc, ac = prep(q), prep(k), prep(v), prep(log_a)
    mask = jnp.tril(jnp.ones((c, c), dtype=bool))

    def step(s, inp):
        qi, ki, vi, ai = inp
        b = jnp.cumsum(ai, axis=2)
        b_last = b[:, :, -1:, :]
        q_dec = qi * jnp.exp(b)
        k_inv = ki * jnp.exp(-b)
        k_end = ki * jnp.exp(b_last - b)
        scores = jnp.where(mask, jnp.einsum('bhtd,bhsd->bhts', q_dec, k_inv), 0.0)
        o = jnp.einsum('bhtd,bhdv->bhtv', q_dec, s) + jnp.einsum('bhts,bhsv->bhtv', scores, vi)
        s_new = jnp.exp(b_last[:, :, 0, :])[..., None] * s + jnp.einsum('bhsd,bhsv->bhdv', k_end, vi)
        return s_new, o

    s_fin, o = lax.scan(step, s0.astype(jnp.float32), (qc, kc, vc, ac))
    o = o.transpose(1, 0, 3, 2, 4).reshape(bsz, n * c, H, dv)[:, :L]
    return o, s_fin


def gla_branch(q, k, v, r, a_lr, w_a2, b_a, norm_g, s0):
    bsz, L = q.shape[:2]
    log_a = jax.nn.log_sigmoid((a_lr @ w_a2 + b_a).astype(jnp.float32)) / GLA_TAU
    qh = q.reshape(bsz, L, GLA_HEADS, GLA_HK).astype(jnp.float32) * (GLA_HK ** -0.5)
    kh = k.reshape(bsz, L, GLA_HEADS, GLA_HK)
    vh = v.reshape(bsz, L, GLA_HEADS, GLA_HV)
    ah = log_a.reshape(bsz, L, GLA_HEADS, GLA_HK)
    o, s_new = gla_recurrence(qh, kh, vh, ah, s0)
    o = o * lax.rsqrt(jnp.mean(jnp.square(o), -1, keepdims=True) + RMS_EPS) * norm_g.astype(jnp.float32)
    o = o.reshape(bsz, L, GLA_DV).astype(r.dtype) * jax.nn.silu(r)
    return o, s_new


def s5_branch(u, a_re, a_im, log_dt, b_re, b_im, c_re, c_im, d_skip, w_glu, b_glu, s0_re, s0_im):
    bsz, L = u.shape[:2]
    f32 = jnp.float32
    ug = u.reshape(bsz, L, S5_GROUPS, S5_GROUP).astype(f32)
    lam = lax.complex(a_re.astype(f32), a_im.astype(f32))
    dt = jnp.exp(log_dt.astype(f32))[:, None]
    a_bar = jnp.exp(lam * dt)
    b_c = lax.complex(b_re.astype(f32), b_im.astype(f32))
    b_bar = ((a_bar - 1.0) / lam)[..., None] * b_c
    bu = jnp.einsum('gpn,blgn->blgp', b_bar, ug.astype(jnp.complex64))
    a_seq = jnp.broadcast_to(a_bar, bu.shape)

    def combine(e1, e2):
        a1, x1 = e1
        a2, x2 = e2
        return a2 * a1, a2 * x1 + x2

    a_cum, s = lax.associative_scan(combine, (a_seq, bu), axis=1)
    s0 = lax.complex(s0_re.astype(f32), s0_im.astype(f32))
    s = s + a_cum * s0[:, None]
    c_c = lax.complex(c_re.astype(f32), c_im.astype(f32))
    y = jnp.einsum('gnp,blgp->blgn', c_c, s).real + d_skip.astype(f32).reshape(S5_GROUPS, S5_GROUP) * ug
    y = jax.nn.gelu(y.reshape(bsz, L, S5_WIDTH))
    y = y * jax.nn.sigmoid(y @ w_glu.astype(f32) + b_glu.astype(f32))
    s_last = s[:, -1]
    return y.astype(u.dtype), s_last.real, s_last.imag


def mixer_block(x, p, s_gla, s_re, s_im):
    proj = x @ p["w_in"]
    q, k, v, r, a_lr, u, z_gla, z_s5 = jnp.split(proj, IN_OFFSETS, axis=-1)
    o_gla, s_gla_new = gla_branch(q, k, v, r, a_lr, p["gla_w_a2"], p["gla_b_a"], p["gla_norm_g"], s_gla)
    o_s5, s_re_new, s_im_new = s5_branch(u, p["s5_a_re"], p["s5_a_im"], p["s5_log_dt"], p["s5_b_re"],
                                         p["s5_b_im"], p["s5_c_re"], p["s5_c_im"], p["s5_d"],
                                         p["s5_w_glu"], p["s5_b_glu"], s_re, s_im)
    w_br = p["w_branch"]
    merged = (jax.nn.sigmoid(z_gla) * (o_gla @ w_br[:GLA_DV])
              + jax.nn.sigmoid(z_s5) * (o_s5 @ w_br[GLA_DV:]))
    return merged @ p["w_out"], s_gla_new, s_re_new, s_im_new


def peer_ffn(x, w_q, key1, key2, u_tab, v_tab):
    lead = x.shape[:-1]
    xt = x.reshape(-1, D_MODEL)
    T = xt.shape[0]
    f32 = jnp.float32
    q = (xt @ w_q).reshape(T, PEER_HEADS, 2, PEER_HALF).astype(f32)
    s1 = jnp.einsum('thd,kd->thk', q[:, :, 0], key1.astype(f32))
    s2 = jnp.einsum('thd,kd->thk', q[:, :, 1], key2.astype(f32))
    v1, i1 = lax.top_k(s1, PEER_TOPK)
    v2, i2 = lax.top_k(s2, PEER_TOPK)
    cand = (v1[..., :, None] + v2[..., None, :]).reshape(T, PEER_HEADS, PEER_TOPK * PEER_TOPK)
    cand_idx = (i1[..., :, None] * PEER_KEYS + i2[..., None, :]).reshape(T, PEER_HEADS, PEER_TOPK * PEER_TOPK)
    top_s, pos = lax.top_k(cand, PEER_TOPK)
    idx = jnp.take_along_axis(cand_idx, pos, axis=-1)
    gate = jax.nn.softmax(top_s, axis=-1)
    HK = PEER_HEADS * PEER_TOPK
    nb = -(-T // PEER_BLOCK)
    pad = nb * PEER_BLOCK - T
    xb = jnp.pad(xt, ((0, pad), (0, 0))).reshape(nb, PEER_BLOCK, D_MODEL)
    ib = jnp.pad(idx.reshape(T, HK), ((0, pad), (0, 0))).reshape(nb, PEER_BLOCK, HK)
    gb = jnp.pad(gate.reshape(T, HK), ((0, pad), (0, 0))).reshape(nb, PEER_BLOCK, HK)

    def block(args):
        xi, ii, gi = args
        u_sel = u_tab[ii]
        h = jax.nn.gelu(jnp.einsum('td,ted->te', xi, u_sel).astype(f32))
        v_sel = v_tab[ii]
        return jnp.einsum('te,ted->td', (gi * h).astype(v_sel.dtype), v_sel)

    y = lax.map(block, (xb, ib, gb)).reshape(nb * PEER_BLOCK, D_MODEL)[:T]
    return y.reshape(*lead, D_MODEL).astype(x.dtype)


def forward(x, st_gla, st_re, st_im, w):
    new_gla, new_re, new_im = [], [], []
    for l in range(DEPTH):
        p = {name: arr[l] for name, arr in w.items()}
        mix, s_gla, s_re, s_im = mixer_block(x, p, st_gla[l], st_re[l], st_im[l])
        x = layer_norm(DN_ALPHA * x + mix, p["ln1_g"], p["ln1_b"])
        ffn = peer_ffn(x, p["peer_w_q"], p["peer_key1"], p["peer_key2"], p["peer_u"], p["peer_v"])
        x = layer_norm(DN_ALPHA * x + ffn, p["ln2_g"], p["ln2_b"])
        new_gla.append(s_gla)
        new_re.append(s_re)
        new_im.append(s_im)
    return x, jnp.stack(new_gla), jnp.stack(new_re), jnp.stack(new_im)


def setup_inputs(seed: int = 0) -> dict:
    key = jax.random.key(seed)
    ks = jax.random.split(key, 32)
    nrm = lambda k, shape, s: jax.random.normal(k, shape, jnp.float32) * s
    L_ = DEPTH
    inp = {}
    inp["x_prompt"] = nrm(ks[0], (BATCH, SEQ, D_MODEL), 1.0)
    inp["x_sample"] = nrm(ks[1], (DEC_BATCH, DEC_SEQ, D_MODEL), 1.0)
    inp["state_gla"] = nrm(ks[2], (L_, DEC_BATCH, GLA_HEADS, GLA_HK, GLA_HV), 1.0)
    inp["state_s5_re"] = nrm(ks[3], (L_, DEC_BATCH, S5_GROUPS, S5_STATE), 1.0)
    inp["state_s5_im"] = nrm(ks[4], (L_, DEC_BATCH, S5_GROUPS, S5_STATE), 1.0)
    inp["ln1_g"] = 1.0 + nrm(ks[5], (L_, D_MODEL), 0.02)
    inp["ln1_b"] = nrm(ks[6], (L_, D_MODEL), 0.02)
    inp["w_in"] = nrm(ks[7], (L_, D_MODEL, W_IN_COLS), D_MODEL ** -0.5)
    inp["gla_w_a2"] = nrm(ks[8], (L_, GLA_RANK, GLA_DK), GLA_RANK ** -0.5)
    inp["gla_b_a"] = nrm(ks[9], (L_, GLA_DK), 0.1)
    inp["gla_norm_g"] = 1.0 + nrm(ks[10], (L_, GLA_HV), 0.02)
    inp["s5_a_re"] = -0.5 + nrm(ks[11], (L_, S5_GROUPS, S5_STATE), 0.01)
    inp["s5_a_im"] = (jnp.pi * jnp.arange(S5_STATE, dtype=jnp.float32))[None, None, :] + nrm(ks[12], (L_, S5_GROUPS, S5_STATE), 0.01)
    inp["s5_log_dt"] = jax.random.uniform(ks[13], (L_, S5_GROUPS), jnp.float32, math.log(S5_DT_MIN), math.log(S5_DT_MAX))
    inp["s5_b_re"] = nrm(ks[14], (L_, S5_GROUPS, S5_STATE, S5_GROUP), (2 * S5_GROUP) ** -0.5)
    inp["s5_b_im"] = nrm(ks[15], (L_, S5_GROUPS, S5_STATE, S5_GROUP), (2 * S5_GROUP) ** -0.5)
    inp["s5_c_re"] = nrm(ks[16], (L_, S5_GROUPS, S5_GROUP, S5_STATE), (2 * S5_STATE) ** -0.5)
    inp["s5_c_im"] = nrm(ks[17], (L_, S5_GROUPS, S5_GROUP, S5_STATE), (2 * S5_STATE) ** -0.5)
    inp["s5_d"] = nrm(ks[18], (L_, S5_WIDTH), 1.0)
    inp["s5_w_glu"] = nrm(ks[19], (L_, S5_WIDTH, S5_WIDTH), S5_WIDTH ** -0.5)
    inp["s5_b_glu"] = nrm(ks[20], (L_, S5_WIDTH), 0.01)
    inp["w_branch"] = nrm(ks[21], (L_, GLA_DV + S5_WIDTH, D_MODEL), GLA_DV ** -0.5)
    inp["w_out"] = nrm(ks[22], (L_, D_MODEL, D_MODEL), DN_BETA * D_MODEL ** -0.5)
    inp["ln2_g"] = 1.0 + nrm(ks[23], (L_, D_MODEL), 0.02)
    inp["ln2_b"] = nrm(ks[24], (L_, D_MODEL), 0.02)
    inp["peer_w_q"] = nrm(ks[25], (L_, D_MODEL, PEER_HEADS * PEER_QDIM), D_MODEL ** -0.5)
    inp["peer_key1"] = nrm(ks[26], (L_, PEER_KEYS, PEER_HALF), PEER_HALF ** -0.5)
    inp["peer_key2"] = nrm(ks[27], (L_, PEER_KEYS, PEER_HALF), PEER_HALF ** -0.5)
    inp["peer_u"] = nrm(ks[28], (L_, PEER_EXPERTS, D_MODEL), D_MODEL ** -0.5)
    inp["peer_v"] = nrm(ks[29], (L_, PEER_EXPERTS, D_MODEL), DN_BETA * PEER_HEADS ** -0.5)
    return inp


def reference(x_prompt, x_sample, state_gla, state_s5_re, state_s5_im,
              ln1_g, ln1_b, w_in, gla_w_a2, gla_b_a, gla_norm_g,
              s5_a_re, s5_a_im, s5_log_dt, s5_b_re, s5_b_im, s5_c_re, s5_c_im,
              s5_d, s5_w_glu, s5_b_glu, w_branch, w_out, ln2_g, ln2_b,
              peer_w_q, peer_key1, peer_key2, peer_u, peer_v):
    w = dict(ln1_g=ln1_g, ln1_b=ln1_b, w_in=w_in, gla_w_a2=gla_w_a2, gla_b_a=gla_b_a,
             gla_norm_g=gla_norm_g, s5_a_re=s5_a_re, s5_a_im=s5_a_im, s5_log_dt=s5_log_dt,
             s5_b_re=s5_b_re, s5_b_im=s5_b_im, s5_c_re=s5_c_re, s5_c_im=s5_c_im, s5_d=s5_d,
             s5_w_glu=s5_w_glu, s5_b_glu=s5_b_glu, w_branch=w_branch, w_out=w_out,
             ln2_g=ln2_g, ln2_b=ln2_b, peer_w_q=peer_w_q, peer_key1=peer_key1,
             peer_key2=peer_key2, peer_u=peer_u, peer_v=peer_v)
    pb = x_prompt.shape[0]
    zero_gla = jnp.zeros((DEPTH, pb, GLA_HEADS, GLA_HK, GLA_HV), jnp.float32)
    zero_s5 = jnp.zeros((DEPTH, pb, S5_GROUPS, S5_STATE), jnp.float32)
    y_prompt, gla_p, re_p, im_p = forward(x_prompt, zero_gla, zero_s5, zero_s5, w)
    y_sample, gla_s, re_s, im_s = forward(x_sample, state_gla, state_s5_re, state_s5_im, w)
    return (y_prompt, y_sample, gla_p, re_p, im_p, gla_s, re_s, im_s)
```

```python
import math
from contextlib import ExitStack
import numpy as np
import concourse.bass as bass
import concourse.mybir as mybir
from concourse.bass_utils import run_bass_kernel_spmd

F32 = mybir.dt.float32
BF16 = mybir.dt.bfloat16
I32 = mybir.dt.int32
ALU = mybir.AluOpType
AF = mybir.ActivationFunctionType
AX = mybir.AxisListType

ENGS = ("pe", "act", "dve", "pool", "sp")
NDSEM = 8
D = 4096
ALPHA = 4.0 ** 0.25
TWO_PI = 2.0 * math.pi


class Buf:
    __slots__ = ("w", "r")

    def __init__(self):
        self.w = None
        self.r = []


class TL:
    __slots__ = ("ap", "b")

    def __init__(self, ap):
        self.ap = ap
        self.b = Buf()


class Op:
    __slots__ = ("eng", "fn", "deps", "dma", "needed", "sig", "dsem", "dval", "idx")

    def __init__(self, eng, fn, dma):
        self.eng = eng
        self.fn = fn
        self.dma = dma
        self.deps = []
        self.needed = False
        self.sig = 0
        self.dsem = None
        self.dval = 0
        self.idx = 0


class Sched:
    def __init__(self, nc):
        self.nc = nc
        self.ops = {e: [] for e in ENGS}
        self.ndma = {e: 0 for e in ENGS}
        self.lastc = {}
        self.dmas = []

    def op(self, eng, fn, reads=(), writes=(), dma=False):
        o = Op(eng, fn, dma)
        deps = []
        for t in reads:
            if t.b.w is not None:
                deps.append(t.b.w)
        for t in writes:
            if t.b.w is not None:
                deps.append(t.b.w)
            deps.extend(t.b.r)
        seen = set()
        for d in deps:
            if id(d) in seen:
                continue
            seen.add(id(d))
            if eng == "pe" and d.eng == "pe" and not d.dma and not dma:
                continue
            o.deps.append(d)
        for t in reads:
            t.b.r.append(o)
        for t in writes:
            t.b.w = o
            t.b.r = []
        if dma:
            o.idx = self.ndma[eng]
            self.ndma[eng] += 1
            self.dmas.append(o)
        else:
            self.lastc[eng] = o
        self.ops[eng].append(o)
        return o

    def dma(self, eng, out, in_, reads=(), writes=(), **kw):
        return self.op(eng, lambda e: e.dma_start(out=out, in_=in_, **kw), reads, writes, dma=True)

    def fence(self):
        deps = list(self.lastc.values()) + self.dmas
        self.dmas = []
        for e in ENGS:
            o = Op(e, None, False)
            o.deps = list(deps)
            self.ops[e].append(o)

    def emit(self):
        nc = self.nc
        self.fence()
        for e in ENGS:
            for o in self.ops[e]:
                for d in o.deps:
                    d.needed = True
        for e in ENGS:
            c = 0
            for o in self.ops[e]:
                if o.fn is not None and not o.dma and o.needed:
                    c += 1
                    o.sig = c
        with ExitStack() as st:
            csem = {e: st.enter_context(nc.semaphore("c_" + e)) for e in ENGS if e != "sp"}
            dsem = {e: [st.enter_context(nc.semaphore("d_%s%d" % (e, i))) for i in range(NDSEM)]
                    for e in ("sp", "act", "pool")}
            for e in ENGS:
                for o in self.ops[e]:
                    if o.dma:
                        o.dsem = dsem[e][o.idx % NDSEM]
                        o.dval = 16 * (o.idx // NDSEM + 1)
            block = st.enter_context(nc.Block())
            engobj = {"pe": block.tensor, "act": block.scalar, "dve": block.vector,
                      "pool": block.gpsimd, "sp": block.sync}

            def mk(e):
                def body(eng):
                    waited = {}

                    def wait(sem, val):
                        if waited.get(id(sem), 0) >= val:
                            return
                        waited[id(sem)] = val
                        eng.wait_ge(sem, val)

                    for o in self.ops[e]:
                        for d in o.deps:
                            if d.dma:
                                wait(d.dsem, d.dval)
                            else:
                                wait(csem[d.eng], d.sig)
                        if o.fn is None:
                            continue
                        if o.dma and o.dval > 16:
                            wait(o.dsem, o.dval - 16)
                        ins = o.fn(eng)
                        if o.dma:
                            ins.then_inc(o.dsem, 16)
                        elif o.needed:
                            ins.then_inc(csem[e], 1)
                return body

            for e in ENGS:
                engobj[e](mk(e))


class Arena:
    def __init__(self, ap):
        self.ap = ap
        self.p = 0
        self.n = ap.shape[1]

    def f32(self, n, parts=128):
        a = self.ap[0:parts, self.p:self.p + n]
        self.p += n
        assert self.p <= self.n, "SBUF arena overflow %d > %d" % (self.p, self.n)
        return TL(a)

    def bf(self, n, parts=128):
        m = (n + 1) // 2
        a = self.ap[0:parts, self.p:self.p + m].bitcast(BF16)[:, 0:n]
        self.p += m
        assert self.p <= self.n, "SBUF arena overflow %d > %d" % (self.p, self.n)
        return TL(a)

    def i32(self, n, parts=128):
        a = self.ap[0:parts, self.p:self.p + n].bitcast(I32)
        self.p += n
        assert self.p <= self.n
        return TL(a)


ONLY = None
CUT = None
WSHAPES = [
    ("ln1_g", (2, D)), ("ln1_b", (2, D)), ("w_in", (2, D, 16400)), ("gla_w_a2", (2, 16, 1024)),
    ("gla_b_a", (2, 1024)), ("gla_norm_g", (2, 512)), ("s5_a_re", (2, 128, 64)), ("s5_a_im", (2, 128, 64)),
    ("s5_log_dt", (2, 128)), ("s5_b_re", (2, 128, 64, 16)), ("s5_b_im", (2, 128, 64, 16)),
    ("s5_c_re", (2, 128, 16, 64)), ("s5_c_im", (2, 128, 16, 64)), ("s5_d", (2, 2048)),
    ("s5_w_glu", (2, 2048, 2048)), ("s5_b_glu", (2, 2048)), ("w_branch", (2, D, D)), ("w_out", (2, D, D)),
    ("ln2_g", (2, D)), ("ln2_b", (2, D)), ("peer_w_q", (2, D, 2048)), ("peer_key1", (2, 128, 128)),
    ("peer_key2", (2, 128, 128)), ("peer_u", (2, 16384, D)), ("peer_v", (2, 16384, D)),
]


def build(NSB, NPT, NLAYER=2, stop_after=None, debug=False):
    TPS = NPT + 1
    TSB = TPS * 128
    T = NSB * TSB
    NS = NSB * 16
    NCP = NPT * 16
    NCB = NCP + 16
    nc = bass.Bass("TRN2", target_bir_lowering=False)
    di = lambda n, s, dt=F32: nc.dram_tensor(n, list(s), dt, kind="ExternalInput").ap()
    do = lambda n, s, dt=F32: nc.dram_tensor(n, list(s), dt, kind="ExternalOutput").ap()
    ds = lambda n, s, dt=F32: nc.dram_tensor(n, list(s), dt, kind=("ExternalOutput" if debug else "Internal")).ap()
    x_in = di("x", [T, D])
    sgla = di("sgla", [2, NS, 4, 256, 512])
    ss5 = [di("ss5re", [2, NS, 128, 64]), di("ss5im", [2, NS, 128, 64])]
    W = {n: di(n, s) for n, s in WSHAPES if ONLY is None or n in ONLY}
    y_out = do("y", [T, D])
    glap = do("glap", [2, 4, 256, 512])
    s5p = [do("s5rep", [2, 128, 64]), do("s5imp", [2, 128, 64])]
    glas = do("glas", [2, NS, 4, 256, 512])
    s5s = [do("s5res", [2, NS, 128, 64]), do("s5ims", [2, NS, 128, 64])]
    X1 = ds("X1", [T, D])
    X2 = ds("X2", [T, D])
    FF = ds("FF", [T, D])
    QK = ds("QK", [T, 2048])
    Vs = ds("Vs", [T, 2048], BF16)
    RS = ds("RS", [T, 2048], BF16)
    ALR = ds("ALR", [T, 16])
    Us = ds("Us", [T, 2048], BF16)
    Z = ds("Z", [T, 8192], BF16)
    OG = ds("OG", [T, 2048], BF16)
    YG = ds("YG", [T, 2048], BF16)
    MM = ds("MM", [T, D], BF16)
    UT = ds("UT", [32, 128, 16384], BF16)
    SMA = ds("SMA", [128, 128, 256], BF16)
    SMC = ds("SMC", [64, 128, 256], BF16)
    GST = ds("GST", [128, 8, 512])

    S = Sched(nc)
    st = ExitStack()
    NA = 50 * 1024
    arena_t = st.enter_context(nc.sbuf_tensor("arena", [128, NA], F32))
    ps_t = st.enter_context(nc.psum_tensor("ps", [128, 4096], F32))
    A = Arena(arena_t[:, :])
    PB = [TL(ps_t[:, i * 512:(i + 1) * 512]) for i in range(8)]
    bank_ctr = [0]

    def nbank(lo=0, hi=8):
        i = lo + bank_ctr[0] % (hi - lo)
        bank_ctr[0] += 1
        return PB[i]

    rr = [0]

    def alt(engs=("dve", "act")):
        rr[0] += 1
        return engs[rr[0] % len(engs)]

    def cp(eng, out, in_, reads, writes):
        if eng == "act":
            return S.op("act", lambda e: e.copy(out=out, in_=in_), reads, writes)
        return S.op(eng, lambda e: e.tensor_copy(out=out, in_=in_), reads, writes)

    def tt(eng, out, in0, in1, op, reads, writes):
        return S.op(eng, lambda e: e.tensor_tensor(out=out, in0=in0, in1=in1, op=op), reads, writes)

    def ts(eng, out, in0, s1, s2, op0, op1, reads, writes):
        if s2 is None:
            return S.op(eng, lambda e: e.tensor_scalar(out=out, in0=in0, scalar1=s1, scalar2=None, op0=op0), reads, writes)
        return S.op(eng, lambda e: e.tensor_scalar(out=out, in0=in0, scalar1=s1, scalar2=s2, op0=op0, op1=op1), reads, writes)

    def stt(eng, out, in0, scalar, in1, op0, op1, reads, writes):
        return S.op(eng, lambda e: e.scalar_tensor_tensor(out=out, in0=in0, scalar=scalar, in1=in1, op0=op0, op1=op1), reads, writes)

    def act(out, in_, func, reads, writes, bias=None, scale=None, accum=None):
        kw = {}
        if bias is not None:
            kw["bias"] = bias
        if scale is not None:
            kw["scale"] = scale
        if accum is not None:
            kw["accum_out"] = accum
        return S.op("act", lambda e: e.activation(out=out, in_=in_, func=func, **kw), reads, writes)

    def mm(out, lhsT, rhs, start, stop, reads, writes):
        return S.op("pe", lambda e: e.matmul(out, lhsT=lhsT, rhs=rhs, start=start, stop=stop), reads, writes)

    def trp(out, in_, ident, reads, writes):
        return S.op("pe", lambda e: e.transpose(out=out, in_=in_, identity=ident), reads, writes)

    def memset(eng, ap, val, writes):
        return S.op(eng, lambda e: e.memset(ap, val), (), writes)

    def asel(ap, pattern, op, base, cm, tl):
        return S.op("pool", lambda e: e.affine_select(out=ap, in_=ap, pattern=pattern, compare_op=op, fill=0.0,
                                                      base=base, channel_multiplier=cm), [tl], [tl])

    ident_b = A.bf(128)
    ident_f = A.f32(128)
    triL = A.f32(128)
    triR = A.f32(128)
    triLs = A.f32(128)
    triRs = A.f32(128)
    same = A.f32(128)
    bmask = A.f32(128)
    seqsel = A.f32(16)
    onescol = A.f32(1)
    epscol = A.f32(2)
    colmask = A.bf(16 * 128)
    colmask_f = A.f32(16 * 128)
    for t_, v_ in ((ident_f, 1.0), (triL, 1.0), (triR, 1.0), (same, 1.0), (bmask, 1.0), (seqsel, 1.0),
                   (onescol, 1.0), (colmask_f, 1.0)):
        memset("pool", t_.ap, v_, [t_])
    memset("pool", epscol.ap[:, 0:1], 1e-5, [epscol])
    memset("pool", epscol.ap[:, 1:2], 1e-6, [epscol])
    asel(ident_f.ap, [[-1, 128]], ALU.is_equal, 0, 1, ident_f)
    asel(triL.ap, [[1, 128]], ALU.is_ge, 0, -1, triL)
    asel(triR.ap, [[-1, 128]], ALU.is_ge, -1, 1, triR)
    sv = same.ap.rearrange("p (j i) -> p j i", j=16)
    asel(sv, [[-8, 16], [0, 8]], ALU.is_ge, 0, 1, same)
    asel(sv, [[8, 16], [0, 8]], ALU.is_ge, 7, -1, same)
    asel(bmask.ap.rearrange("p (t n) -> p t n", t=8), [[16, 8], [0, 16]], ALU.is_ge, 15, -1, bmask)
    asel(seqsel.ap, [[-8, 16]], ALU.is_ge, 0, 1, seqsel)
    asel(seqsel.ap, [[8, 16]], ALU.is_ge, 7, -1, seqsel)
    cmv = colmask_f.ap.rearrange("p (j t) -> p j t", j=16)
    asel(cmv, [[-8, 16], [1, 128]], ALU.is_ge, 0, 0, colmask_f)
    asel(cmv, [[8, 16], [-1, 128]], ALU.is_ge, 7, 0, colmask_f)
    cp("dve", ident_b.ap, ident_f.ap, [ident_f], [ident_b])
    cp("dve", colmask.ap, colmask_f.ap, [colmask_f], [colmask])
    tt("dve", triLs.ap, triL.ap, same.ap, ALU.mult, [triL, same], [triLs])
    tt("dve", triRs.ap, triR.ap, same.ap, ALU.mult, [triR, same], [triRs])
    XST = A.f32(256, parts=64)
    A8 = A.f32(256, parts=64)
    PERS = A.p

    def tr_bf(src_tl, src_ap, nch, dst_tl, dst_ap3, rows=128):
        for k0 in range(0, nch, 8):
            nb = min(8, nch - k0)
            bank = nbank()
            pb = bank.ap.bitcast(BF16)
            for k in range(nb):
                trp(pb[:, k * 128:k * 128 + rows], src_ap[:, (k0 + k) * 128:(k0 + k + 1) * 128],
                    ident_b.ap[0:rows, 0:rows], [src_tl, ident_b], [bank])
            cp(alt(), dst_ap3[:, k0:k0 + nb, :],
               pb[:, 0:nb * 128].rearrange("p (k t) -> p k t", k=nb)[:, :, 0:rows], [bank], [dst_tl])

    def load_T(src_dram, row0, ntiles, ncols, dstT_ap3, dst_tls, fcast):
        bufs = [A.bf(ncols) for _ in range(2)]
        fb = [A.f32(ncols) for _ in range(2)] if fcast else None
        for i in range(ntiles):
            b = bufs[i % 2]
            rows = slice(row0 + i * 128, row0 + (i + 1) * 128)
            if fcast:
                f = fb[i % 2]
                S.dma("sp", f.ap, src_dram[rows, :], writes=[f])
                cp(alt(), b.ap, f.ap, [f], [b])
            else:
                S.dma("sp", b.ap, src_dram[rows, :], writes=[b])
            tr_bf(b, b.ap, ncols // 128, dst_tls[i], dstT_ap3[:, :, i * 128:(i + 1) * 128])

    def tok_blocks(n):
        return [(t0, min(512, n - t0)) for t0 in range(0, n, 512)]

    def tiles_of(t0, n):
        return list(range(t0 // 128, (t0 + n + 127) // 128))

    def stage_XA(l, sb, xsrc):
        S.fence()
        A.p = PERS
        row0 = sb * TSB
        xT_ap = A.bf(32 * TSB).ap.rearrange("p (c t) -> p c t", c=32)
        xT = [TL(xT_ap[:, :, i * 128:(i + 1) * 128]) for i in range(TPS)]
        mark = A.p
        load_T(xsrc, row0, TPS, D, xT_ap, xT, True)
        S.fence()
        A.p = mark
        wb = [A.bf(32 * 512) for _ in range(2)]
        stg = [A.f32(512) for _ in range(4)]
        segs = [(0, 2048, "qk"), (2048, 2048, "v"), (4096, 2048, "r"), (6144, 16, "alr"), (6160, 2048, "u"),
                (8208, 4096, "zg"), (12304, 4096, "zs")]
        blk = 0
        for (c0, ncs, kind) in segs:
            for n0 in range(0, ncs, 512):
                nb = min(512, ncs - n0)
                w = wb[blk % 2]
                w3 = w.ap.rearrange("p (c n) -> p c n", c=32)
                S.dma("pool", w3[:, :, 0:nb],
                      W["w_in"][l, :, c0 + n0:c0 + n0 + nb].rearrange("(c p) n -> p c n", p=128), writes=[w])
                for i in range(TPS):
                    bank = nbank()
                    for c in range(32):
                        mm(bank.ap[:, 0:nb], xT_ap[:, c, i * 128:(i + 1) * 128], w3[:, c, 0:nb], c == 0, c == 31,
                           [xT[i], w], [bank])
                    sg = stg[(blk * TPS + i) % 4]
                    rows = slice(row0 + i * 128, row0 + (i + 1) * 128)
                    sgb = sg.ap.bitcast(BF16)[:, 0:nb]
                    if kind == "qk":
                        cp("dve", sg.ap[:, 0:nb], bank.ap[:, 0:nb], [bank], [sg])
                        S.dma("sp", QK[rows, n0:n0 + nb], sg.ap[:, 0:nb], reads=[sg])
                    elif kind == "alr":
                        cp("dve", sg.ap[:, 0:nb], bank.ap[:, 0:nb], [bank], [sg])
                        S.dma("sp", ALR[rows, 0:16], sg.ap[:, 0:16], reads=[sg])
                    elif kind == "v":
                        cp("dve", sgb, bank.ap[:, 0:nb], [bank], [sg])
                        S.dma("sp", Vs[rows, n0:n0 + nb], sgb, reads=[sg])
                    elif kind == "u":
                        cp("dve", sgb, bank.ap[:, 0:nb], [bank], [sg])
                        S.dma("sp", Us[rows, n0:n0 + nb], sgb, reads=[sg])
                    elif kind == "r":
                        act(sgb, bank.ap[:, 0:nb], AF.Silu, [bank], [sg])
                        S.dma("sp", RS[rows, n0:n0 + nb], sgb, reads=[sg])
                    else:
                        zo = n0 if kind == "zg" else 4096 + n0
                        act(sgb, bank.ap[:, 0:nb], AF.Sigmoid, [bank], [sg])
                        S.dma("sp", Z[rows, zo:zo + nb], sgb, reads=[sg])
                blk += 1

    def stage_G(l, sb):
        S.fence()
        A.p = PERS
        row0 = sb * TSB
        Sf = A.f32(8 * 512)
        Sb = A.bf(8 * 512)
        Sf3 = Sf.ap.rearrange("p (k v) -> p k v", k=8)
        Sb3 = Sb.ap.rearrange("p (k v) -> p k v", k=8)
        Sft = [TL(Sf3[:, k, :]) for k in range(8)]
        Sbt = [TL(Sb3[:, k, :]) for k in range(8)]
        waug = A.f32(1024)
        aug = A.f32(128)
        gt = A.f32(512)
        if sb == 0:
            memset("dve", Sf.ap, 0.0, Sft)
        else:
            S.dma("sp", Sf3, GST[:, :, :], writes=Sft)
        for k in range(8):
            cp("act", Sb3[:, k, :], Sf3[:, k, :], [Sft[k]], [Sbt[k]])
        memset("dve", waug.ap[0:33, :], 0.0, [waug])
        memset("dve", aug.ap[0:33, :], 0.0, [aug])
        memset("dve", aug.ap[32:33, :], 1.0, [aug])
        S.dma("sp", waug.ap[0:16, :], W["gla_w_a2"][l, :, :], writes=[waug])
        S.dma("sp", waug.ap[32:33, :], W["gla_b_a"][l:l + 1, :], writes=[waug])
        S.dma("sp", gt.ap, W["gla_norm_g"][l, :].partition_broadcast(128), writes=[gt])
        qkb = [A.f32(2048) for _ in range(2)]
        vbf = [A.bf(2048) for _ in range(2)]
        rbf = [A.bf(2048) for _ in range(2)]
        alrb = [A.f32(16) for _ in range(2)]
        la = A.f32(1024)
        eb = A.f32(1024)
        enb = A.f32(1024)
        ee = A.f32(1024)
        ebl = A.f32(128)
        qd = A.bf(1024)
        ki = A.bf(1024)
        ke = A.bf(1024)
        qdT = A.bf(1024)
        kiT = A.bf(1024)
        qdT3 = qdT.ap.rearrange("p (k t) -> p k t", k=8)
        kiT3 = kiT.ap.rearrange("p (k t) -> p k t", k=8)
        scm = [A.bf(128) for _ in range(2)]
        qm = [A.bf(2 * 16 * 128) for _ in range(2)]
        sjb = [A.bf(1024) for _ in range(3)]
        kem = [A.bf(1024) for _ in range(2)]
        sjf = [A.f32(1024) for _ in range(2)]
        sjo = [A.f32(1024) for _ in range(2)]
        on = A.f32(512)
        junk = A.f32(512)
        scf = A.f32(128)
        obs = A.f32(512)
        stmp = [A.f32(512) for _ in range(2)]
        ogb = [A.bf(2048) for _ in range(2)]
        ss = A.f32(16)
        for i in range(TPS):
            sample = (i == NPT)
            rows = slice(row0 + i * 128, row0 + (i + 1) * 128)
            qk = qkb[i % 2]
            v = vbf[i % 2]
            r = rbf[i % 2]
            alr = alrb[i % 2]
            og = ogb[i % 2]
            S.dma("sp", qk.ap, QK[rows, :], writes=[qk])
            S.dma("sp", v.ap, Vs[rows, :], writes=[v])
            S.dma("sp", r.ap, RS[rows, :], writes=[r])
            S.dma("sp", alr.ap, ALR[rows, :], writes=[alr])
            bk = nbank()
            trp(bk.ap[0:16, 0:128], alr.ap, ident_f.ap, [alr, ident_f], [bk])
            cp("dve", aug.ap[0:16, :], bk.ap[0:16, 0:128], [bk], [aug])
            b0, b1 = nbank(), nbank()
            mm(b0.ap, aug.ap[0:33, :], waug.ap[0:33, 0:512], True, True, [aug, waug], [b0])
            mm(b1.ap, aug.ap[0:33, :], waug.ap[0:33, 512:1024], True, True, [aug, waug], [b1])
            act(la.ap[:, 0:512], b0.ap, AF.Exp, [b0], [la], scale=-1.0)
            act(la.ap[:, 512:1024], b1.ap, AF.Exp, [b1], [la], scale=-1.0)
            act(la.ap, la.ap, AF.Ln, [la, onescol], [la], bias=onescol.ap)
            ts("dve", la.ap, la.ap, -1.0 / 16.0, None, ALU.mult, None, [la], [la])
            tl_ = triLs if sample else triL
            tr_ = triRs if sample else triR
            nsq = 16 if sample else 1
            sel = seqsel if sample else onescol
            bb = [nbank(), nbank()]
            be = [nbank(), nbank()]
            for hh in range(2):
                mm(bb[hh].ap, tl_.ap, la.ap[:, hh * 512:(hh + 1) * 512], True, True, [tl_, la], [bb[hh]])
                mm(be[hh].ap, tr_.ap, la.ap[:, hh * 512:(hh + 1) * 512], True, True, [tr_, la], [be[hh]])
            bl = nbank()
            for dc in range(8):
                mm(bl.ap[:, dc * 16:dc * 16 + nsq], la.ap[:, dc * 128:(dc + 1) * 128], sel.ap[:, 0:nsq], True, True,
                   [la, sel], [bl])
            for hh in range(2):
                sl = slice(hh * 512, (hh + 1) * 512)
                act(eb.ap[:, sl], bb[hh].ap, AF.Exp, [bb[hh]], [eb])
                act(enb.ap[:, sl], bb[hh].ap, AF.Exp, [bb[hh]], [enb], scale=-1.0)
                act(ee.ap[:, sl], be[hh].ap, AF.Exp, [be[hh]], [ee])
            eblv = ebl.ap.rearrange("p (d j) -> p d j", j=16)
            act(eblv[:, :, 0:nsq], bl.ap[:, 0:128].rearrange("p (d j) -> p d j", j=16)[:, :, 0:nsq], AF.Exp, [bl], [ebl])
            stt("dve", qd.ap, qk.ap[:, 0:1024], 0.0625, eb.ap, ALU.mult, ALU.mult, [qk, eb], [qd])
            tt("pool", ki.ap, qk.ap[:, 1024:2048], enb.ap, ALU.mult, [qk, enb], [ki])
            tt("pool", ke.ap, qk.ap[:, 1024:2048], ee.ap, ALU.mult, [qk, ee], [ke])
            tr_bf(qd, qd.ap, 8, qdT, qdT3)
            tr_bf(ki, ki.ap, 8, kiT, kiT3)
            memset("dve", ss.ap, 0.0, [ss])
            for h in range(4):
                vh = v.ap[:, h * 512:(h + 1) * 512]
                sc = nbank()
                for j in range(2):
                    mm(sc.ap[:, 0:128], kiT3[:, 2 * h + j, :], qdT3[:, 2 * h + j, :], j == 0, j == 1, [kiT, qdT], [sc])
                sm_ = scm[h % 2]
                cp("act", scf.ap, sc.ap[:, 0:128], [sc], [scf])
                tt("dve", sm_.ap, scf.ap, tl_.ap, ALU.mult, [scf, tl_], [sm_])
                ob = nbank()
                mm(ob.ap, sm_.ap, vh, True, False, [sm_, v], [ob])
                if not sample:
                    for j in range(2):
                        mm(ob.ap, qdT3[:, 2 * h + j, :], Sb3[:, 2 * h + j, :], False, j == 1, [qdT, Sbt[2 * h + j]], [ob])
                else:
                    q_ = qm[h % 2]
                    q4 = q_.ap.rearrange("p (j s t) -> p j s t", j=2, s=16)
                    for j in range(2):
                        tt("dve", q4[:, j], qdT3[:, 2 * h + j, :].unsqueeze(1).to_broadcast([128, 16, 128]),
                           colmask.ap.rearrange("p (s t) -> p s t", s=16), ALU.mult, [qdT, colmask], [q_])
                    for sq in range(16):
                        sj = sjb[(h * 16 + sq) % 3]
                        sj3 = sj.ap.rearrange("p (j v) -> p j v", j=2)
                        S.dma("pool", sj3, sgla[l, sb * 16 + sq, h].rearrange("(j p) v -> p j v", p=128), writes=[sj])
                        for j in range(2):
                            mm(ob.ap, q4[:, j, sq, :], sj3[:, j, :], False, (sq == 15 and j == 1), [q_, sj], [ob])
                if not sample:
                    for j in range(2):
                        k = 2 * h + j
                        sbk = nbank()
                        mm(sbk.ap, ke.ap[:, k * 128:(k + 1) * 128], vh, True, True, [ke, v], [sbk])
                        sp_ = stmp[k % 2]
                        cp("act", sp_.ap, sbk.ap, [sbk], [sp_])
                        stt("dve", Sf3[:, k, :], Sf3[:, k, :], ebl.ap[:, k * 16:k * 16 + 1], sp_.ap, ALU.mult, ALU.add,
                            [Sft[k], ebl, sp_], [Sft[k]])
                        cp("act", Sb3[:, k, :], Sf3[:, k, :], [Sft[k]], [Sbt[k]])
                cp("act", obs.ap, ob.ap, [ob], [obs])
                act(junk.ap, obs.ap, AF.Square, [obs], [junk, ss], accum=ss.ap[:, h:h + 1])
                ts("dve", ss.ap[:, 4 + h:5 + h], ss.ap[:, h:h + 1], 1.0 / 512.0, None, ALU.mult, None, [ss], [ss])
                act(ss.ap[:, 8 + h:9 + h], ss.ap[:, 4 + h:5 + h], AF.Ln, [ss, epscol], [ss], bias=epscol.ap[:, 1:2])
                act(ss.ap[:, 12 + h:13 + h], ss.ap[:, 8 + h:9 + h], AF.Exp, [ss], [ss], scale=-0.5)
                stt("dve", on.ap, obs.ap, ss.ap[:, 12 + h:13 + h], gt.ap, ALU.mult, ALU.mult, [obs, ss, gt], [on])
                tt("pool", og.ap[:, h * 512:(h + 1) * 512], on.ap, r.ap[:, h * 512:(h + 1) * 512], ALU.mult, [on, r], [og])
            S.dma("sp", OG[rows, :], og.ap, reads=[og])
            if sample:
                for sq in range(16):
                    km = kem[sq % 2]
                    ts("dve", km.ap, ke.ap, seqsel.ap[:, sq:sq + 1], None, ALU.mult, None, [ke, seqsel], [km])
                    for h in range(4):
                        vh = v.ap[:, h * 512:(h + 1) * 512]
                        f_ = sjf[(sq * 4 + h) % 2]
                        o_ = sjo[(sq * 4 + h) % 2]
                        f3 = f_.ap.rearrange("p (j v) -> p j v", j=2)
                        o3 = o_.ap.rearrange("p (j v) -> p j v", j=2)
                        S.dma("sp", f3, sgla[l, sb * 16 + sq, h].rearrange("(j p) v -> p j v", p=128), writes=[f_])
                        for j in range(2):
                            k = 2 * h + j
                            sbk = nbank()
                            mm(sbk.ap, km.ap[:, k * 128:(k + 1) * 128], vh, True, True, [km, v], [sbk])
                            sp_ = stmp[k % 2]
                            cp("act", sp_.ap, sbk.ap, [sbk], [sp_])
                            stt("dve", o3[:, j, :], f3[:, j, :], ebl.ap[:, k * 16 + sq:k * 16 + sq + 1], sp_.ap,
                                ALU.mult, ALU.add, [f_, ebl, sp_], [o_])
                        S.dma("sp", glas[l, sb * 16 + sq, h].rearrange("(j p) v -> p j v", p=128), o3, reads=[o_])
        if sb == NSB - 1:
            S.dma("sp", glap[l].rearrange("h (j p) v -> p (h j) v", p=128), Sf3, reads=Sft)
        else:
            S.dma("sp", GST[:, :, :], Sf3, reads=Sft)

    def gen_s5(l):
        S.fence()
        A.p = PERS
        araw = A.f32(128)
        are = A.f32(128, parts=64)
        aim = A.f32(128, parts=64)
        dt = A.f32(128, parts=64)
        ldr = A.f32(128, parts=64)
        th = A.f32(128, parts=64)
        P = A.f32(2 * 17 * 128, parts=64)
        Q = A.f32(2 * 16 * 128, parts=64)
        P4 = P.ap.rearrange("p (r k g) -> p r k g", r=2, k=17)
        Q4 = Q.ap.rearrange("p (r k g) -> p r k g", r=2, k=16)
        t1 = A.f32(128, parts=64)
        t2 = A.f32(128, parts=64)
        t3 = A.f32(128, parts=64)
        ti_ = A.i32(128, parts=64)
        mag = A.f32(128, parts=64)
        cr = A.f32(128, parts=64)
        ci = A.f32(128, parts=64)
        big1 = A.f32(16 * 128, parts=64)
        big2 = A.f32(16 * 128, parts=64)
        S.dma("sp", araw.ap[:, 0:64], W["s5_a_re"][l], writes=[araw])
        S.dma("sp", araw.ap[:, 64:128], W["s5_a_im"][l], writes=[araw])
        bk = nbank()
        trp(bk.ap[0:64, 0:128], araw.ap[:, 0:64], ident_f.ap, [araw, ident_f], [bk])
        trp(bk.ap[0:64, 128:256], araw.ap[:, 64:128], ident_f.ap, [araw, ident_f], [bk])
        cp("dve", are.ap, bk.ap[0:64, 0:128], [bk], [are])
        cp("dve", aim.ap, bk.ap[0:64, 128:256], [bk], [aim])
        S.dma("sp", dt.ap, W["s5_log_dt"][l, :].partition_broadcast(64), writes=[dt])
        act(dt.ap, dt.ap, AF.Exp, [dt], [dt])
        tt("dve", ldr.ap, are.ap, dt.ap, ALU.mult, [are, dt], [ldr])
        tt("dve", th.ap, aim.ap, dt.ap, ALU.mult, [aim, dt], [th])
        for kk in range(17):
            k = kk - 8
            act(mag.ap, ldr.ap, AF.Exp, [ldr], [mag], scale=float(k))
            for r_, off in ((0, math.pi / 2), (1, 0.0)):
                ts("dve", t1.ap, th.ap, float(k), off, ALU.mult, ALU.add, [th], [t1])
                ts("dve", ti_.ap, t1.ap, 1.0 / TWO_PI, None, ALU.mult, None, [t1], [ti_])
                cp("dve", t2.ap, ti_.ap, [ti_], [t2])
                stt("dve", t3.ap, t2.ap, -TWO_PI, t1.ap, ALU.mult, ALU.add, [t2, t1], [t3])
                ts("dve", t3.ap, t3.ap, -3.14159, 3.14159, ALU.max, ALU.min, [t3], [t3])
                act(t2.ap, t3.ap, AF.Sin, [t3], [t2])
                tt("dve", P4[:, r_, kk, :], mag.ap, t2.ap, ALU.mult, [mag, t2], [P])
        if CUT == "a":
            return
        ts("dve", t1.ap, P4[:, 0, 9, :], -1.0, None, ALU.add, None, [P], [t1])
        tt("dve", t2.ap, are.ap, are.ap, ALU.mult, [are], [t2])
        tt("dve", t3.ap, aim.ap, aim.ap, ALU.mult, [aim], [t3])
        tt("dve", t2.ap, t2.ap, t3.ap, ALU.add, [t2, t3], [t2])
        S.op("dve", lambda e: e.reciprocal(out=t2.ap, in_=t2.ap), [t2], [t2])
        tt("dve", cr.ap, t1.ap, are.ap, ALU.mult, [t1, are], [cr])
        tt("dve", t3.ap, P4[:, 1, 9, :], aim.ap, ALU.mult, [P, aim], [t3])
        tt("dve", cr.ap, cr.ap, t3.ap, ALU.add, [cr, t3], [cr])
        tt("dve", cr.ap, cr.ap, t2.ap, ALU.mult, [cr, t2], [cr])
        tt("dve", ci.ap, P4[:, 1, 9, :], are.ap, ALU.mult, [P, are], [ci])
        tt("dve", t3.ap, t1.ap, aim.ap, ALU.mult, [t1, aim], [t3])
        tt("dve", ci.ap, ci.ap, t3.ap, ALU.subtract, [ci, t3], [ci])
        tt("dve", ci.ap, ci.ap, t2.ap, ALU.mult, [ci, t2], [ci])
        crb = cr.ap.unsqueeze(1).to_broadcast([64, 16, 128])
        cib = ci.ap.unsqueeze(1).to_broadcast([64, 16, 128])
        b1v = big1.ap.rearrange("p (k g) -> p k g", k=16)
        b2v = big2.ap.rearrange("p (k g) -> p k g", k=16)
        tt("dve", b1v, P4[:, 0, 0:16, :], crb, ALU.mult, [P, cr], [big1])
        tt("dve", b2v, P4[:, 1, 0:16, :], cib, ALU.mult, [P, ci], [big2])
        tt("dve", Q4[:, 0], b1v, b2v, ALU.subtract, [big1, big2], [Q])
        tt("dve", b1v, P4[:, 1, 0:16, :], crb, ALU.mult, [P, cr], [big1])
        tt("dve", b2v, P4[:, 0, 0:16, :], cib, ALU.mult, [P, ci], [big2])
        tt("dve", Q4[:, 1], b1v, b2v, ALU.add, [big1, big2], [Q])
        cp("dve", A8.ap.rearrange("p (r g) -> p r g", r=2), P4[:, :, 16, :], [P], [A8])
        if CUT == "b":
            return
        mark = A.p
        GG = 16
        for gc in range(128 // GG):
            g0 = gc * GG
            S.fence()
            A.p = mark
            BRI = A.f32(2 * GG * 16, parts=64)
            BRI4 = BRI.ap.rearrange("p (r g m) -> p r g m", r=2, g=GG)
            craw = A.f32(2 * 16 * 64, parts=GG)
            craw4 = craw.ap.rearrange("g (r n p) -> g r n p", r=2, n=16)
            CR = A.f32(2 * GG * 16, parts=64)
            CR4 = CR.ap.rearrange("p (r g n) -> p r g n", r=2, g=GG)
            BTW = A.f32(2 * GG * 128, parts=64)
            BTT = A.f32(2 * GG * 128, parts=64)
            CB = A.f32(2 * GG * 128, parts=64)
            BTW5 = BTW.ap.rearrange("p (r g s m) -> p r g s m", r=2, g=GG, s=8)
            BTT5 = BTT.ap.rearrange("p (r g s m) -> p r g s m", r=2, g=GG, s=8)
            CB5 = CB.ap.rearrange("p (r g t n) -> p r g t n", r=2, g=GG, t=8)
            BTW4 = BTW.ap.rearrange("p (r g k) -> p r g k", r=2, g=GG)
            BTT4 = BTT.ap.rearrange("p (r g k) -> p r g k", r=2, g=GG)
            CB4 = CB.ap.rearrange("p (r g k) -> p r g k", r=2, g=GG)
            u1 = A.f32(GG * 16, parts=64)
            u2 = A.f32(GG * 16, parts=64)
            u1v = u1.ap.rearrange("p (g m) -> p g m", g=GG)
            u2v = u2.ap.rearrange("p (g m) -> p g m", g=GG)
            smaS = A.bf(GG * 256)
            smcS = A.bf(GG * 256, parts=64)
            smaS3 = smaS.ap.rearrange("p (g k) -> p g k", g=GG)
            smcS3 = smcS.ap.rearrange("p (g k) -> p g k", g=GG)
            dcol = A.f32(GG)
            tmpT = A.f32(128)
            for r_, nm in ((0, "s5_b_re"), (1, "s5_b_im")):
                S.dma("sp", BRI4[:, r_], W[nm][l, g0:g0 + GG].rearrange("g p m -> p g m"), writes=[BRI])
            for r_, nm in ((0, "s5_c_re"), (1, "s5_c_im")):
                S.dma("sp", craw4[:, r_], W[nm][l, g0:g0 + GG], writes=[craw])
            for s_ in range(8):
                S.dma("sp", dcol.ap[s_ * 16:(s_ + 1) * 16, :],
                      W["s5_d"][l, g0 * 16:(g0 + GG) * 16].rearrange("(g m) -> m g", m=16), writes=[dcol],
                      allow_slow_non_contiguous=True)
            bks = [nbank(), nbank()]
            for r_ in range(2):
                for n_ in range(16):
                    trp(bks[r_].ap[0:64, n_ * GG:(n_ + 1) * GG], craw4[:, r_, n_, :], ident_f.ap[0:GG, 0:GG],
                        [craw, ident_f], [bks[r_]])
                cp("dve", CR4[:, r_].rearrange("p g n -> p n g"),
                   bks[r_].ap[0:64, 0:16 * GG].rearrange("p (n g) -> p n g", n=16), [bks[r_]], [CR])

            def cmul(dst5, Xri, Yr, Yi, neg_im, idx):
                yrb = Yr.unsqueeze(2).to_broadcast([64, GG, 16])
                yib = Yi.unsqueeze(2).to_broadcast([64, GG, 16])
                tt("dve", u1v, Xri[:, 0], yrb, ALU.mult, [BRI, CR, P, Q], [u1])
                tt("pool", u2v, Xri[:, 1], yib, ALU.mult, [BRI, CR, P, Q], [u2])
                tt("dve", dst5[:, 0, :, idx, :], u1v, u2v, ALU.subtract, [u1, u2], [BTW, BTT, CB])
                tt("dve", u1v, Xri[:, 0], yib, ALU.mult, [BRI, CR, P, Q], [u1])
                tt("pool", u2v, Xri[:, 1], yrb, ALU.mult, [BRI, CR, P, Q], [u2])
                if neg_im:
                    stt("dve", dst5[:, 1, :, idx, :], u1v, -1.0, u2v, ALU.mult, ALU.subtract, [u1, u2], [BTW, BTT, CB])
                else:
                    tt("dve", dst5[:, 1, :, idx, :], u1v, u2v, ALU.add, [u1, u2], [BTW, BTT, CB])

            if CUT == "c":
                return
            gs = slice(g0, g0 + GG)
            for s_ in range(8):
                cmul(BTW5, BRI4, Q4[:, 0, 15 - s_, gs], Q4[:, 1, 15 - s_, gs], False, s_)
                cmul(BTT5, BRI4, Q4[:, 0, 7 - s_, gs], Q4[:, 1, 7 - s_, gs], False, s_)
                cmul(CB5, CR4, P4[:, 0, s_ + 9, gs], P4[:, 1, s_ + 9, gs], True, s_)
            if CUT == "d":
                return
            for g in range(GG):
                bk = nbank()
                trp(bk.ap[:, 0:64], BTW4[:, 0, g, :], ident_f.ap[0:64, 0:64], [BTW, ident_f], [bk])
                trp(bk.ap[:, 64:128], BTW4[:, 1, g, :], ident_f.ap[0:64, 0:64], [BTW, ident_f], [bk])
                if CUT == "d1":
                    continue
                mm(bk.ap[:, 128:256], BTT4[:, 0, g, :], CB4[:, 0, g, :], True, False, [BTT, CB], [bk])
                mm(bk.ap[:, 128:256], BTT4[:, 1, g, :], CB4[:, 1, g, :], False, True, [BTT, CB], [bk])
                if CUT == "d2":
                    continue
                cp("act", smaS3[:, g, 0:128], bk.ap[:, 0:128], [bk], [smaS])
                if CUT == "d3":
                    continue
                cp("act", tmpT.ap, bk.ap[:, 128:256], [bk], [tmpT])
                tt("dve", tmpT.ap, tmpT.ap, bmask.ap, ALU.mult, [tmpT, bmask], [tmpT])
                if CUT == "d4":
                    continue
                stt("dve", smaS3[:, g, 128:256], ident_f.ap, dcol.ap[:, g:g + 1], tmpT.ap, ALU.mult, ALU.add,
                    [ident_f, dcol, tmpT], [smaS])
            if CUT in ("d1", "d2", "d3", "d4"):
                return
            if CUT == "e":
                return
            cp("act", smcS3[:, :, 0:128], CB4[:, 0], [CB], [smcS])
            cp("act", smcS3[:, :, 128:256], CB4[:, 1], [CB], [smcS])
            if CUT == "f":
                return
            S.dma("sp", SMA[:, g0:g0 + GG, :], smaS3, reads=[smaS])
            S.dma("sp", SMC[:, g0:g0 + GG, :], smcS3, reads=[smcS])

    def stage_S(l, sb):
        row0 = sb * TSB
        GQ = 32
        if sb == 0:
            S.fence()
            memset("dve", XST.ap, 0.0, [XST])
        XST3 = XST.ap.rearrange("p (r g) -> p r g", r=2)
        A83 = A8.ap.rearrange("p (r g) -> p r g", r=2)
        for gc in range(128 // GQ):
            g0 = gc * GQ
            S.fence()
            A.p = PERS
            sma = A.bf(GQ * 256)
            smc = A.bf(GQ * 256, parts=64)
            sma3 = sma.ap.rearrange("p (g k) -> p g k", g=GQ)
            smc3 = smc.ap.rearrange("p (g k) -> p g k", g=GQ)
            S.dma("sp", sma3, SMA[:, g0:g0 + GQ, :], writes=[sma])
            S.dma("sp", smc3, SMC[:, g0:g0 + GQ, :], writes=[smc])
            ug = A.bf(GQ * NCB)
            ug3 = ug.ap.rearrange("p (g c) -> p g c", g=GQ)
            mark_s = A.p
            ub = A.bf(8 * GQ * 16)
            ubs = A.bf(8 * GQ * 16, parts=16)
            ub4 = ub.ap.rearrange("c (s g m) -> c s g m", s=8, g=GQ)
            ubs4 = ubs.ap.rearrange("c (s g m) -> c s g m", s=8, g=GQ)
            cols = slice(g0 * 16, (g0 + GQ) * 16)
            S.dma("sp", ub.ap[0:NCP, :].rearrange("c (s f) -> c s f", s=8),
                  Us[row0:row0 + NPT * 128, cols].rearrange("(c s) f -> c s f", s=8), writes=[ub])
            S.dma("sp", ubs.ap.rearrange("c (s f) -> c s f", s=8),
                  Us[row0 + NPT * 128:row0 + TSB, cols].rearrange("(c s) f -> c s f", s=8), writes=[ubs])
            ub2 = A.bf(8 * GQ * 16)
            ubs2 = A.bf(8 * GQ * 16, parts=16)
            cp("dve", ub2.ap[0:NCP, :].rearrange("c (g s m) -> c g s m", g=GQ, s=8), ub4[0:NCP].rearrange("c s g m -> c g s m"), [ub], [ub2])
            cp("act", ubs2.ap.rearrange("c (g s m) -> c g s m", g=GQ, s=8), ubs4.rearrange("c s g m -> c g s m"), [ubs], [ubs2])
            ub2v = ub2.ap.rearrange("c (g k) -> c g k", g=GQ)
            ubs2v = ubs2.ap.rearrange("c (g k) -> c g k", g=GQ)
            for gq in range(0, GQ, 4):
                bank = nbank()
                pb = bank.ap.bitcast(BF16)
                for k in range(4):
                    g = gq + k
                    trp(pb[:, k * 256:k * 256 + NCP], ub2v[0:NCP, g, :], ident_b.ap[0:NCP, 0:NCP], [ub2, ident_b], [bank])
                    trp(pb[:, k * 256 + 128:k * 256 + 144], ubs2v[:, g, :], ident_b.ap[0:16, 0:16], [ubs2, ident_b], [bank])
                pbv = pb.rearrange("p (k c) -> p k c", k=4)
                cp(alt(), ug3[:, gq:gq + 4, 0:NCP], pbv[:, :, 0:NCP], [bank], [ug])
                cp(alt(), ug3[:, gq:gq + 4, NCP:NCB], pbv[:, :, 128:144], [bank], [ug])
            S.fence()
            A.p = mark_s
            W2 = A.f32(2 * GQ * NCB, parts=64)
            W4 = W2.ap.rearrange("p (r g c) -> p r g c", r=2, g=GQ)
            for g in range(GQ):
                bank = nbank()
                for r_ in range(2):
                    mm(bank.ap[0:64, r_ * 256:r_ * 256 + NCB], sma3[:, g, r_ * 64:(r_ + 1) * 64], ug3[:, g, :], True, True,
                       [sma, ug], [bank])
                cp(alt(), W4[:, :, g, :], bank.ap[0:64, :].rearrange("p (r c) -> p r c", r=2)[:, :, 0:NCB], [bank], [W2])
            XP = A.f32(2 * GQ * (NCP + 1), parts=64)
            XP4 = XP.ap.rearrange("p (r g c) -> p r g c", r=2, g=GQ)
            XS = A.f32(2 * GQ * 16, parts=64)
            XS4 = XS.ap.rearrange("p (r g j) -> p r g j", r=2, g=GQ)
            XN = A.f32(2 * GQ * 16, parts=64)
            XN4 = XN.ap.rearrange("p (r g j) -> p r g j", r=2, g=GQ)
            XB = A.bf(2 * GQ * NCB, parts=64)
            XB4 = XB.ap.rearrange("p (r g c) -> p r g c", r=2, g=GQ)
            AA = A.f32(2 * GQ, parts=64)
            II = A.f32(2 * GQ, parts=64)
            AA3 = AA.ap.rearrange("p (r g) -> p r g", r=2)
            II3 = II.ap.rearrange("p (r g) -> p r g", r=2)
            tA = A.f32(2 * GQ, parts=64)
            tU = A.f32(2 * GQ, parts=64)
            tA3 = tA.ap.rearrange("p (r g) -> p r g", r=2)
            tU3 = tU.ap.rearrange("p (r g) -> p r g", r=2)
            for r_ in range(2):
                cp("pool", AA3[:, r_, :], A83[:, 0, g0:g0 + GQ], [A8], [AA])
                cp("pool", II3[:, r_, :], A83[:, 1, g0:g0 + GQ], [A8], [II])
            cp("pool", XP4[:, :, :, 0], XST3[:, :, g0:g0 + GQ], [XST], [XP])
            for c in range(NCP):
                xc = XP4[:, :, :, c]
                tt("pool", tA3, AA3, xc, ALU.mult, [AA, XP], [tA])
                tt("pool", tU3, II3, xc, ALU.mult, [II, XP], [tU])
                tt("pool", tA3, tA3, W4[:, :, :, c], ALU.add, [tA, W2], [tA])
                tt("pool", XP4[:, 0, :, c + 1], tA3[:, 0, :], tU3[:, 1, :], ALU.subtract, [tA, tU], [XP])
                tt("pool", XP4[:, 1, :, c + 1], tA3[:, 1, :], tU3[:, 0, :], ALU.add, [tA, tU], [XP])
            cp("pool", XST3[:, :, g0:g0 + GQ], XP4[:, :, :, NCP], [XP], [XST])
            sin = A.f32(16 * 2 * 64, parts=GQ)
            sin4 = sin.ap.rearrange("g (j r p) -> g j r p", j=16, r=2)
            for r_ in range(2):
                S.dma("sp", sin4[:, :, r_, :], ss5[r_][l, sb * 16:(sb + 1) * 16, g0:g0 + GQ, :].rearrange("j g p -> g j p"),
                      writes=[sin])
            bks = [nbank(), nbank()]
            for j in range(16):
                for r_ in range(2):
                    idx = j * 2 + r_
                    trp(bks[idx // 16].ap[0:64, (idx % 16) * GQ:(idx % 16 + 1) * GQ], sin4[:, j, r_, :],
                        ident_f.ap[0:GQ, 0:GQ], [sin, ident_f], [bks[idx // 16]])
            for hb in range(2):
                cp("dve", XS4[:, :, :, hb * 8:(hb + 1) * 8].rearrange("p r g j -> p j r g"),
                   bks[hb].ap[0:64, 0:16 * GQ].rearrange("p (j r g) -> p j r g", j=8, r=2), [bks[hb]], [XS])
            t5 = A.f32(2 * GQ * 16, parts=64)
            u5 = A.f32(2 * GQ * 16, parts=64)
            t54 = t5.ap.rearrange("p (r g j) -> p r g j", r=2, g=GQ)
            u54 = u5.ap.rearrange("p (r g j) -> p r g j", r=2, g=GQ)
            AAb = AA3.unsqueeze(3).to_broadcast([64, 2, GQ, 16])
            IIb = II3.unsqueeze(3).to_broadcast([64, 2, GQ, 16])
            tt("dve", t54, XS4, AAb, ALU.mult, [XS, AA], [t5])
            tt("dve", u54, XS4, IIb, ALU.mult, [XS, II], [u5])
            tt("dve", t54, t54, W4[:, :, :, NCP:NCB], ALU.add, [t5, W2], [t5])
            tt("dve", XN4[:, 0], t54[:, 0], u54[:, 1], ALU.subtract, [t5, u5], [XN])
            tt("dve", XN4[:, 1], t54[:, 1], u54[:, 0], ALU.add, [t5, u5], [XN])
            sout = A.f32(16 * 2 * 64, parts=GQ)
            sout4 = sout.ap.rearrange("g (j r p) -> g j r p", j=16, r=2)
            for q4 in range(4):
                bank = nbank()
                for jj in range(4):
                    for r_ in range(2):
                        j = q4 * 4 + jj
                        o_ = (jj * 2 + r_) * 64
                        trp(bank.ap[0:GQ, o_:o_ + 64], XN4[:, r_, :, j], ident_f.ap[0:64, 0:64], [XN, ident_f], [bank])
                cp(alt(), sout.ap[:, q4 * 512:(q4 + 1) * 512], bank.ap[0:GQ, :], [bank], [sout])
            for r_ in range(2):
                S.dma("sp", s5s[r_][l, sb * 16:(sb + 1) * 16, g0:g0 + GQ, :].rearrange("j g p -> g j p"),
                      sout4[:, :, r_, :], reads=[sout])
            cp("act", XB4[:, :, :, 0:NCP], XP4[:, :, :, 0:NCP], [XP], [XB])
            cp("act", XB4[:, :, :, NCP:NCB], XS4, [XS], [XB])
            yb = A.bf(8 * GQ * 16)
            ybs = A.bf(8 * GQ * 16, parts=16)
            yb4 = yb.ap.rearrange("c (t g n) -> c t g n", t=8, g=GQ)
            ybs4 = ybs.ap.rearrange("c (t g n) -> c t g n", t=8, g=GQ)
            for (c0, m_, ydst, ytl) in ((0, NCP, yb4, yb), (NCP, 16, ybs4, ybs)):
                for gq in range(0, GQ, 4):
                    bank = nbank()
                    for k in range(4):
                        g = gq + k
                        o = bank.ap[0:m_, k * 128:(k + 1) * 128]
                        mm(o, ug3[:, g, c0:c0 + m_], sma3[:, g, 128:256], True, False, [ug, sma], [bank])
                        mm(o, XB4[:, 0, g, c0:c0 + m_], smc3[:, g, 0:128], False, False, [XB, smc], [bank])
                        mm(o, XB4[:, 1, g, c0:c0 + m_], smc3[:, g, 128:256], False, True, [XB, smc], [bank])
                    act(ydst[0:m_, :, gq:gq + 4, :].rearrange("c t g n -> c g t n"),
                        bank.ap[0:m_, :].rearrange("c (g t n) -> c g t n", g=4, t=8), AF.Gelu_apprx_tanh, [bank], [ytl])
            S.dma("sp", YG[row0:row0 + NPT * 128, cols].rearrange("(c t) f -> c t f", t=8),
                  yb.ap[0:NCP, :].rearrange("c (t f) -> c t f", t=8), reads=[yb])
            S.dma("sp", YG[row0 + NPT * 128:row0 + TSB, cols].rearrange("(c t) f -> c t f", t=8),
                  ybs.ap.rearrange("c (t f) -> c t f", t=8), reads=[ybs])
        if sb == NSB - 1:
            S.fence()
            A.p = PERS
            pso = A.f32(128)
            bank = nbank()
            for r_ in range(2):
                trp(bank.ap[:, r_ * 64:(r_ + 1) * 64], XST3[:, r_, :], ident_f.ap[0:64, 0:64], [XST, ident_f], [bank])
            cp("dve", pso.ap, bank.ap[:, 0:128], [bank], [pso])
            for r_ in range(2):
                S.dma("sp", s5p[r_][l], pso.ap[:, r_ * 64:(r_ + 1) * 64], reads=[pso])

    def gen_ut(l):
        S.fence()
        A.p = PERS
        ub_ = [A.bf(D) for _ in range(2)]
        ust = [A.bf(32 * 128) for _ in range(2)]
        UTv = UT.rearrange("c d e -> d c e")
        for ec in range(128):
            b = ub_[ec % 2]
            s_ = ust[ec % 2]
            S.dma("pool", b.ap, W["peer_u"][l, ec * 128:(ec + 1) * 128, :], writes=[b])
            tr_bf(b, b.ap, 32, s_, s_.ap.rearrange("p (c e) -> p c e", c=32))
            S.dma("sp", UTv[:, :, ec * 128:(ec + 1) * 128], s_.ap.rearrange("p (c e) -> p c e", c=32), reads=[s_])

    def stage_B12(l, sb):
        S.fence()
        A.p = PERS
        row0 = sb * TSB
        os5T = A.bf(16 * TSB)
        os5T3 = os5T.ap.rearrange("p (c t) -> p c t", c=16)
        osT = [TL(os5T3[:, c, :]) for c in range(16)]
        mark = A.p
        ygT = A.bf(16 * TSB)
        ygT3 = ygT.ap.rearrange("p (c t) -> p c t", c=16)
        ygt = [TL(ygT3[:, :, i * 128:(i + 1) * 128]) for i in range(TPS)]
        load_T(YG, row0, TPS, 2048, ygT3, ygt, False)
        bglu = A.f32(16)
        S.dma("sp", bglu.ap, W["s5_b_glu"][l, :].rearrange("(c p) -> p c", p=128), writes=[bglu],
              allow_slow_non_contiguous=True)
        wg = [A.bf(16 * 128) for _ in range(2)]
        sg = [A.f32(512) for _ in range(2)]
        n_ = 0
        for cc in range(16):
            w = wg[cc % 2]
            w3 = w.ap.rearrange("p (c n) -> p c n", c=16)
            S.dma("pool", w3, W["s5_w_glu"][l, :, cc * 128:(cc + 1) * 128].rearrange("(c p) n -> p c n", p=128), writes=[w])
            for (t0, n) in tok_blocks(TSB):
                bank = nbank()
                rd = [ygt[i] for i in tiles_of(t0, n)]
                for c in range(16):
                    mm(bank.ap[:, 0:n], w3[:, c, :], ygT3[:, c, t0:t0 + n], c == 0, c == 15, [w] + rd, [bank])
                s_ = sg[n_ % 2]
                n_ += 1
                act(s_.ap[:, 0:n], bank.ap[:, 0:n], AF.Sigmoid, [bank, bglu], [s_], bias=bglu.ap[:, cc:cc + 1])
                tt("dve", os5T3[:, cc, t0:t0 + n], s_.ap[:, 0:n], ygT3[:, cc, t0:t0 + n], ALU.mult, [s_] + rd, [osT[cc]])
        S.fence()
        A.p = mark
        ogT = A.bf(16 * TSB)
        ogT3 = ogT.ap.rearrange("p (c t) -> p c t", c=16)
        ogt = [TL(ogT3[:, :, i * 128:(i + 1) * 128]) for i in range(TPS)]
        load_T(OG, row0, TPS, 2048, ogT3, ogt, False)
        wbr = [A.bf(32 * 512) for _ in range(2)]
        zt = [A.bf(1024) for _ in range(2)]
        m1 = A.f32(512)
        m2 = A.f32(512)
        mo = [A.bf(512) for _ in range(2)]
        n_ = 0
        for cb in range(8):
            w = wbr[cb % 2]
            w3 = w.ap.rearrange("p (c n) -> p c n", c=32)
            S.dma("pool", w3, W["w_branch"][l, :, cb * 512:(cb + 1) * 512].rearrange("(c p) n -> p c n", p=128), writes=[w])
            for i in range(TPS):
                rows = slice(row0 + i * 128, row0 + (i + 1) * 128)
                ba, bb_ = nbank(), nbank()
                for c in range(16):
                    mm(ba.ap, ogT3[:, c, i * 128:(i + 1) * 128], w3[:, c, :], c == 0, c == 15, [ogt[i], w], [ba])
                for c in range(16):
                    mm(bb_.ap, os5T3[:, c, i * 128:(i + 1) * 128], w3[:, 16 + c, :], c == 0, c == 15, [osT[c], w], [bb_])
                z_ = zt[n_ % 2]
                o_ = mo[n_ % 2]
                n_ += 1
                S.dma("sp", z_.ap[:, 0:512], Z[rows, cb * 512:(cb + 1) * 512], writes=[z_])
                S.dma("sp", z_.ap[:, 512:1024], Z[rows, 4096 + cb * 512:4096 + (cb + 1) * 512], writes=[z_])
                cp("act", m1.ap, ba.ap, [ba], [m1])
                cp("act", m2.ap, bb_.ap, [bb_], [m2])
                tt("dve", m1.ap, m1.ap, z_.ap[:, 0:512], ALU.mult, [m1, z_], [m1])
                tt("dve", m2.ap, m2.ap, z_.ap[:, 512:1024], ALU.mult, [m2, z_], [m2])
                tt("pool", o_.ap, m1.ap, m2.ap, ALU.add, [m1, m2], [o_])
                S.dma("sp", MM[rows, cb * 512:(cb + 1) * 512], o_.ap, reads=[o_])

    def stage_B3(l, sb):
        S.fence()
        A.p = PERS
        row0 = sb * TSB
        mT = A.bf(32 * TSB)
        mT3 = mT.ap.rearrange("p (c t) -> p c t", c=32)
        mt = [TL(mT3[:, :, i * 128:(i + 1) * 128]) for i in range(TPS)]
        mark = A.p
        load_T(MM, row0, TPS, D, mT3, mt, False)
        S.fence()
        A.p = mark
        wo = [A.bf(32 * 512) for _ in range(2)]
        hs = [A.f32(512) for _ in range(3)]
        n_ = 0
        for cb in range(8):
            w = wo[cb % 2]
            w3 = w.ap.rearrange("p (c n) -> p c n", c=32)
            S.dma("pool", w3, W["w_out"][l, :, cb * 512:(cb + 1) * 512].rearrange("(c p) n -> p c n", p=128), writes=[w])
            for i in range(TPS):
                rows = slice(row0 + i * 128, row0 + (i + 1) * 128)
                bank = nbank()
                for c in range(32):
                    mm(bank.ap, mT3[:, c, i * 128:(i + 1) * 128], w3[:, c, :], c == 0, c == 31, [mt[i], w], [bank])
                h_ = hs[n_ % 3]
                n_ += 1
                cp(alt(), h_.ap, bank.ap, [bank], [h_])
                S.dma("sp", FF[rows, cb * 512:(cb + 1) * 512], h_.ap, reads=[h_])

    def stage_LN(l, sb, resid, gname, bname, dst):
        S.fence()
        A.p = PERS
        row0 = sb * TSB
        g_ = A.f32(D)
        b_ = A.f32(D)
        S.dma("sp", g_.ap, W[gname][l, :].partition_broadcast(128), writes=[g_])
        S.dma("sp", b_.ap, W[bname][l, :].partition_broadcast(128), writes=[b_])
        ft = [A.f32(D) for _ in range(2)]
        xt = [A.f32(D) for _ in range(2)]
        ot = [A.f32(D) for _ in range(2)]
        stt_ = [A.f32(8) for _ in range(2)]
        for i in range(TPS):
            rows = slice(row0 + i * 128, row0 + (i + 1) * 128)
            f = ft[i % 2]
            x = xt[i % 2]
            o = ot[i % 2]
            s_ = stt_[i % 2]
            S.dma("sp", f.ap, FF[rows, :], writes=[f])
            S.dma("sp", x.ap, resid[rows, :], writes=[x])
            memset("dve", s_.ap, 0.0, [s_])
            stt("dve", f.ap, x.ap, ALPHA, f.ap, ALU.mult, ALU.add, [x, f], [f])
            act(o.ap, f.ap, AF.Identity, [f], [o, s_], accum=s_.ap[:, 0:1])
            ts("dve", s_.ap[:, 1:2], s_.ap[:, 0:1], -1.0 / D, None, ALU.mult, None, [s_], [s_])
            act(o.ap, f.ap, AF.Square, [f, s_], [o, s_], bias=s_.ap[:, 1:2], accum=s_.ap[:, 2:3])
            ts("dve", s_.ap[:, 3:4], s_.ap[:, 2:3], 1.0 / D, None, ALU.mult, None, [s_], [s_])
            act(s_.ap[:, 4:5], s_.ap[:, 3:4], AF.Ln, [s_, epscol], [s_], bias=epscol.ap[:, 0:1])
            act(s_.ap[:, 5:6], s_.ap[:, 4:5], AF.Exp, [s_], [s_], scale=-0.5)
            ts("dve", o.ap, f.ap, s_.ap[:, 1:2], s_.ap[:, 5:6], ALU.add, ALU.mult, [f, s_], [o])
            tt("pool", o.ap, o.ap, g_.ap, ALU.mult, [o, g_], [o])
            tt("pool", o.ap, o.ap, b_.ap, ALU.add, [o, b_], [o])
            S.dma("sp", dst[rows, :], o.ap, reads=[o])

    def stage_B4(l, sb):
        row0 = sb * TSB
        UTv = UT.rearrange("c d e -> d c e")
        GT = 3
        for gt0 in range(0, TPS, GT):
            ng = min(GT, TPS - gt0)
            ntok = ng * 128
            S.fence()
            A.p = PERS
            x2T = A.bf(32 * ntok)
            x2T3 = x2T.ap.rearrange("p (c t) -> p c t", c=32)
            x2t = [TL(x2T3[:, :, i * 128:(i + 1) * 128]) for i in range(ng)]
            s1p = [A.f32(1024) for _ in range(ng)]
            s2c = [A.f32(1024) for _ in range(ng)]
            thp = [A.f32(8) for _ in range(ng)]
            mark = A.p
            load_T(X2, row0 + gt0 * 128, ng, D, x2T3, x2t, True)
            kraw = A.f32(256)
            kbf = A.bf(256)
            kT = A.bf(256)
            S.dma("sp", kraw.ap[:, 0:128], W["peer_key1"][l], writes=[kraw])
            S.dma("sp", kraw.ap[:, 128:256], W["peer_key2"][l], writes=[kraw])
            cp("dve", kbf.ap, kraw.ap, [kraw], [kbf])
            tr_bf(kbf, kbf.ap, 2, kT, kT.ap.rearrange("p (k t) -> p k t", k=2))
            qT = A.bf(16 * ntok)
            qT3 = qT.ap.rearrange("p (c t) -> p c t", c=16)
            qTt = [TL(qT3[:, c, :]) for c in range(16)]
            wq = [A.bf(32 * 128) for _ in range(2)]
            for cc in range(16):
                w = wq[cc % 2]
                w3 = w.ap.rearrange("p (c n) -> p c n", c=32)
                S.dma("pool", w3, W["peer_w_q"][l, :, cc * 128:(cc + 1) * 128].rearrange("(c p) n -> p c n", p=128), writes=[w])
                bank = nbank()
                for c in range(32):
                    mm(bank.ap[:, 0:ntok], w3[:, c, :], x2T3[:, c, :], c == 0, c == 31, [w] + x2t, [bank])
                cp(alt(), qT3[:, cc, :], bank.ap[:, 0:ntok], [bank], [qTt[cc]])
            sc = A.f32(2048)
            sc3 = sc.ap.rearrange("p (a k) -> p a k", a=16)
            sc4 = sc.ap.rearrange("p (h f k) -> p h f k", h=8, f=2)
            m16 = A.f32(256)
            m163 = m16.ap.rearrange("p (a k) -> p a k", a=16)
            m164 = m16.ap.rearrange("p (h f k) -> p h f k", h=8, f=2)
            wk = [A.f32(128) for _ in range(2)]
            cand = [A.f32(256) for _ in range(2)]
            wk2 = [A.f32(256) for _ in range(2)]
            c16 = A.f32(128)
            c163 = c16.ap.rearrange("p (h k) -> p h k", h=8)
            ex = A.f32(128)
            ex3 = ex.ap.rearrange("p (h k) -> p h k", h=8)
            sm8 = A.f32(32)
            for ti in range(ng):
                bks = [nbank(), nbank(), nbank(), nbank()]
                for a in range(16):
                    mm(bks[a // 4].ap[:, (a % 4) * 128:(a % 4 + 1) * 128], qT3[:, a, ti * 128:(ti + 1) * 128],
                       kT.ap[:, (a % 2) * 128:(a % 2 + 1) * 128], True, True, [qTt[a], kT], [bks[a // 4]])
                for q_ in range(4):
                    cp(alt(), sc.ap[:, q_ * 512:(q_ + 1) * 512], bks[q_].ap, [bks[q_]], [sc])
                for a in range(16):
                    w_ = wk[a % 2]
                    S.op("dve", lambda e, a=a: e.max(out=m163[:, a, 0:8], in_=sc3[:, a, :]), [sc], [m16])
                    S.op("dve", lambda e, a=a, w_=w_: e.match_replace(out=w_.ap, in_to_replace=m163[:, a, 0:8],
                                                                     in_values=sc3[:, a, :], imm_value=-1e30), [sc, m16], [w_])
                    S.op("dve", lambda e, a=a, w_=w_: e.max(out=m163[:, a, 8:16], in_=w_.ap), [w_], [m16])
                for h in range(8):
                    cd = cand[h % 2]
                    w2 = wk2[h % 2]
                    tt("pool", cd.ap.rearrange("p (i j) -> p i j", i=16),
                       m164[:, h, 0, :].unsqueeze(2).to_broadcast([128, 16, 16]),
                       m164[:, h, 1, :].unsqueeze(1).to_broadcast([128, 16, 16]), ALU.add, [m16], [cd])
                    S.op("dve", lambda e, h=h, cd=cd: e.max(out=c163[:, h, 0:8], in_=cd.ap), [cd], [c16])
                    S.op("dve", lambda e, h=h, cd=cd, w2=w2: e.match_replace(out=w2.ap, in_to_replace=c163[:, h, 0:8],
                                                                            in_values=cd.ap, imm_value=-1e30), [cd, c16], [w2])
                    S.op("dve", lambda e, h=h, w2=w2: e.max(out=c163[:, h, 8:16], in_=w2.ap), [w2], [c16])
                tt("dve", ex3, c163, c163[:, :, 0:1].to_broadcast([128, 8, 16]), ALU.subtract, [c16], [ex])
                act(ex.ap, ex.ap, AF.Exp, [ex], [ex])
                S.op("dve", lambda e: e.tensor_reduce(out=sm8.ap[:, 0:8], in_=ex3, axis=AX.X, op=ALU.add), [ex], [sm8])
                act(sm8.ap[:, 8:16], sm8.ap[:, 0:8], AF.Ln, [sm8], [sm8])
                tt("dve", sm8.ap[:, 16:24], sm8.ap[:, 8:16], c163[:, :, 0], ALU.add, [sm8, c16], [sm8])
                tt("dve", thp[ti].ap, c163[:, :, 15], sm8.ap[:, 16:24], ALU.subtract, [c16, sm8], [thp[ti]])
                tt("dve", s1p[ti].ap.rearrange("p (h k) -> p h k", h=8), sc4[:, :, 0, :],
                   sm8.ap[:, 16:24].unsqueeze(2).to_broadcast([128, 8, 128]), ALU.subtract, [sc, sm8], [s1p[ti]])
                cp("pool", s2c[ti].ap.rearrange("p (h k) -> p h k", h=8), sc4[:, :, 1, :], [sc], [s2c[ti]])
            S.fence()
            A.p = mark
            utb = A.bf(32 * 512)
            utb3 = utb.ap.rearrange("p (c e) -> p c e", c=32)
            vb = A.bf(4 * 2048)
            vb3 = vb.ap.rearrange("p (c d) -> p c d", c=4)
            acc = [A.f32(D) for _ in range(ng)]
            Ap = A.f32(2048)
            Eb = A.bf(2048)
            Ap4 = Ap.ap.rearrange("p (h a k) -> p h a k", h=4, a=4)
            Eb4 = Eb.ap.rearrange("p (h a k) -> p h a k", h=4, a=4)
            wt = A.f32(512)
            wt2 = A.f32(512)
            gl = A.f32(512)
            wgb = A.bf(512)
            wgT = [A.bf(512) for _ in range(ng)]
            vtmp = A.f32(2048)
            for eb_ in range(32):
                S.dma("sp", utb3, UTv[:, :, eb_ * 512:(eb_ + 1) * 512], writes=[utb])
                for ti in range(ng):
                    s1v = s1p[ti].ap.rearrange("p (h k) -> p h k", h=8)
                    s2v = s2c[ti].ap.rearrange("p (h k) -> p h k", h=8)
                    for hh in range(2):
                        hs_ = slice(hh * 4, hh * 4 + 4)
                        tt("pool", Ap4, s1v[:, hs_, eb_ * 4:eb_ * 4 + 4].unsqueeze(3).to_broadcast([128, 4, 4, 128]),
                           s2v[:, hs_, :].unsqueeze(2).to_broadcast([128, 4, 4, 128]), ALU.add, [s1p[ti], s2c[ti]], [Ap])
                        act(Eb.ap, Ap.ap, AF.Exp, [Ap], [Eb])
                        tt("dve", Ap4, Ap4, thp[ti].ap[:, hs_].unsqueeze(2).unsqueeze(3).to_broadcast([128, 4, 4, 128]),
                           ALU.is_ge, [Ap, thp[ti]], [Ap])
                        tt("dve", Ap.ap, Ap.ap, Eb.ap, ALU.mult, [Ap, Eb], [Ap])
                        dst_ = wt if hh == 0 else wt2
                        S.op("dve", lambda e, dst_=dst_: e.tensor_reduce(out=dst_.ap, in_=Ap.ap.rearrange("p (h k) -> p k h", h=4),
                                                                        axis=AX.X, op=ALU.add), [Ap], [dst_])
                    tt("pool", wt.ap, wt.ap, wt2.ap, ALU.add, [wt, wt2], [wt])
                    hb = nbank(0, 2)
                    for c in range(32):
                        mm(hb.ap, x2T3[:, c, ti * 128:(ti + 1) * 128], utb3[:, c, :], c == 0, c == 31, [x2t[ti], utb], [hb])
                    act(gl.ap, hb.ap, AF.Gelu_apprx_tanh, [hb], [gl])
                    tt("dve", wgb.ap, gl.ap, wt.ap, ALU.mult, [gl, wt], [wgb])
                    tr_bf(wgb, wgb.ap, 4, wgT[ti], wgT[ti].ap.rearrange("p (c t) -> p c t", c=4))
                for half in range(2):
                    S.dma("pool", vb3, W["peer_v"][l, eb_ * 512:(eb_ + 1) * 512, half * 2048:(half + 1) * 2048]
                          .rearrange("(c p) d -> p c d", p=128), writes=[vb])
                    for ti in range(ng):
                        wT3 = wgT[ti].ap.rearrange("p (c t) -> p c t", c=4)
                        for db in range(4):
                            for ch in range(4):
                                mm(PB[4 + db].ap, wT3[:, ch, :], vb3[:, ch, db * 512:(db + 1) * 512], ch == 0, ch == 3,
                                   [wgT[ti], vb], [PB[4 + db]])
                        a_ = acc[ti].ap[:, half * 2048:(half + 1) * 2048]
                        if eb_ == 0:
                            cp("act", a_, ps_t[:, 2048:4096], PB[4:8], [acc[ti]])
                        else:
                            cp("act", vtmp.ap, ps_t[:, 2048:4096], PB[4:8], [vtmp])
                            tt("dve", a_, a_, vtmp.ap, ALU.add, [vtmp, acc[ti]], [acc[ti]])
            for ti in range(ng):
                rows = slice(row0 + (gt0 + ti) * 128, row0 + (gt0 + ti + 1) * 128)
                S.dma("sp", FF[rows, :], acc[ti].ap, reads=[acc[ti]])

    nst = [0]

    def run_stage(f, *a):
        if stop_after is not None and nst[0] >= stop_after:
            return
        nst[0] += 1
        f(*a)

    for l in range(NLAYER):
        xsrc = x_in if l == 0 else X1
        dst = X1 if l < NLAYER - 1 else y_out
        run_stage(gen_s5, l)
        run_stage(gen_ut, l)
        for sb in range(NSB):
            run_stage(stage_XA, l, sb, xsrc)
            run_stage(stage_G, l, sb)
            run_stage(stage_S, l, sb)
            run_stage(stage_B12, l, sb)
            run_stage(stage_B3, l, sb)
            run_stage(stage_LN, l, sb, xsrc, "ln1_g", "ln1_b", X2)
            run_stage(stage_B4, l, sb)
            run_stage(stage_LN, l, sb, X2, "ln2_g", "ln2_b", dst)
    S.emit()
    st.close()
    return nc


_NC_CACHE = {}


def run_cfg(inp, NSB, NPT, NCORE, NLAYER=2, stop_after=None, debug=False):
    key = (NSB, NPT, NLAYER, stop_after, debug)
    if key not in _NC_CACHE:
        _NC_CACHE[key] = build(NSB, NPT, NLAYER, stop_after, debug)
    nc = _NC_CACHE[key]
    TPS = NPT + 1
    NS = NSB * 16
    PL = NPT * 128
    xp = np.asarray(inp["x_prompt"], np.float32)
    xs = np.asarray(inp["x_sample"], np.float32)
    in_maps = []
    for i in range(NCORE):
        parts = []
        for sb in range(NSB):
            parts.append(xp[i, sb * PL:(sb + 1) * PL])
            parts.append(xs[i * NS + sb * 16:i * NS + (sb + 1) * 16].reshape(128, D))
        m = {"x": np.ascontiguousarray(np.concatenate(parts, 0)),
             "sgla": np.ascontiguousarray(inp["state_gla"][:, i * NS:(i + 1) * NS]),
             "ss5re": np.ascontiguousarray(inp["state_s5_re"][:, i * NS:(i + 1) * NS]),
             "ss5im": np.ascontiguousarray(inp["state_s5_im"][:, i * NS:(i + 1) * NS])}
        for n, _ in WSHAPES:
            if ONLY is None or n in ONLY:
                m[n] = np.asarray(inp[n], np.float32)
        in_maps.append(m)
    res = run_bass_kernel_spmd(nc, in_maps, core_ids=list(range(NCORE)))
    R = res.results
    if debug:
        return R
    yp = np.zeros((NCORE, NSB * PL, D), np.float32)
    ys = np.zeros((NCORE * NS, 8, D), np.float32)
    for i in range(NCORE):
        y = np.asarray(R[i]["y"]).reshape(NSB, TPS * 128, D)
        for sb in range(NSB):
            yp[i, sb * PL:(sb + 1) * PL] = y[sb, :PL]
            ys[i * NS + sb * 16:i * NS + (sb + 1) * 16] = y[sb, PL:].reshape(16, 8, D)
    st1 = lambda k: np.stack([np.asarray(R[i][k]) for i in range(NCORE)], 1)
    cat1 = lambda k: np.concatenate([np.asarray(R[i][k]) for i in range(NCORE)], 1)
    return (yp, ys, st1("glap"), st1("s5rep"), st1("s5imp"), cat1("glas"), cat1("s5res"), cat1("s5ims"))


def kernel(**inputs):
    return run_cfg(inputs, 2, 8, 4)
```

```python
import math
from contextlib import ExitStack
import numpy as np
import concourse.bass as bass
import concourse.mybir as mybir
from concourse.bass_utils import run_bass_kernel_spmd

F32 = mybir.dt.float32
BF16 = mybir.dt.bfloat16
I32 = mybir.dt.int32
ALU = mybir.AluOpType
AF = mybir.ActivationFunctionType
AX = mybir.AxisListType

ENGS = ("pe", "act", "dve", "pool", "sp")
NDSEM = 8
D = 4096
ALPHA = 4.0 ** 0.25
TWO_PI = 2.0 * math.pi


class Buf:
    __slots__ = ("w", "r")

    def __init__(self):
        self.w = None
        self.r = []


class TL:
    __slots__ = ("ap", "b")

    def __init__(self, ap):
        self.ap = ap
        self.b = Buf()


class Op:
    __slots__ = ("eng", "fn", "deps", "dma", "needed", "sig", "dsem", "dval", "idx")

    def __init__(self, eng, fn, dma):
        self.eng = eng
        self.fn = fn
        self.dma = dma
        self.deps = []
        self.needed = False
        self.sig = 0
        self.dsem = None
        self.dval = 0
        self.idx = 0


class Sched:
    def __init__(self, nc):
        self.nc = nc
        self.ops = {e: [] for e in ENGS}
        self.ndma = {e: 0 for e in ENGS}
        self.lastc = {}
        self.dmas = []

    def op(self, eng, fn, reads=(), writes=(), dma=False):
        o = Op(eng, fn, dma)
        deps = []
        for t in reads:
            if t.b.w is not None:
                deps.append(t.b.w)
        for t in writes:
            if t.b.w is not None:
                deps.append(t.b.w)
            deps.extend(t.b.r)
        seen = set()
        for d in deps:
            if id(d) in seen:
                continue
            seen.add(id(d))
            if eng == "pe" and d.eng == "pe" and not d.dma and not dma:
                continue
            o.deps.append(d)
        for t in reads:
            t.b.r.append(o)
        for t in writes:
            t.b.w = o
            t.b.r = []
        if dma:
            o.idx = self.ndma[eng]
            self.ndma[eng] += 1
            self.dmas.append(o)
        else:
            self.lastc[eng] = o
        self.ops[eng].append(o)
        return o

    def dma(self, eng, out, in_, reads=(), writes=(), **kw):
        return self.op(eng, lambda e: e.dma_start(out=out, in_=in_, **kw), reads, writes, dma=True)

    def fence(self):
        deps = list(self.lastc.values()) + self.dmas
        self.dmas = []
        for e in ENGS:
            o = Op(e, None, False)
            o.deps = list(deps)
            self.ops[e].append(o)

    def emit(self):
        nc = self.nc
        self.fence()
        for e in ENGS:
            for o in self.ops[e]:
                for d in o.deps:
                    d.needed = True
        for e in ENGS:
            c = 0
            for o in self.ops[e]:
                if o.fn is not None and not o.dma and o.needed:
                    c += 1
                    o.sig = c
        with ExitStack() as st:
            csem = {e: st.enter_context(nc.semaphore("c_" + e)) for e in ENGS if e != "sp"}
            dsem = {e: [st.enter_context(nc.semaphore("d_%s%d" % (e, i))) for i in range(NDSEM)]
                    for e in ("sp", "act", "pool")}
            for e in ENGS:
                for o in self.ops[e]:
                    if o.dma:
                        o.dsem = dsem[e][o.idx % NDSEM]
                        o.dval = 16 * (o.idx // NDSEM + 1)
            block = st.enter_context(nc.Block())
            engobj = {"pe": block.tensor, "act": block.scalar, "dve": block.vector,
                      "pool": block.gpsimd, "sp": block.sync}

            def mk(e):
                def body(eng):
                    waited = {}

                    def wait(sem, val):
                        if waited.get(id(sem), 0) >= val:
                            return
                        waited[id(sem)] = val
                        eng.wait_ge(sem, val)

                    for o in self.ops[e]:
                        for d in o.deps:
                            if d.dma:
                                wait(d.dsem, d.dval)
                            else:
                                wait(csem[d.eng], d.sig)
                        if o.fn is None:
                            continue
                        if o.dma and o.dval > 16:
                            wait(o.dsem, o.dval - 16)
                        ins = o.fn(eng)
                        if o.dma:
                            ins.then_inc(o.dsem, 16)
                        elif o.needed:
                            ins.then_inc(csem[e], 1)
                return body

            for e in ENGS:
                engobj[e](mk(e))


class Arena:
    def __init__(self, ap):
        self.ap = ap
        self.p = 0
        self.n = ap.shape[1]

    def f32(self, n, parts=128):
        a = self.ap[0:parts, self.p:self.p + n]
        self.p += n
        assert self.p <= self.n, "SBUF arena overflow %d > %d" % (self.p, self.n)
        return TL(a)

    def bf(self, n, parts=128):
        m = (n + 1) // 2
        a = self.ap[0:parts, self.p:self.p + m].bitcast(BF16)[:, 0:n]
        self.p += m
        assert self.p <= self.n, "SBUF arena overflow %d > %d" % (self.p, self.n)
        return TL(a)

    def i32(self, n, parts=128):
        a = self.ap[0:parts, self.p:self.p + n].bitcast(I32)
        self.p += n
        assert self.p <= self.n
        return TL(a)


ONLY = None
CUT = None
WSHAPES = [
    ("ln1_g", (2, D)), ("ln1_b", (2, D)), ("w_in", (2, D, 16400)), ("gla_w_a2", (2, 16, 1024)),
    ("gla_b_a", (2, 1024)), ("gla_norm_g", (2, 512)), ("s5_a_re", (2, 128, 64)), ("s5_a_im", (2, 128, 64)),
    ("s5_log_dt", (2, 128)), ("s5_b_re", (2, 128, 64, 16)), ("s5_b_im", (2, 128, 64, 16)),
    ("s5_c_re", (2, 128, 16, 64)), ("s5_c_im", (2, 128, 16, 64)), ("s5_d", (2, 2048)),
    ("s5_w_glu", (2, 2048, 2048)), ("s5_b_glu", (2, 2048)), ("w_branch", (2, D, D)), ("w_out", (2, D, D)),
    ("ln2_g", (2, D)), ("ln2_b", (2, D)), ("peer_w_q", (2, D, 2048)), ("peer_key1", (2, 128, 128)),
    ("peer_key2", (2, 128, 128)), ("peer_u", (2, 16384, D)), ("peer_v", (2, 16384, D)),
]


def build(NSB, NPT, NLAYER=2, stop_after=None, debug=False, PAIR=False, NCORE=8):
    TPS = NPT + 1
    TSB = TPS * 128
    T = NSB * TSB
    TLc = TSB if PAIR else T
    NS = NSB * 16
    NCP = NPT * 16
    NCB = NCP + 16
    nc = bass.Bass("TRN2", target_bir_lowering=False)
    di = lambda n, s, dt=F32: nc.dram_tensor(n, list(s), dt, kind="ExternalInput").ap()
    do = lambda n, s, dt=F32: nc.dram_tensor(n, list(s), dt, kind="ExternalOutput").ap()
    ds = lambda n, s, dt=F32: nc.dram_tensor(n, list(s), dt, kind=("ExternalOutput" if debug else "Internal")).ap()
    x_in = di("x", [TLc, D])
    pm_in = di("pm", [128, 2]) if PAIR else None
    sgla = di("sgla", [2, NS, 4, 256, 512])
    ss5 = [di("ss5re", [2, NS, 128, 64]), di("ss5im", [2, NS, 128, 64])]
    W = {n: di(n, s) for n, s in WSHAPES if ONLY is None or n in ONLY}
    y_out = do("y", [TLc, D])
    glap = do("glap", [2, 4, 256, 512])
    s5p = [do("s5rep", [2, 128, 64]), do("s5imp", [2, 128, 64])]
    glas = do("glas", [2, NS, 4, 256, 512])
    s5s = [do("s5res", [2, NS, 128, 64]), do("s5ims", [2, NS, 128, 64])]
    X1 = ds("X1", [TLc, D])
    X2 = ds("X2", [TLc, D])
    FF = ds("FF", [TLc, D])
    QKw = ds("QKw", [TLc, 2048])
    Vsw = ds("Vsw", [TLc, 2048], BF16)
    RSw = ds("RSw", [TLc, 2048], BF16)
    ALRw = ds("ALRw", [TLc, 16])
    Usw = ds("Usw", [TLc, 2048], BF16)
    if PAIR:
        QK = ds("QKg", [T, 2048])
        Vs = ds("Vsg", [T, 2048], BF16)
        RS = ds("RSg", [T, 2048], BF16)
        ALR = ds("ALRg", [T, 16])
        Us = ds("Usg", [T, 2048], BF16)
    else:
        QK, Vs, RS, ALR, Us = QKw, Vsw, RSw, ALRw, Usw
    Z = ds("Z", [TLc, 8192], BF16)
    OG = ds("OG", [T, 2048], BF16)
    YG = ds("YG", [T, 2048], BF16)
    MM = ds("MM", [TLc, D], BF16)
    UT = ds("UT", [32, 128, 16384], BF16)
    V16 = ds("V16", [16384, D], BF16)
    SMA = ds("SMA", [128, 128, 256], BF16)
    SMC = ds("SMC", [64, 128, 256], BF16)
    GST = ds("GST", [128, 8, 512])

    S = Sched(nc)
    st = ExitStack()
    NA = 51 * 1024 + 512
    arena_t = st.enter_context(nc.sbuf_tensor("arena", [128, NA], F32))
    ps_t = st.enter_context(nc.psum_tensor("ps", [128, 4096], F32))
    A = Arena(arena_t[:, :])
    PB = [TL(ps_t[:, i * 512:(i + 1) * 512]) for i in range(8)]
    bank_ctr = [0]

    def nbank(lo=0, hi=8):
        i = lo + bank_ctr[0] % (hi - lo)
        bank_ctr[0] += 1
        return PB[i]

    rr = [0]

    def alt(engs=("dve", "act")):
        rr[0] += 1
        return engs[rr[0] % len(engs)]

    def cp(eng, out, in_, reads, writes):
        if eng == "act":
            return S.op("act", lambda e: e.copy(out=out, in_=in_), reads, writes)
        return S.op(eng, lambda e: e.tensor_copy(out=out, in_=in_), reads, writes)

    def tt(eng, out, in0, in1, op, reads, writes):
        return S.op(eng, lambda e: e.tensor_tensor(out=out, in0=in0, in1=in1, op=op), reads, writes)

    def ts(eng, out, in0, s1, s2, op0, op1, reads, writes):
        if s2 is None:
            return S.op(eng, lambda e: e.tensor_scalar(out=out, in0=in0, scalar1=s1, scalar2=None, op0=op0), reads, writes)
        return S.op(eng, lambda e: e.tensor_scalar(out=out, in0=in0, scalar1=s1, scalar2=s2, op0=op0, op1=op1), reads, writes)

    def stt(eng, out, in0, scalar, in1, op0, op1, reads, writes):
        return S.op(eng, lambda e: e.scalar_tensor_tensor(out=out, in0=in0, scalar=scalar, in1=in1, op0=op0, op1=op1), reads, writes)

    def act(out, in_, func, reads, writes, bias=None, scale=None, accum=None):
        kw = {}
        if bias is not None:
            kw["bias"] = bias
        if scale is not None:
            kw["scale"] = scale
        if accum is not None:
            kw["accum_out"] = accum
        return S.op("act", lambda e: e.activation(out=out, in_=in_, func=func, **kw), reads, writes)

    def mm(out, lhsT, rhs, start, stop, reads, writes):
        return S.op("pe", lambda e: e.matmul(out, lhsT=lhsT, rhs=rhs, start=start, stop=stop), reads, writes)

    def trp(out, in_, ident, reads, writes):
        return S.op("pe", lambda e: e.transpose(out=out, in_=in_, identity=ident), reads, writes)

    def memset(eng, ap, val, writes):
        return S.op(eng, lambda e: e.memset(ap, val), (), writes)

    def asel(ap, pattern, op, base, cm, tl):
        return S.op("pool", lambda e: e.affine_select(out=ap, in_=ap, pattern=pattern, compare_op=op, fill=0.0,
                                                      base=base, channel_multiplier=cm), [tl], [tl])

    ident_b = A.bf(128)
    ident_f = A.f32(128)
    triL = A.f32(128)
    triR = A.f32(128)
    triLs = A.f32(128)
    triRs = A.f32(128)
    same = A.f32(128)
    bmask = A.f32(128)
    seqsel = A.f32(16)
    onescol = A.f32(1)
    epscol = A.f32(2)
    colmask = A.bf(16 * 128)
    XST = A.f32(256, parts=64)
    A8 = A.f32(256, parts=64)
    PERS = A.p
    colmask_f = A.f32(16 * 128)
    for t_, v_ in ((ident_f, 1.0), (triL, 1.0), (triR, 1.0), (same, 1.0), (bmask, 1.0), (seqsel, 1.0),
                   (onescol, 1.0), (colmask_f, 1.0)):
        memset("pool", t_.ap, v_, [t_])
    memset("pool", epscol.ap[:, 0:1], 1e-5, [epscol])
    memset("pool", epscol.ap[:, 1:2], 1e-6, [epscol])
    asel(ident_f.ap, [[-1, 128]], ALU.is_equal, 0, 1, ident_f)
    asel(triL.ap, [[1, 128]], ALU.is_ge, 0, -1, triL)
    asel(triR.ap, [[-1, 128]], ALU.is_ge, -1, 1, triR)
    sv = same.ap.rearrange("p (j i) -> p j i", j=16)
    asel(sv, [[-8, 16], [0, 8]], ALU.is_ge, 0, 1, same)
    asel(sv, [[8, 16], [0, 8]], ALU.is_ge, 7, -1, same)
    asel(bmask.ap.rearrange("p (t n) -> p t n", t=8), [[16, 8], [0, 16]], ALU.is_ge, 15, -1, bmask)
    asel(seqsel.ap, [[-8, 16]], ALU.is_ge, 0, 1, seqsel)
    asel(seqsel.ap, [[8, 16]], ALU.is_ge, 7, -1, seqsel)
    cmv = colmask_f.ap.rearrange("p (j t) -> p j t", j=16)
    asel(cmv, [[-8, 16], [1, 128]], ALU.is_ge, 0, 0, colmask_f)
    asel(cmv, [[8, 16], [-1, 128]], ALU.is_ge, 7, 0, colmask_f)
    cp("dve", ident_b.ap, ident_f.ap, [ident_f], [ident_b])
    cp("dve", colmask.ap, colmask_f.ap, [colmask_f], [colmask])
    tt("dve", triLs.ap, triL.ap, same.ap, ALU.mult, [triL, same], [triLs])
    tt("dve", triRs.ap, triR.ap, same.ap, ALU.mult, [triR, same], [triRs])

    def tr_bf(src_tl, src_ap, nch, dst_tl, dst_ap3, rows=128):
        for k0 in range(0, nch, 8):
            nb = min(8, nch - k0)
            bank = nbank()
            pb = bank.ap.bitcast(BF16)
            for k in range(nb):
                trp(pb[:, k * 128:k * 128 + rows], src_ap[:, (k0 + k) * 128:(k0 + k + 1) * 128],
                    ident_b.ap[0:rows, 0:rows], [src_tl, ident_b], [bank])
            cp(alt(), dst_ap3[:, k0:k0 + nb, :],
               pb[:, 0:nb * 128].rearrange("p (k t) -> p k t", k=nb)[:, :, 0:rows], [bank], [dst_tl])

    def load_T(src_dram, row0, ntiles, ncols, dstT_ap3, dst_tls, fcast, blend=False):
        bufs = [A.bf(ncols) for _ in range(2)]
        if blend:
            bufs2 = [A.bf(ncols) for _ in range(2)]
            pmt = A.f32(2)
            S.dma("sp", pmt.ap, pm_in[:, :], writes=[pmt])
            for i in range(ntiles):
                b = bufs[i % 2]
                b2 = bufs2[i % 2]
                S.dma("sp", b.ap, src_dram[i * 128:(i + 1) * 128, :], writes=[b])
                S.dma("sp", b2.ap, src_dram[TSB + i * 128:TSB + (i + 1) * 128, :], writes=[b2])
                ts("dve", b.ap, b.ap, pmt.ap[:, 0:1], None, ALU.mult, None, [b, pmt], [b])
                stt("dve", b.ap, b2.ap, pmt.ap[:, 1:2], b.ap, ALU.mult, ALU.add, [b, b2, pmt], [b])
                tr_bf(b, b.ap, ncols // 128, dst_tls[i], dstT_ap3[:, :, i * 128:(i + 1) * 128])
            return
        fb = [A.f32(ncols) for _ in range(2)] if fcast else None
        for i in range(ntiles):
            b = bufs[i % 2]
            rows = slice(row0 + i * 128, row0 + (i + 1) * 128)
            if fcast:
                f = fb[i % 2]
                S.dma("sp", f.ap, src_dram[rows, :], writes=[f])
                cp(alt(), b.ap, f.ap, [f], [b])
            else:
                S.dma("sp", b.ap, src_dram[rows, :], writes=[b])
            tr_bf(b, b.ap, ncols // 128, dst_tls[i], dstT_ap3[:, :, i * 128:(i + 1) * 128])

    def tok_blocks(n):
        return [(t0, min(512, n - t0)) for t0 in range(0, n, 512)]

    def tiles_of(t0, n):
        return list(range(t0 // 128, (t0 + n + 127) // 128))

    def stage_XA(l, sb, xsrc):
        S.fence()
        A.p = PERS
        row0 = sb * TSB
        xT_ap = A.bf(32 * TSB).ap.rearrange("p (c t) -> p c t", c=32)
        xT = [TL(xT_ap[:, :, i * 128:(i + 1) * 128]) for i in range(TPS)]
        mark = A.p
        load_T(xsrc, row0, TPS, D, xT_ap, xT, True)
        S.fence()
        A.p = mark
        wb = [A.bf(32 * 512) for _ in range(2)]
        stg = [A.f32(512) for _ in range(4)]
        segs = [(0, 2048, "qk"), (2048, 2048, "v"), (4096, 2048, "r"), (6144, 16, "alr"), (6160, 2048, "u"),
                (8208, 4096, "zg"), (12304, 4096, "zs")]
        blk = 0
        for (c0, ncs, kind) in segs:
            for n0 in range(0, ncs, 512):
                nb = min(512, ncs - n0)
                w = wb[blk % 2]
                w3 = w.ap.rearrange("p (c n) -> p c n", c=32)
                S.dma("pool", w3[:, :, 0:nb],
                      W["w_in"][l, :, c0 + n0:c0 + n0 + nb].rearrange("(c p) n -> p c n", p=128), writes=[w])
                for i in range(TPS):
                    bank = nbank()
                    for c in range(32):
                        mm(bank.ap[:, 0:nb], xT_ap[:, c, i * 128:(i + 1) * 128], w3[:, c, 0:nb], c == 0, c == 31,
                           [xT[i], w], [bank])
                    sg = stg[(blk * TPS + i) % 4]
                    rows = slice(row0 + i * 128, row0 + (i + 1) * 128)
                    sgb = sg.ap.bitcast(BF16)[:, 0:nb]
                    if kind == "qk":
                        cp("dve", sg.ap[:, 0:nb], bank.ap[:, 0:nb], [bank], [sg])
                        S.dma("sp", QKw[rows, n0:n0 + nb], sg.ap[:, 0:nb], reads=[sg])
                    elif kind == "alr":
                        cp("dve", sg.ap[:, 0:nb], bank.ap[:, 0:nb], [bank], [sg])
                        S.dma("sp", ALRw[rows, 0:16], sg.ap[:, 0:16], reads=[sg])
                    elif kind == "v":
                        cp("dve", sgb, bank.ap[:, 0:nb], [bank], [sg])
                        S.dma("sp", Vsw[rows, n0:n0 + nb], sgb, reads=[sg])
                    elif kind == "u":
                        cp("dve", sgb, bank.ap[:, 0:nb], [bank], [sg])
                        S.dma("sp", Usw[rows, n0:n0 + nb], sgb, reads=[sg])
                    elif kind == "r":
                        act(sgb, bank.ap[:, 0:nb], AF.Silu, [bank], [sg])
                        S.dma("sp", RSw[rows, n0:n0 + nb], sgb, reads=[sg])
                    else:
                        zo = n0 if kind == "zg" else 4096 + n0
                        act(sgb, bank.ap[:, 0:nb], AF.Sigmoid, [bank], [sg])
                        S.dma("sp", Z[rows, zo:zo + nb], sgb, reads=[sg])
                blk += 1

    def stage_G(l, sb):
        S.fence()
        A.p = PERS
        row0 = sb * TSB
        Sf = A.f32(8 * 512)
        Sb = A.bf(8 * 512)
        Sf3 = Sf.ap.rearrange("p (k v) -> p k v", k=8)
        Sb3 = Sb.ap.rearrange("p (k v) -> p k v", k=8)
        Sft = [TL(Sf3[:, k, :]) for k in range(8)]
        Sbt = [TL(Sb3[:, k, :]) for k in range(8)]
        waug = A.f32(1024)
        aug = A.f32(128)
        gt = A.f32(512)
        if sb == 0:
            memset("dve", Sf.ap, 0.0, Sft)
        else:
            S.dma("sp", Sf3, GST[:, :, :], writes=Sft)
        for k in range(8):
            cp("act", Sb3[:, k, :], Sf3[:, k, :], [Sft[k]], [Sbt[k]])
        memset("dve", waug.ap[0:33, :], 0.0, [waug])
        memset("dve", aug.ap[0:33, :], 0.0, [aug])
        memset("dve", aug.ap[32:33, :], 1.0, [aug])
        S.dma("sp", waug.ap[0:16, :], W["gla_w_a2"][l, :, :], writes=[waug])
        S.dma("sp", waug.ap[32:33, :], W["gla_b_a"][l:l + 1, :], writes=[waug])
        S.dma("sp", gt.ap, W["gla_norm_g"][l, :].partition_broadcast(128), writes=[gt])
        qkb = [A.f32(2048) for _ in range(2)]
        vbf = [A.bf(2048) for _ in range(2)]
        rbf = [A.bf(2048) for _ in range(2)]
        alrb = [A.f32(16) for _ in range(2)]
        la = A.f32(1024)
        eb = A.f32(1024)
        enb = A.f32(1024)
        ee = A.f32(1024)
        ebl = A.f32(128)
        qd = A.bf(1024)
        ki = A.bf(1024)
        ke = A.bf(1024)
        qdT = A.bf(1024)
        kiT = A.bf(1024)
        qdT3 = qdT.ap.rearrange("p (k t) -> p k t", k=8)
        kiT3 = kiT.ap.rearrange("p (k t) -> p k t", k=8)
        scm = [A.bf(128) for _ in range(2)]
        qm = [A.bf(2 * 16 * 128) for _ in range(2)]
        sjb = [A.bf(1024) for _ in range(3)]
        kem = [A.bf(1024) for _ in range(2)]
        sjf = [A.f32(1024) for _ in range(2)]
        sjo = [A.f32(1024) for _ in range(2)]
        on = A.f32(512)
        junk = A.f32(512)
        scf = A.f32(128)
        obs = A.f32(512)
        stmp = [A.f32(512) for _ in range(2)]
        ogb = [A.bf(2048) for _ in range(2)]
        ss = A.f32(16)
        for i in range(TPS):
            sample = (i == NPT)
            rows = slice(row0 + i * 128, row0 + (i + 1) * 128)
            qk = qkb[i % 2]
            v = vbf[i % 2]
            r = rbf[i % 2]
            alr = alrb[i % 2]
            og = ogb[i % 2]
            S.dma("sp", qk.ap, QK[rows, :], writes=[qk])
            S.dma("sp", v.ap, Vs[rows, :], writes=[v])
            S.dma("sp", r.ap, RS[rows, :], writes=[r])
            S.dma("sp", alr.ap, ALR[rows, :], writes=[alr])
            bk = nbank()
            trp(bk.ap[0:16, 0:128], alr.ap, ident_f.ap, [alr, ident_f], [bk])
            cp("dve", aug.ap[0:16, :], bk.ap[0:16, 0:128], [bk], [aug])
            b0, b1 = nbank(), nbank()
            mm(b0.ap, aug.ap[0:33, :], waug.ap[0:33, 0:512], True, True, [aug, waug], [b0])
            mm(b1.ap, aug.ap[0:33, :], waug.ap[0:33, 512:1024], True, True, [aug, waug], [b1])
            act(la.ap[:, 0:512], b0.ap, AF.Exp, [b0], [la], scale=-1.0)
            act(la.ap[:, 512:1024], b1.ap, AF.Exp, [b1], [la], scale=-1.0)
            act(la.ap, la.ap, AF.Ln, [la, onescol], [la], bias=onescol.ap)
            ts("dve", la.ap, la.ap, -1.0 / 16.0, None, ALU.mult, None, [la], [la])
            tl_ = triLs if sample else triL
            tr_ = triRs if sample else triR
            nsq = 16 if sample else 1
            sel = seqsel if sample else onescol
            bb = [nbank(), nbank()]
            be = [nbank(), nbank()]
            for hh in range(2):
                mm(bb[hh].ap, tl_.ap, la.ap[:, hh * 512:(hh + 1) * 512], True, True, [tl_, la], [bb[hh]])
                mm(be[hh].ap, tr_.ap, la.ap[:, hh * 512:(hh + 1) * 512], True, True, [tr_, la], [be[hh]])
            bl = nbank()
            for dc in range(8):
                mm(bl.ap[:, dc * 16:dc * 16 + nsq], la.ap[:, dc * 128:(dc + 1) * 128], sel.ap[:, 0:nsq], True, True,
                   [la, sel], [bl])
            for hh in range(2):
                sl = slice(hh * 512, (hh + 1) * 512)
                act(eb.ap[:, sl], bb[hh].ap, AF.Exp, [bb[hh]], [eb])
                act(enb.ap[:, sl], bb[hh].ap, AF.Exp, [bb[hh]], [enb], scale=-1.0)
                act(ee.ap[:, sl], be[hh].ap, AF.Exp, [be[hh]], [ee])
            eblv = ebl.ap.rearrange("p (d j) -> p d j", j=16)
            act(eblv[:, :, 0:nsq], bl.ap[:, 0:128].rearrange("p (d j) -> p d j", j=16)[:, :, 0:nsq], AF.Exp, [bl], [ebl])
            stt("dve", qd.ap, qk.ap[:, 0:1024], 0.0625, eb.ap, ALU.mult, ALU.mult, [qk, eb], [qd])
            tt("pool", ki.ap, qk.ap[:, 1024:2048], enb.ap, ALU.mult, [qk, enb], [ki])
            tt("pool", ke.ap, qk.ap[:, 1024:2048], ee.ap, ALU.mult, [qk, ee], [ke])
            tr_bf(qd, qd.ap, 8, qdT, qdT3)
            tr_bf(ki, ki.ap, 8, kiT, kiT3)
            memset("dve", ss.ap, 0.0, [ss])
            for h in range(4):
                vh = v.ap[:, h * 512:(h + 1) * 512]
                sc = nbank()
                for j in range(2):
                    mm(sc.ap[:, 0:128], kiT3[:, 2 * h + j, :], qdT3[:, 2 * h + j, :], j == 0, j == 1, [kiT, qdT], [sc])
                sm_ = scm[h % 2]
                cp("act", scf.ap, sc.ap[:, 0:128], [sc], [scf])
                tt("dve", sm_.ap, scf.ap, tl_.ap, ALU.mult, [scf, tl_], [sm_])
                ob = nbank()
                mm(ob.ap, sm_.ap, vh, True, False, [sm_, v], [ob])
                if not sample:
                    for j in range(2):
                        mm(ob.ap, qdT3[:, 2 * h + j, :], Sb3[:, 2 * h + j, :], False, j == 1, [qdT, Sbt[2 * h + j]], [ob])
                else:
                    q_ = qm[h % 2]
                    q4 = q_.ap.rearrange("p (j s t) -> p j s t", j=2, s=16)
                    for j in range(2):
                        tt("dve", q4[:, j], qdT3[:, 2 * h + j, :].unsqueeze(1).to_broadcast([128, 16, 128]),
                           colmask.ap.rearrange("p (s t) -> p s t", s=16), ALU.mult, [qdT, colmask], [q_])
                    for sq in range(16):
                        sj = sjb[(h * 16 + sq) % 3]
                        sj3 = sj.ap.rearrange("p (j v) -> p j v", j=2)
                        S.dma("pool", sj3, sgla[l, sb * 16 + sq, h].rearrange("(j p) v -> p j v", p=128), writes=[sj])
                        for j in range(2):
                            mm(ob.ap, q4[:, j, sq, :], sj3[:, j, :], False, (sq == 15 and j == 1), [q_, sj], [ob])
                if not sample:
                    for j in range(2):
                        k = 2 * h + j
                        sbk = nbank()
                        mm(sbk.ap, ke.ap[:, k * 128:(k + 1) * 128], vh, True, True, [ke, v], [sbk])
                        sp_ = stmp[k % 2]
                        cp("act", sp_.ap, sbk.ap, [sbk], [sp_])
                        stt("dve", Sf3[:, k, :], Sf3[:, k, :], ebl.ap[:, k * 16:k * 16 + 1], sp_.ap, ALU.mult, ALU.add,
                            [Sft[k], ebl, sp_], [Sft[k]])
                        cp("act", Sb3[:, k, :], Sf3[:, k, :], [Sft[k]], [Sbt[k]])
                cp("act", obs.ap, ob.ap, [ob], [obs])
                act(junk.ap, obs.ap, AF.Square, [obs], [junk, ss], accum=ss.ap[:, h:h + 1])
                ts("dve", ss.ap[:, 4 + h:5 + h], ss.ap[:, h:h + 1], 1.0 / 512.0, None, ALU.mult, None, [ss], [ss])
                act(ss.ap[:, 8 + h:9 + h], ss.ap[:, 4 + h:5 + h], AF.Ln, [ss, epscol], [ss], bias=epscol.ap[:, 1:2])
                act(ss.ap[:, 12 + h:13 + h], ss.ap[:, 8 + h:9 + h], AF.Exp, [ss], [ss], scale=-0.5)
                stt("dve", on.ap, obs.ap, ss.ap[:, 12 + h:13 + h], gt.ap, ALU.mult, ALU.mult, [obs, ss, gt], [on])
                tt("pool", og.ap[:, h * 512:(h + 1) * 512], on.ap, r.ap[:, h * 512:(h + 1) * 512], ALU.mult, [on, r], [og])
            S.dma("sp", OG[rows, :], og.ap, reads=[og])
            if sample:
                for sq in range(16):
                    km = kem[sq % 2]
                    ts("dve", km.ap, ke.ap, seqsel.ap[:, sq:sq + 1], None, ALU.mult, None, [ke, seqsel], [km])
                    for h in range(4):
                        vh = v.ap[:, h * 512:(h + 1) * 512]
                        f_ = sjf[(sq * 4 + h) % 2]
                        o_ = sjo[(sq * 4 + h) % 2]
                        f3 = f_.ap.rearrange("p (j v) -> p j v", j=2)
                        o3 = o_.ap.rearrange("p (j v) -> p j v", j=2)
                        S.dma("sp", f3, sgla[l, sb * 16 + sq, h].rearrange("(j p) v -> p j v", p=128), writes=[f_])
                        for j in range(2):
                            k = 2 * h + j
                            sbk = nbank()
                            mm(sbk.ap, km.ap[:, k * 128:(k + 1) * 128], vh, True, True, [km, v], [sbk])
                            sp_ = stmp[k % 2]
                            cp("act", sp_.ap, sbk.ap, [sbk], [sp_])
                            stt("dve", o3[:, j, :], f3[:, j, :], ebl.ap[:, k * 16 + sq:k * 16 + sq + 1], sp_.ap,
                                ALU.mult, ALU.add, [f_, ebl, sp_], [o_])
                        S.dma("sp", glas[l, sb * 16 + sq, h].rearrange("(j p) v -> p j v", p=128), o3, reads=[o_])
        if sb == NSB - 1:
            S.dma("sp", glap[l].rearrange("h (j p) v -> p (h j) v", p=128), Sf3, reads=Sft)
        else:
            S.dma("sp", GST[:, :, :], Sf3, reads=Sft)

    def gen_s5(l):
        S.fence()
        A.p = PERS
        araw = A.f32(128)
        are = A.f32(128, parts=64)
        aim = A.f32(128, parts=64)
        dt = A.f32(128, parts=64)
        ldr = A.f32(128, parts=64)
        th = A.f32(128, parts=64)
        P = A.f32(2 * 17 * 128, parts=64)
        Q = A.f32(2 * 16 * 128, parts=64)
        P4 = P.ap.rearrange("p (r k g) -> p r k g", r=2, k=17)
        Q4 = Q.ap.rearrange("p (r k g) -> p r k g", r=2, k=16)
        t1 = A.f32(128, parts=64)
        t2 = A.f32(128, parts=64)
        t3 = A.f32(128, parts=64)
        ti_ = A.i32(128, parts=64)
        mag = A.f32(128, parts=64)
        cr = A.f32(128, parts=64)
        ci = A.f32(128, parts=64)
        big1 = A.f32(16 * 128, parts=64)
        big2 = A.f32(16 * 128, parts=64)
        S.dma("sp", araw.ap[:, 0:64], W["s5_a_re"][l], writes=[araw])
        S.dma("sp", araw.ap[:, 64:128], W["s5_a_im"][l], writes=[araw])
        bk = nbank()
        trp(bk.ap[0:64, 0:128], araw.ap[:, 0:64], ident_f.ap, [araw, ident_f], [bk])
        trp(bk.ap[0:64, 128:256], araw.ap[:, 64:128], ident_f.ap, [araw, ident_f], [bk])
        cp("dve", are.ap, bk.ap[0:64, 0:128], [bk], [are])
        cp("dve", aim.ap, bk.ap[0:64, 128:256], [bk], [aim])
        S.dma("sp", dt.ap, W["s5_log_dt"][l, :].partition_broadcast(64), writes=[dt])
        act(dt.ap, dt.ap, AF.Exp, [dt], [dt])
        tt("dve", ldr.ap, are.ap, dt.ap, ALU.mult, [are, dt], [ldr])
        tt("dve", th.ap, aim.ap, dt.ap, ALU.mult, [aim, dt], [th])
        for kk in range(17):
            k = kk - 8
            act(mag.ap, ldr.ap, AF.Exp, [ldr], [mag], scale=float(k))
            for r_, off in ((0, math.pi / 2), (1, 0.0)):
                ts("dve", t1.ap, th.ap, float(k), off, ALU.mult, ALU.add, [th], [t1])
                ts("dve", ti_.ap, t1.ap, 1.0 / TWO_PI, None, ALU.mult, None, [t1], [ti_])
                cp("dve", t2.ap, ti_.ap, [ti_], [t2])
                stt("dve", t3.ap, t2.ap, -TWO_PI, t1.ap, ALU.mult, ALU.add, [t2, t1], [t3])
                ts("dve", t3.ap, t3.ap, -3.14159, 3.14159, ALU.max, ALU.min, [t3], [t3])
                act(t2.ap, t3.ap, AF.Sin, [t3], [t2])
                tt("dve", P4[:, r_, kk, :], mag.ap, t2.ap, ALU.mult, [mag, t2], [P])
        if CUT == "a":
            return
        ts("dve", t1.ap, P4[:, 0, 9, :], -1.0, None, ALU.add, None, [P], [t1])
        tt("dve", t2.ap, are.ap, are.ap, ALU.mult, [are], [t2])
        tt("dve", t3.ap, aim.ap, aim.ap, ALU.mult, [aim], [t3])
        tt("dve", t2.ap, t2.ap, t3.ap, ALU.add, [t2, t3], [t2])
        S.op("dve", lambda e: e.reciprocal(out=t2.ap, in_=t2.ap), [t2], [t2])
        tt("dve", cr.ap, t1.ap, are.ap, ALU.mult, [t1, are], [cr])
        tt("dve", t3.ap, P4[:, 1, 9, :], aim.ap, ALU.mult, [P, aim], [t3])
        tt("dve", cr.ap, cr.ap, t3.ap, ALU.add, [cr, t3], [cr])
        tt("dve", cr.ap, cr.ap, t2.ap, ALU.mult, [cr, t2], [cr])
        tt("dve", ci.ap, P4[:, 1, 9, :], are.ap, ALU.mult, [P, are], [ci])
        tt("dve", t3.ap, t1.ap, aim.ap, ALU.mult, [t1, aim], [t3])
        tt("dve", ci.ap, ci.ap, t3.ap, ALU.subtract, [ci, t3], [ci])
        tt("dve", ci.ap, ci.ap, t2.ap, ALU.mult, [ci, t2], [ci])
        crb = cr.ap.unsqueeze(1).to_broadcast([64, 16, 128])
        cib = ci.ap.unsqueeze(1).to_broadcast([64, 16, 128])
        b1v = big1.ap.rearrange("p (k g) -> p k g", k=16)
        b2v = big2.ap.rearrange("p (k g) -> p k g", k=16)
        tt("dve", b1v, P4[:, 0, 0:16, :], crb, ALU.mult, [P, cr], [big1])
        tt("dve", b2v, P4[:, 1, 0:16, :], cib, ALU.mult, [P, ci], [big2])
        tt("dve", Q4[:, 0], b1v, b2v, ALU.subtract, [big1, big2], [Q])
        tt("dve", b1v, P4[:, 1, 0:16, :], crb, ALU.mult, [P, cr], [big1])
        tt("dve", b2v, P4[:, 0, 0:16, :], cib, ALU.mult, [P, ci], [big2])
        tt("dve", Q4[:, 1], b1v, b2v, ALU.add, [big1, big2], [Q])
        cp("dve", A8.ap.rearrange("p (r g) -> p r g", r=2), P4[:, :, 16, :], [P], [A8])
        if CUT == "b":
            return
        mark = A.p
        GG = 16
        for gc in range(128 // GG):
            g0 = gc * GG
            S.fence()
            A.p = mark
            BRI = A.f32(2 * GG * 16, parts=64)
            BRI4 = BRI.ap.rearrange("p (r g m) -> p r g m", r=2, g=GG)
            craw = A.f32(2 * 16 * 64, parts=GG)
            craw4 = craw.ap.rearrange("g (r n p) -> g r n p", r=2, n=16)
            CR = A.f32(2 * GG * 16, parts=64)
            CR4 = CR.ap.rearrange("p (r g n) -> p r g n", r=2, g=GG)
            BTW = A.f32(2 * GG * 128, parts=64)
            BTT = A.f32(2 * GG * 128, parts=64)
            CB = A.f32(2 * GG * 128, parts=64)
            BTW5 = BTW.ap.rearrange("p (r g s m) -> p r g s m", r=2, g=GG, s=8)
            BTT5 = BTT.ap.rearrange("p (r g s m) -> p r g s m", r=2, g=GG, s=8)
            CB5 = CB.ap.rearrange("p (r g t n) -> p r g t n", r=2, g=GG, t=8)
            BTW4 = BTW.ap.rearrange("p (r g k) -> p r g k", r=2, g=GG)
            BTT4 = BTT.ap.rearrange("p (r g k) -> p r g k", r=2, g=GG)
            CB4 = CB.ap.rearrange("p (r g k) -> p r g k", r=2, g=GG)
            u1 = A.f32(GG * 16, parts=64)
            u2 = A.f32(GG * 16, parts=64)
            u1v = u1.ap.rearrange("p (g m) -> p g m", g=GG)
            u2v = u2.ap.rearrange("p (g m) -> p g m", g=GG)
            smaS = A.bf(GG * 256)
            smcS = A.bf(GG * 256, parts=64)
            smaS3 = smaS.ap.rearrange("p (g k) -> p g k", g=GG)
            smcS3 = smcS.ap.rearrange("p (g k) -> p g k", g=GG)
            dcol = A.f32(GG)
            tmpT = A.f32(128)
            for r_, nm in ((0, "s5_b_re"), (1, "s5_b_im")):
                S.dma("sp", BRI4[:, r_], W[nm][l, g0:g0 + GG].rearrange("g p m -> p g m"), writes=[BRI])
            for r_, nm in ((0, "s5_c_re"), (1, "s5_c_im")):
                S.dma("sp", craw4[:, r_], W[nm][l, g0:g0 + GG], writes=[craw])
            for s_ in range(8):
                S.dma("sp", dcol.ap[s_ * 16:(s_ + 1) * 16, :],
                      W["s5_d"][l, g0 * 16:(g0 + GG) * 16].rearrange("(g m) -> m g", m=16), writes=[dcol],
                      allow_slow_non_contiguous=True)
            bks = [nbank(), nbank()]
            for r_ in range(2):
                for n_ in range(16):
                    trp(bks[r_].ap[0:64, n_ * GG:(n_ + 1) * GG], craw4[:, r_, n_, :], ident_f.ap[0:GG, 0:GG],
                        [craw, ident_f], [bks[r_]])
                cp("dve", CR4[:, r_].rearrange("p g n -> p n g"),
                   bks[r_].ap[0:64, 0:16 * GG].rearrange("p (n g) -> p n g", n=16), [bks[r_]], [CR])

            def cmul(dst5, Xri, Yr, Yi, neg_im, idx):
                yrb = Yr.unsqueeze(2).to_broadcast([64, GG, 16])
                yib = Yi.unsqueeze(2).to_broadcast([64, GG, 16])
                tt("dve", u1v, Xri[:, 0], yrb, ALU.mult, [BRI, CR, P, Q], [u1])
                tt("pool", u2v, Xri[:, 1], yib, ALU.mult, [BRI, CR, P, Q], [u2])
                tt("dve", dst5[:, 0, :, idx, :], u1v, u2v, ALU.subtract, [u1, u2], [BTW, BTT, CB])
                tt("dve", u1v, Xri[:, 0], yib, ALU.mult, [BRI, CR, P, Q], [u1])
                tt("pool", u2v, Xri[:, 1], yrb, ALU.mult, [BRI, CR, P, Q], [u2])
                if neg_im:
                    stt("dve", dst5[:, 1, :, idx, :], u1v, -1.0, u2v, ALU.mult, ALU.subtract, [u1, u2], [BTW, BTT, CB])
                else:
                    tt("dve", dst5[:, 1, :, idx, :], u1v, u2v, ALU.add, [u1, u2], [BTW, BTT, CB])

            if CUT == "c":
                return
            gs = slice(g0, g0 + GG)
            for s_ in range(8):
                cmul(BTW5, BRI4, Q4[:, 0, 15 - s_, gs], Q4[:, 1, 15 - s_, gs], False, s_)
                cmul(BTT5, BRI4, Q4[:, 0, 7 - s_, gs], Q4[:, 1, 7 - s_, gs], False, s_)
                cmul(CB5, CR4, P4[:, 0, s_ + 9, gs], P4[:, 1, s_ + 9, gs], True, s_)
            if CUT == "d":
                return
            for g in range(GG):
                bk = nbank()
                trp(bk.ap[:, 0:64], BTW4[:, 0, g, :], ident_f.ap[0:64, 0:64], [BTW, ident_f], [bk])
                trp(bk.ap[:, 64:128], BTW4[:, 1, g, :], ident_f.ap[0:64, 0:64], [BTW, ident_f], [bk])
                if CUT == "d1":
                    continue
                mm(bk.ap[:, 128:256], BTT4[:, 0, g, :], CB4[:, 0, g, :], True, False, [BTT, CB], [bk])
                mm(bk.ap[:, 128:256], BTT4[:, 1, g, :], CB4[:, 1, g, :], False, True, [BTT, CB], [bk])
                if CUT == "d2":
                    continue
                cp("act", smaS3[:, g, 0:128], bk.ap[:, 0:128], [bk], [smaS])
                if CUT == "d3":
                    continue
                cp("act", tmpT.ap, bk.ap[:, 128:256], [bk], [tmpT])
                tt("dve", tmpT.ap, tmpT.ap, bmask.ap, ALU.mult, [tmpT, bmask], [tmpT])
                if CUT == "d4":
                    continue
                stt("dve", smaS3[:, g, 128:256], ident_f.ap, dcol.ap[:, g:g + 1], tmpT.ap, ALU.mult, ALU.add,
                    [ident_f, dcol, tmpT], [smaS])
            if CUT in ("d1", "d2", "d3", "d4"):
                return
            if CUT == "e":
                return
            cp("act", smcS3[:, :, 0:128], CB4[:, 0], [CB], [smcS])
            cp("act", smcS3[:, :, 128:256], CB4[:, 1], [CB], [smcS])
            if CUT == "f":
                return
            S.dma("sp", SMA[:, g0:g0 + GG, :], smaS3, reads=[smaS])
            S.dma("sp", SMC[:, g0:g0 + GG, :], smcS3, reads=[smcS])

    def stage_S(l, sb):
        row0 = sb * TSB
        GQ = 32
        if sb == 0:
            S.fence()
            memset("dve", XST.ap, 0.0, [XST])
        XST3 = XST.ap.rearrange("p (r g) -> p r g", r=2)
        A83 = A8.ap.rearrange("p (r g) -> p r g", r=2)
        for gc in range(128 // GQ):
            g0 = gc * GQ
            S.fence()
            A.p = PERS
            sma = A.bf(GQ * 256)
            smc = A.bf(GQ * 256, parts=64)
            sma3 = sma.ap.rearrange("p (g k) -> p g k", g=GQ)
            smc3 = smc.ap.rearrange("p (g k) -> p g k", g=GQ)
            S.dma("sp", sma3, SMA[:, g0:g0 + GQ, :], writes=[sma])
            S.dma("sp", smc3, SMC[:, g0:g0 + GQ, :], writes=[smc])
            ug = A.bf(GQ * NCB)
            ug3 = ug.ap.rearrange("p (g c) -> p g c", g=GQ)
            mark_s = A.p
            ub = A.bf(8 * GQ * 16)
            ubs = A.bf(8 * GQ * 16, parts=16)
            ub4 = ub.ap.rearrange("c (s g m) -> c s g m", s=8, g=GQ)
            ubs4 = ubs.ap.rearrange("c (s g m) -> c s g m", s=8, g=GQ)
            cols = slice(g0 * 16, (g0 + GQ) * 16)
            S.dma("sp", ub.ap[0:NCP, :].rearrange("c (s f) -> c s f", s=8),
                  Us[row0:row0 + NPT * 128, cols].rearrange("(c s) f -> c s f", s=8), writes=[ub])
            S.dma("sp", ubs.ap.rearrange("c (s f) -> c s f", s=8),
                  Us[row0 + NPT * 128:row0 + TSB, cols].rearrange("(c s) f -> c s f", s=8), writes=[ubs])
            ub2 = A.bf(8 * GQ * 16)
            ubs2 = A.bf(8 * GQ * 16, parts=16)
            cp("dve", ub2.ap[0:NCP, :].rearrange("c (g s m) -> c g s m", g=GQ, s=8), ub4[0:NCP].rearrange("c s g m -> c g s m"), [ub], [ub2])
            cp("act", ubs2.ap.rearrange("c (g s m) -> c g s m", g=GQ, s=8), ubs4.rearrange("c s g m -> c g s m"), [ubs], [ubs2])
            ub2v = ub2.ap.rearrange("c (g k) -> c g k", g=GQ)
            ubs2v = ubs2.ap.rearrange("c (g k) -> c g k", g=GQ)
            for gq in range(0, GQ, 4):
                bank = nbank()
                pb = bank.ap.bitcast(BF16)
                for k in range(4):
                    g = gq + k
                    trp(pb[:, k * 256:k * 256 + NCP], ub2v[0:NCP, g, :], ident_b.ap[0:NCP, 0:NCP], [ub2, ident_b], [bank])
                    trp(pb[:, k * 256 + 128:k * 256 + 144], ubs2v[:, g, :], ident_b.ap[0:16, 0:16], [ubs2, ident_b], [bank])
                pbv = pb.rearrange("p (k c) -> p k c", k=4)
                cp(alt(), ug3[:, gq:gq + 4, 0:NCP], pbv[:, :, 0:NCP], [bank], [ug])
                cp(alt(), ug3[:, gq:gq + 4, NCP:NCB], pbv[:, :, 128:144], [bank], [ug])
            S.fence()
            A.p = mark_s
            W2 = A.f32(2 * GQ * NCB, parts=64)
            W4 = W2.ap.rearrange("p (r g c) -> p r g c", r=2, g=GQ)
            for g in range(GQ):
                bank = nbank()
                for r_ in range(2):
                    mm(bank.ap[0:64, r_ * 256:r_ * 256 + NCB], sma3[:, g, r_ * 64:(r_ + 1) * 64], ug3[:, g, :], True, True,
                       [sma, ug], [bank])
                cp(alt(), W4[:, :, g, :], bank.ap[0:64, :].rearrange("p (r c) -> p r c", r=2)[:, :, 0:NCB], [bank], [W2])
            XP = A.f32(2 * GQ * (NCP + 1), parts=64)
            XP4 = XP.ap.rearrange("p (r g c) -> p r g c", r=2, g=GQ)
            XS = A.f32(2 * GQ * 16, parts=64)
            XS4 = XS.ap.rearrange("p (r g j) -> p r g j", r=2, g=GQ)
            XN = A.f32(2 * GQ * 16, parts=64)
            XN4 = XN.ap.rearrange("p (r g j) -> p r g j", r=2, g=GQ)
            XB = A.bf(2 * GQ * NCB, parts=64)
            XB4 = XB.ap.rearrange("p (r g c) -> p r g c", r=2, g=GQ)
            AA = A.f32(2 * GQ, parts=64)
            II = A.f32(2 * GQ, parts=64)
            AA3 = AA.ap.rearrange("p (r g) -> p r g", r=2)
            II3 = II.ap.rearrange("p (r g) -> p r g", r=2)
            tA = A.f32(2 * GQ, parts=64)
            tU = A.f32(2 * GQ, parts=64)
            tA3 = tA.ap.rearrange("p (r g) -> p r g", r=2)
            tU3 = tU.ap.rearrange("p (r g) -> p r g", r=2)
            for r_ in range(2):
                cp("pool", AA3[:, r_, :], A83[:, 0, g0:g0 + GQ], [A8], [AA])
                cp("pool", II3[:, r_, :], A83[:, 1, g0:g0 + GQ], [A8], [II])
            cp("pool", XP4[:, :, :, 0], XST3[:, :, g0:g0 + GQ], [XST], [XP])
            for c in range(NCP):
                xc = XP4[:, :, :, c]
                tt("pool", tA3, AA3, xc, ALU.mult, [AA, XP], [tA])
                tt("pool", tU3, II3, xc, ALU.mult, [II, XP], [tU])
                tt("pool", tA3, tA3, W4[:, :, :, c], ALU.add, [tA, W2], [tA])
                tt("pool", XP4[:, 0, :, c + 1], tA3[:, 0, :], tU3[:, 1, :], ALU.subtract, [tA, tU], [XP])
                tt("pool", XP4[:, 1, :, c + 1], tA3[:, 1, :], tU3[:, 0, :], ALU.add, [tA, tU], [XP])
            cp("pool", XST3[:, :, g0:g0 + GQ], XP4[:, :, :, NCP], [XP], [XST])
            sin = A.f32(16 * 2 * 64, parts=GQ)
            sin4 = sin.ap.rearrange("g (j r p) -> g j r p", j=16, r=2)
            for r_ in range(2):
                S.dma("sp", sin4[:, :, r_, :], ss5[r_][l, sb * 16:(sb + 1) * 16, g0:g0 + GQ, :].rearrange("j g p -> g j p"),
                      writes=[sin])
            bks = [nbank(), nbank()]
            for j in range(16):
                for r_ in range(2):
                    idx = j * 2 + r_
                    trp(bks[idx // 16].ap[0:64, (idx % 16) * GQ:(idx % 16 + 1) * GQ], sin4[:, j, r_, :],
                        ident_f.ap[0:GQ, 0:GQ], [sin, ident_f], [bks[idx // 16]])
            for hb in range(2):
                cp("dve", XS4[:, :, :, hb * 8:(hb + 1) * 8].rearrange("p r g j -> p j r g"),
                   bks[hb].ap[0:64, 0:16 * GQ].rearrange("p (j r g) -> p j r g", j=8, r=2), [bks[hb]], [XS])
            t5 = A.f32(2 * GQ * 16, parts=64)
            u5 = A.f32(2 * GQ * 16, parts=64)
            t54 = t5.ap.rearrange("p (r g j) -> p r g j", r=2, g=GQ)
            u54 = u5.ap.rearrange("p (r g j) -> p r g j", r=2, g=GQ)
            AAb = AA3.unsqueeze(3).to_broadcast([64, 2, GQ, 16])
            IIb = II3.unsqueeze(3).to_broadcast([64, 2, GQ, 16])
            tt("dve", t54, XS4, AAb, ALU.mult, [XS, AA], [t5])
            tt("dve", u54, XS4, IIb, ALU.mult, [XS, II], [u5])
            tt("dve", t54, t54, W4[:, :, :, NCP:NCB], ALU.add, [t5, W2], [t5])
            tt("dve", XN4[:, 0], t54[:, 0], u54[:, 1], ALU.subtract, [t5, u5], [XN])
            tt("dve", XN4[:, 1], t54[:, 1], u54[:, 0], ALU.add, [t5, u5], [XN])
            sout = A.f32(16 * 2 * 64, parts=GQ)
            sout4 = sout.ap.rearrange("g (j r p) -> g j r p", j=16, r=2)
            for q4 in range(4):
                bank = nbank()
                for jj in range(4):
                    for r_ in range(2):
                        j = q4 * 4 + jj
                        o_ = (jj * 2 + r_) * 64
                        trp(bank.ap[0:GQ, o_:o_ + 64], XN4[:, r_, :, j], ident_f.ap[0:64, 0:64], [XN, ident_f], [bank])
                cp(alt(), sout.ap[:, q4 * 512:(q4 + 1) * 512], bank.ap[0:GQ, :], [bank], [sout])
            for r_ in range(2):
                S.dma("sp", s5s[r_][l, sb * 16:(sb + 1) * 16, g0:g0 + GQ, :].rearrange("j g p -> g j p"),
                      sout4[:, :, r_, :], reads=[sout])
            cp("act", XB4[:, :, :, 0:NCP], XP4[:, :, :, 0:NCP], [XP], [XB])
            cp("act", XB4[:, :, :, NCP:NCB], XS4, [XS], [XB])
            yb = A.bf(8 * GQ * 16)
            ybs = A.bf(8 * GQ * 16, parts=16)
            yb4 = yb.ap.rearrange("c (t g n) -> c t g n", t=8, g=GQ)
            ybs4 = ybs.ap.rearrange("c (t g n) -> c t g n", t=8, g=GQ)
            for (c0, m_, ydst, ytl) in ((0, NCP, yb4, yb), (NCP, 16, ybs4, ybs)):
                for gq in range(0, GQ, 4):
                    bank = nbank()
                    for k in range(4):
                        g = gq + k
                        o = bank.ap[0:m_, k * 128:(k + 1) * 128]
                        mm(o, ug3[:, g, c0:c0 + m_], sma3[:, g, 128:256], True, False, [ug, sma], [bank])
                        mm(o, XB4[:, 0, g, c0:c0 + m_], smc3[:, g, 0:128], False, False, [XB, smc], [bank])
                        mm(o, XB4[:, 1, g, c0:c0 + m_], smc3[:, g, 128:256], False, True, [XB, smc], [bank])
                    act(ydst[0:m_, :, gq:gq + 4, :].rearrange("c t g n -> c g t n"),
                        bank.ap[0:m_, :].rearrange("c (g t n) -> c g t n", g=4, t=8), AF.Gelu_apprx_tanh, [bank], [ytl])
            S.dma("sp", YG[row0:row0 + NPT * 128, cols].rearrange("(c t) f -> c t f", t=8),
                  yb.ap[0:NCP, :].rearrange("c (t f) -> c t f", t=8), reads=[yb])
            S.dma("sp", YG[row0 + NPT * 128:row0 + TSB, cols].rearrange("(c t) f -> c t f", t=8),
                  ybs.ap.rearrange("c (t f) -> c t f", t=8), reads=[ybs])
        if sb == NSB - 1:
            S.fence()
            A.p = PERS
            pso = A.f32(128)
            bank = nbank()
            for r_ in range(2):
                trp(bank.ap[:, r_ * 64:(r_ + 1) * 64], XST3[:, r_, :], ident_f.ap[0:64, 0:64], [XST, ident_f], [bank])
            cp("dve", pso.ap, bank.ap[:, 0:128], [bank], [pso])
            for r_ in range(2):
                S.dma("sp", s5p[r_][l], pso.ap[:, r_ * 64:(r_ + 1) * 64], reads=[pso])

    def gen_ut(l):
        S.fence()
        A.p = PERS
        ub_ = [A.bf(D) for _ in range(2)]
        ust = [A.bf(32 * 128) for _ in range(2)]
        vcb = [A.bf(D) for _ in range(2)]
        UTv = UT.rearrange("c d e -> d c e")
        for ec in range(128):
            b = ub_[ec % 2]
            s_ = ust[ec % 2]
            vc = vcb[ec % 2]
            S.dma("pool", vc.ap, W["peer_v"][l, ec * 128:(ec + 1) * 128, :], writes=[vc])
            S.dma("sp", V16[ec * 128:(ec + 1) * 128, :], vc.ap, reads=[vc])
            S.dma("pool", b.ap, W["peer_u"][l, ec * 128:(ec + 1) * 128, :], writes=[b])
            tr_bf(b, b.ap, 32, s_, s_.ap.rearrange("p (c e) -> p c e", c=32))
            S.dma("sp", UTv[:, :, ec * 128:(ec + 1) * 128], s_.ap.rearrange("p (c e) -> p c e", c=32), reads=[s_])

    def stage_B12(l, sb):
        S.fence()
        A.p = PERS
        row0 = sb * TSB
        os5T = A.bf(16 * TSB)
        os5T3 = os5T.ap.rearrange("p (c t) -> p c t", c=16)
        osT = [TL(os5T3[:, c, :]) for c in range(16)]
        mark = A.p
        ygT = A.bf(16 * TSB)
        ygT3 = ygT.ap.rearrange("p (c t) -> p c t", c=16)
        ygt = [TL(ygT3[:, :, i * 128:(i + 1) * 128]) for i in range(TPS)]
        load_T(YG, row0, TPS, 2048, ygT3, ygt, False, blend=PAIR)
        bglu = A.f32(16)
        S.dma("sp", bglu.ap, W["s5_b_glu"][l, :].rearrange("(c p) -> p c", p=128), writes=[bglu],
              allow_slow_non_contiguous=True)
        wg = [A.bf(16 * 128) for _ in range(2)]
        sg = [A.f32(512) for _ in range(2)]
        n_ = 0
        for cc in range(16):
            w = wg[cc % 2]
            w3 = w.ap.rearrange("p (c n) -> p c n", c=16)
            S.dma("pool", w3, W["s5_w_glu"][l, :, cc * 128:(cc + 1) * 128].rearrange("(c p) n -> p c n", p=128), writes=[w])
            for (t0, n) in tok_blocks(TSB):
                bank = nbank()
                rd = [ygt[i] for i in tiles_of(t0, n)]
                for c in range(16):
                    mm(bank.ap[:, 0:n], w3[:, c, :], ygT3[:, c, t0:t0 + n], c == 0, c == 15, [w] + rd, [bank])
                s_ = sg[n_ % 2]
                n_ += 1
                act(s_.ap[:, 0:n], bank.ap[:, 0:n], AF.Sigmoid, [bank, bglu], [s_], bias=bglu.ap[:, cc:cc + 1])
                tt("dve", os5T3[:, cc, t0:t0 + n], s_.ap[:, 0:n], ygT3[:, cc, t0:t0 + n], ALU.mult, [s_] + rd, [osT[cc]])
        S.fence()
        A.p = mark
        ogT = A.bf(16 * TSB)
        ogT3 = ogT.ap.rearrange("p (c t) -> p c t", c=16)
        ogt = [TL(ogT3[:, :, i * 128:(i + 1) * 128]) for i in range(TPS)]
        load_T(OG, row0, TPS, 2048, ogT3, ogt, False, blend=PAIR)
        wbr = [A.bf(32 * 512) for _ in range(2)]
        zt = [A.bf(1024) for _ in range(2)]
        m1 = A.f32(512)
        m2 = A.f32(512)
        mo = [A.bf(512) for _ in range(2)]
        n_ = 0
        for cb in range(8):
            w = wbr[cb % 2]
            w3 = w.ap.rearrange("p (c n) -> p c n", c=32)
            S.dma("pool", w3, W["w_branch"][l, :, cb * 512:(cb + 1) * 512].rearrange("(c p) n -> p c n", p=128), writes=[w])
            for i in range(TPS):
                rows = slice(row0 + i * 128, row0 + (i + 1) * 128)
                ba, bb_ = nbank(), nbank()
                for c in range(16):
                    mm(ba.ap, ogT3[:, c, i * 128:(i + 1) * 128], w3[:, c, :], c == 0, c == 15, [ogt[i], w], [ba])
                for c in range(16):
                    mm(bb_.ap, os5T3[:, c, i * 128:(i + 1) * 128], w3[:, 16 + c, :], c == 0, c == 15, [osT[c], w], [bb_])
                z_ = zt[n_ % 2]
                o_ = mo[n_ % 2]
                n_ += 1
                S.dma("sp", z_.ap[:, 0:512], Z[rows, cb * 512:(cb + 1) * 512], writes=[z_])
                S.dma("sp", z_.ap[:, 512:1024], Z[rows, 4096 + cb * 512:4096 + (cb + 1) * 512], writes=[z_])
                cp("act", m1.ap, ba.ap, [ba], [m1])
                cp("act", m2.ap, bb_.ap, [bb_], [m2])
                tt("dve", m1.ap, m1.ap, z_.ap[:, 0:512], ALU.mult, [m1, z_], [m1])
                tt("dve", m2.ap, m2.ap, z_.ap[:, 512:1024], ALU.mult, [m2, z_], [m2])
                tt("pool", o_.ap, m1.ap, m2.ap, ALU.add, [m1, m2], [o_])
                S.dma("sp", MM[rows, cb * 512:(cb + 1) * 512], o_.ap, reads=[o_])

    def stage_B3(l, sb):
        S.fence()
        A.p = PERS
        row0 = sb * TSB
        mT = A.bf(32 * TSB)
        mT3 = mT.ap.rearrange("p (c t) -> p c t", c=32)
        mt = [TL(mT3[:, :, i * 128:(i + 1) * 128]) for i in range(TPS)]
        mark = A.p
        load_T(MM, row0, TPS, D, mT3, mt, False)
        S.fence()
        A.p = mark
        wo = [A.bf(32 * 512) for _ in range(2)]
        hs = [A.f32(512) for _ in range(3)]
        n_ = 0
        for cb in range(8):
            w = wo[cb % 2]
            w3 = w.ap.rearrange("p (c n) -> p c n", c=32)
            S.dma("pool", w3, W["w_out"][l, :, cb * 512:(cb + 1) * 512].rearrange("(c p) n -> p c n", p=128), writes=[w])
            for i in range(TPS):
                rows = slice(row0 + i * 128, row0 + (i + 1) * 128)
                bank = nbank()
                for c in range(32):
                    mm(bank.ap, mT3[:, c, i * 128:(i + 1) * 128], w3[:, c, :], c == 0, c == 31, [mt[i], w], [bank])
                h_ = hs[n_ % 3]
                n_ += 1
                cp(alt(), h_.ap, bank.ap, [bank], [h_])
                S.dma("sp", FF[rows, cb * 512:(cb + 1) * 512], h_.ap, reads=[h_])

    def stage_LN(l, sb, resid, gname, bname, dst):
        S.fence()
        A.p = PERS
        row0 = sb * TSB
        g_ = A.f32(D)
        b_ = A.f32(D)
        S.dma("sp", g_.ap, W[gname][l, :].partition_broadcast(128), writes=[g_])
        S.dma("sp", b_.ap, W[bname][l, :].partition_broadcast(128), writes=[b_])
        ft = [A.f32(D) for _ in range(2)]
        xt = [A.f32(D) for _ in range(2)]
        ot = [A.f32(D) for _ in range(2)]
        stt_ = [A.f32(8) for _ in range(2)]
        for i in range(TPS):
            rows = slice(row0 + i * 128, row0 + (i + 1) * 128)
            f = ft[i % 2]
            x = xt[i % 2]
            o = ot[i % 2]
            s_ = stt_[i % 2]
            S.dma("sp", f.ap, FF[rows, :], writes=[f])
            S.dma("sp", x.ap, resid[rows, :], writes=[x])
            memset("dve", s_.ap, 0.0, [s_])
            stt("dve", f.ap, x.ap, ALPHA, f.ap, ALU.mult, ALU.add, [x, f], [f])
            act(o.ap, f.ap, AF.Identity, [f], [o, s_], accum=s_.ap[:, 0:1])
            ts("dve", s_.ap[:, 1:2], s_.ap[:, 0:1], -1.0 / D, None, ALU.mult, None, [s_], [s_])
            act(o.ap, f.ap, AF.Square, [f, s_], [o, s_], bias=s_.ap[:, 1:2], accum=s_.ap[:, 2:3])
            ts("dve", s_.ap[:, 3:4], s_.ap[:, 2:3], 1.0 / D, None, ALU.mult, None, [s_], [s_])
            act(s_.ap[:, 4:5], s_.ap[:, 3:4], AF.Ln, [s_, epscol], [s_], bias=epscol.ap[:, 0:1])
            act(s_.ap[:, 5:6], s_.ap[:, 4:5], AF.Exp, [s_], [s_], scale=-0.5)
            ts("dve", o.ap, f.ap, s_.ap[:, 1:2], s_.ap[:, 5:6], ALU.add, ALU.mult, [f, s_], [o])
            tt("pool", o.ap, o.ap, g_.ap, ALU.mult, [o, g_], [o])
            tt("pool", o.ap, o.ap, b_.ap, ALU.add, [o, b_], [o])
            S.dma("sp", dst[rows, :], o.ap, reads=[o])

    def stage_B4(l, sb):
        row0 = sb * TSB
        UTv = UT.rearrange("c d e -> d c e")
        GT = 3
        for gt0 in range(0, TPS, GT):
            ng = min(GT, TPS - gt0)
            ntok = ng * 128
            S.fence()
            A.p = PERS
            x2T = A.bf(32 * ntok)
            x2T3 = x2T.ap.rearrange("p (c t) -> p c t", c=32)
            x2t = [TL(x2T3[:, :, i * 128:(i + 1) * 128]) for i in range(ng)]
            s1p = [A.f32(1024) for _ in range(ng)]
            s2c = [A.f32(1024) for _ in range(ng)]
            thp = [A.f32(8) for _ in range(ng)]
            mark = A.p
            load_T(X2, row0 + gt0 * 128, ng, D, x2T3, x2t, True)
            kraw = A.f32(256)
            kbf = A.bf(256)
            kT = A.bf(256)
            S.dma("sp", kraw.ap[:, 0:128], W["peer_key1"][l], writes=[kraw])
            S.dma("sp", kraw.ap[:, 128:256], W["peer_key2"][l], writes=[kraw])
            cp("dve", kbf.ap, kraw.ap, [kraw], [kbf])
            tr_bf(kbf, kbf.ap, 2, kT, kT.ap.rearrange("p (k t) -> p k t", k=2))
            qT = A.bf(16 * ntok)
            qT3 = qT.ap.rearrange("p (c t) -> p c t", c=16)
            qTt = [TL(qT3[:, c, :]) for c in range(16)]
            wq = [A.bf(32 * 128) for _ in range(2)]
            for cc in range(16):
                w = wq[cc % 2]
                w3 = w.ap.rearrange("p (c n) -> p c n", c=32)
                S.dma("pool", w3, W["peer_w_q"][l, :, cc * 128:(cc + 1) * 128].rearrange("(c p) n -> p c n", p=128), writes=[w])
                bank = nbank()
                for c in range(32):
                    mm(bank.ap[:, 0:ntok], w3[:, c, :], x2T3[:, c, :], c == 0, c == 31, [w] + x2t, [bank])
                cp(alt(), qT3[:, cc, :], bank.ap[:, 0:ntok], [bank], [qTt[cc]])
            sc = A.f32(2048)
            sc3 = sc.ap.rearrange("p (a k) -> p a k", a=16)
            sc4 = sc.ap.rearrange("p (h f k) -> p h f k", h=8, f=2)
            m16 = A.f32(256)
            m163 = m16.ap.rearrange("p (a k) -> p a k", a=16)
            m164 = m16.ap.rearrange("p (h f k) -> p h f k", h=8, f=2)
            wk = [A.f32(128) for _ in range(2)]
            cand = [A.f32(256) for _ in range(2)]
            wk2 = [A.f32(256) for _ in range(2)]
            c16 = A.f32(128)
            c163 = c16.ap.rearrange("p (h k) -> p h k", h=8)
            ex = A.f32(128)
            ex3 = ex.ap.rearrange("p (h k) -> p h k", h=8)
            sm8 = A.f32(32)
            for ti in range(ng):
                bks = [nbank(), nbank(), nbank(), nbank()]
                for a in range(16):
                    mm(bks[a // 4].ap[:, (a % 4) * 128:(a % 4 + 1) * 128], qT3[:, a, ti * 128:(ti + 1) * 128],
                       kT.ap[:, (a % 2) * 128:(a % 2 + 1) * 128], True, True, [qTt[a], kT], [bks[a // 4]])
                for q_ in range(4):
                    cp(alt(), sc.ap[:, q_ * 512:(q_ + 1) * 512], bks[q_].ap, [bks[q_]], [sc])
                for a in range(16):
                    w_ = wk[a % 2]
                    S.op("dve", lambda e, a=a: e.max(out=m163[:, a, 0:8], in_=sc3[:, a, :]), [sc], [m16])
                    S.op("dve", lambda e, a=a, w_=w_: e.match_replace(out=w_.ap, in_to_replace=m163[:, a, 0:8],
                                                                     in_values=sc3[:, a, :], imm_value=-1e30), [sc, m16], [w_])
                    S.op("dve", lambda e, a=a, w_=w_: e.max(out=m163[:, a, 8:16], in_=w_.ap), [w_], [m16])
                for h in range(8):
                    cd = cand[h % 2]
                    w2 = wk2[h % 2]
                    tt("pool", cd.ap.rearrange("p (i j) -> p i j", i=16),
                       m164[:, h, 0, :].unsqueeze(2).to_broadcast([128, 16, 16]),
                       m164[:, h, 1, :].unsqueeze(1).to_broadcast([128, 16, 16]), ALU.add, [m16], [cd])
                    S.op("dve", lambda e, h=h, cd=cd: e.max(out=c163[:, h, 0:8], in_=cd.ap), [cd], [c16])
                    S.op("dve", lambda e, h=h, cd=cd, w2=w2: e.match_replace(out=w2.ap, in_to_replace=c163[:, h, 0:8],
                                                                            in_values=cd.ap, imm_value=-1e30), [cd, c16], [w2])
                    S.op("dve", lambda e, h=h, w2=w2: e.max(out=c163[:, h, 8:16], in_=w2.ap), [w2], [c16])
                tt("dve", ex3, c163, c163[:, :, 0:1].to_broadcast([128, 8, 16]), ALU.subtract, [c16], [ex])
                act(ex.ap, ex.ap, AF.Exp, [ex], [ex])
                S.op("dve", lambda e: e.tensor_reduce(out=sm8.ap[:, 0:8], in_=ex3, axis=AX.X, op=ALU.add), [ex], [sm8])
                act(sm8.ap[:, 8:16], sm8.ap[:, 0:8], AF.Ln, [sm8], [sm8])
                tt("dve", sm8.ap[:, 16:24], sm8.ap[:, 8:16], c163[:, :, 0], ALU.add, [sm8, c16], [sm8])
                tt("dve", thp[ti].ap, c163[:, :, 15], sm8.ap[:, 16:24], ALU.subtract, [c16, sm8], [thp[ti]])
                tt("dve", s1p[ti].ap.rearrange("p (h k) -> p h k", h=8), sc4[:, :, 0, :],
                   sm8.ap[:, 16:24].unsqueeze(2).to_broadcast([128, 8, 128]), ALU.subtract, [sc, sm8], [s1p[ti]])
                cp("pool", s2c[ti].ap.rearrange("p (h k) -> p h k", h=8), sc4[:, :, 1, :], [sc], [s2c[ti]])
            S.fence()
            A.p = mark
            utb = A.bf(32 * 512)
            utb3 = utb.ap.rearrange("p (c e) -> p c e", c=32)
            vb = A.bf(4 * D)
            vb3 = vb.ap.rearrange("p (c d) -> p c d", c=4)
            acc = [A.f32(D) for _ in range(ng)]
            Ap = A.f32(2048)
            Eb = A.bf(2048)
            Ap4 = Ap.ap.rearrange("p (h a k) -> p h a k", h=4, a=4)
            Eb4 = Eb.ap.rearrange("p (h a k) -> p h a k", h=4, a=4)
            wt = A.f32(512)
            wt2 = A.f32(512)
            gl = A.f32(512)
            wgb = A.bf(512)
            wgT = [A.bf(512) for _ in range(ng)]
            vtmp = A.f32(2048)
            S.dma("sp", utb3, UTv[:, :, 0:512], writes=[utb])
            for eb_ in range(32):
                S.dma("sp", vb3, V16[eb_ * 512:(eb_ + 1) * 512, :].rearrange("(c p) d -> p c d", p=128), writes=[vb])
                for ti in range(ng):
                    s1v = s1p[ti].ap.rearrange("p (h k) -> p h k", h=8)
                    s2v = s2c[ti].ap.rearrange("p (h k) -> p h k", h=8)
                    for hh in range(2):
                        hs_ = slice(hh * 4, hh * 4 + 4)
                        tt("pool", Ap4, s1v[:, hs_, eb_ * 4:eb_ * 4 + 4].unsqueeze(3).to_broadcast([128, 4, 4, 128]),
                           s2v[:, hs_, :].unsqueeze(2).to_broadcast([128, 4, 4, 128]), ALU.add, [s1p[ti], s2c[ti]], [Ap])
                        act(Eb.ap, Ap.ap, AF.Exp, [Ap], [Eb])
                        tt("dve", Ap4, Ap4, thp[ti].ap[:, hs_].unsqueeze(2).unsqueeze(3).to_broadcast([128, 4, 4, 128]),
                           ALU.is_ge, [Ap, thp[ti]], [Ap])
                        tt("dve", Ap.ap, Ap.ap, Eb.ap, ALU.mult, [Ap, Eb], [Ap])
                        dst_ = wt if hh == 0 else wt2
                        S.op("dve", lambda e, dst_=dst_: e.tensor_reduce(out=dst_.ap, in_=Ap.ap.rearrange("p (h k) -> p k h", h=4),
                                                                        axis=AX.X, op=ALU.add), [Ap], [dst_])
                    tt("pool", wt.ap, wt.ap, wt2.ap, ALU.add, [wt, wt2], [wt])
                    hb = nbank(0, 2)
                    for c in range(32):
                        mm(hb.ap, x2T3[:, c, ti * 128:(ti + 1) * 128], utb3[:, c, :], c == 0, c == 31, [x2t[ti], utb], [hb])
                    act(gl.ap, hb.ap, AF.Gelu_apprx_tanh, [hb], [gl])
                    tt("dve", wgb.ap, gl.ap, wt.ap, ALU.mult, [gl, wt], [wgb])
                    tr_bf(wgb, wgb.ap, 4, wgT[ti], wgT[ti].ap.rearrange("p (c t) -> p c t", c=4))
                if eb_ < 31:
                    S.dma("sp", utb3, UTv[:, :, (eb_ + 1) * 512:(eb_ + 2) * 512], writes=[utb])
                for half in range(2):
                    for ti in range(ng):
                        wT3 = wgT[ti].ap.rearrange("p (c t) -> p c t", c=4)
                        for db in range(4):
                            for ch in range(4):
                                mm(PB[4 + db].ap, wT3[:, ch, :], vb3[:, ch, half * 2048 + db * 512:half * 2048 + (db + 1) * 512],
                                   ch == 0, ch == 3, [wgT[ti], vb], [PB[4 + db]])
                        a_ = acc[ti].ap[:, half * 2048:(half + 1) * 2048]
                        if eb_ == 0:
                            cp("act", a_, ps_t[:, 2048:4096], PB[4:8], [acc[ti]])
                        else:
                            cp("act", vtmp.ap, ps_t[:, 2048:4096], PB[4:8], [vtmp])
                            tt("dve", a_, a_, vtmp.ap, ALU.add, [vtmp, acc[ti]], [acc[ti]])
            for ti in range(ng):
                rows = slice(row0 + (gt0 + ti) * 128, row0 + (gt0 + ti + 1) * 128)
                S.dma("sp", FF[rows, :], acc[ti].ap, reads=[acc[ti]])

    nst = [0]

    def run_stage(f, *a):
        if stop_after is not None and nst[0] >= stop_after:
            return
        nst[0] += 1
        f(*a)

    def exchange(l):
        S.fence()
        RG = [[2 * i, 2 * i + 1] for i in range(NCORE // 2)]
        for src, dst in ((QKw, QK), (Vsw, Vs), (RSw, RS), (ALRw, ALR), (Usw, Us)):
            S.op("pool", lambda e, src=src, dst=dst: e.collective_compute(
                "AllGather", ALU.bypass, replica_groups=RG, ins=[src[:, :]], outs=[dst[:, :]]), dma=True)

    for l in range(NLAYER):
        xsrc = x_in if l == 0 else X1
        dst = X1 if l < NLAYER - 1 else y_out
        run_stage(gen_s5, l)
        run_stage(gen_ut, l)
        if PAIR:
            run_stage(stage_XA, l, 0, xsrc)
            run_stage(exchange, l)
            for sb in range(NSB):
                run_stage(stage_G, l, sb)
            for sb in range(NSB):
                run_stage(stage_S, l, sb)
            run_stage(stage_B12, l, 0)
            run_stage(stage_B3, l, 0)
            run_stage(stage_LN, l, 0, xsrc, "ln1_g", "ln1_b", X2)
            run_stage(stage_B4, l, 0)
            run_stage(stage_LN, l, 0, X2, "ln2_g", "ln2_b", dst)
            continue
        for sb in range(NSB):
            run_stage(stage_XA, l, sb, xsrc)
            run_stage(stage_G, l, sb)
            run_stage(stage_S, l, sb)
            run_stage(stage_B12, l, sb)
            run_stage(stage_B3, l, sb)
            run_stage(stage_LN, l, sb, xsrc, "ln1_g", "ln1_b", X2)
            run_stage(stage_B4, l, sb)
            run_stage(stage_LN, l, sb, X2, "ln2_g", "ln2_b", dst)
    S.emit()
    st.close()
    return nc


_NC_CACHE = {}


def run_cfg(inp, NSB, NPT, NCORE, NLAYER=2, stop_after=None, debug=False):
    key = (NSB, NPT, NLAYER, stop_after, debug)
    if key not in _NC_CACHE:
        _NC_CACHE[key] = build(NSB, NPT, NLAYER, stop_after, debug)
    nc = _NC_CACHE[key]
    TPS = NPT + 1
    NS = NSB * 16
    PL = NPT * 128
    xp = np.asarray(inp["x_prompt"], np.float32)
    xs = np.asarray(inp["x_sample"], np.float32)
    in_maps = []
    for i in range(NCORE):
        parts = []
        for sb in range(NSB):
            parts.append(xp[i, sb * PL:(sb + 1) * PL])
            parts.append(xs[i * NS + sb * 16:i * NS + (sb + 1) * 16].reshape(128, D))
        m = {"x": np.ascontiguousarray(np.concatenate(parts, 0)),
             "sgla": np.ascontiguousarray(inp["state_gla"][:, i * NS:(i + 1) * NS]),
             "ss5re": np.ascontiguousarray(inp["state_s5_re"][:, i * NS:(i + 1) * NS]),
             "ss5im": np.ascontiguousarray(inp["state_s5_im"][:, i * NS:(i + 1) * NS])}
        for n, _ in WSHAPES:
            if ONLY is None or n in ONLY:
                m[n] = np.asarray(inp[n], np.float32)
        in_maps.append(m)
    res = run_bass_kernel_spmd(nc, in_maps, core_ids=list(range(NCORE)))
    R = res.results
    if debug:
        return R
    yp = np.zeros((NCORE, NSB * PL, D), np.float32)
    ys = np.zeros((NCORE * NS, 8, D), np.float32)
    for i in range(NCORE):
        y = np.asarray(R[i]["y"]).reshape(NSB, TPS * 128, D)
        for sb in range(NSB):
            yp[i, sb * PL:(sb + 1) * PL] = y[sb, :PL]
            ys[i * NS + sb * 16:i * NS + (sb + 1) * 16] = y[sb, PL:].reshape(16, 8, D)
    st1 = lambda k: np.stack([np.asarray(R[i][k]) for i in range(NCORE)], 1)
    cat1 = lambda k: np.concatenate([np.asarray(R[i][k]) for i in range(NCORE)], 1)
    return (yp, ys, st1("glap"), st1("s5rep"), st1("s5imp"), cat1("glas"), cat1("s5res"), cat1("s5ims"))


def kernel(**inputs):
    return run_cfg(inputs, 2, 8, 4)


def run_pair(inp, NPT, NPAIR, NLAYER=2):
    NCORE = 2 * NPAIR
    key = ("pair", NPT, NLAYER, NCORE)
    if key not in _NC_CACHE:
        _NC_CACHE[key] = build(2, NPT, NLAYER, None, False, True, NCORE)
    nc = _NC_CACHE[key]
    PL = NPT * 128
    xp = np.asarray(inp["x_prompt"], np.float32)
    xs = np.asarray(inp["x_sample"], np.float32)
    in_maps = []
    for c in range(NCORE):
        i, p = c // 2, c % 2
        x = np.concatenate([xp[i, p * PL:(p + 1) * PL], xs[i * 32 + p * 16:i * 32 + (p + 1) * 16].reshape(128, D)], 0)
        pm = np.zeros((128, 2), np.float32)
        pm[:, p] = 1.0
        m = {"x": np.ascontiguousarray(x), "pm": pm,
             "sgla": np.ascontiguousarray(inp["state_gla"][:, i * 32:(i + 1) * 32]),
             "ss5re": np.ascontiguousarray(inp["state_s5_re"][:, i * 32:(i + 1) * 32]),
             "ss5im": np.ascontiguousarray(inp["state_s5_im"][:, i * 32:(i + 1) * 32])}
        for n, _ in WSHAPES:
            m[n] = np.asarray(inp[n], np.float32)
        in_maps.append(m)
    res = run_bass_kernel_spmd(nc, in_maps, core_ids=list(range(NCORE)))
    R = res.results
    yp = np.zeros((NPAIR, 2 * PL, D), np.float32)
    ys = np.zeros((NPAIR * 32, 8, D), np.float32)
    for c in range(NCORE):
        i, p = c // 2, c % 2
        y = np.asarray(R[c]["y"])
        yp[i, p * PL:(p + 1) * PL] = y[:PL]
        ys[i * 32 + p * 16:i * 32 + (p + 1) * 16] = y[PL:].reshape(16, 8, D)
    st1 = lambda k: np.stack([np.asarray(R[2 * i][k]) for i in range(NPAIR)], 1)
    cat1 = lambda k: np.concatenate([np.asarray(R[2 * i][k]) for i in range(NPAIR)], 1)
    return (yp, ys, st1("glap"), st1("s5rep"), st1("s5imp"), cat1("glas"), cat1("s5res"), cat1("s5ims"))
```

```python
import math
from contextlib import ExitStack
import numpy as np
import concourse.bass as bass
import concourse.mybir as mybir
from concourse.bass_utils import run_bass_kernel_spmd

F32 = mybir.dt.float32
BF16 = mybir.dt.bfloat16
I32 = mybir.dt.int32
ALU = mybir.AluOpType
AF = mybir.ActivationFunctionType
AX = mybir.AxisListType

ENGS = ("pe", "act", "dve", "pool", "sp")
NDSEM = 8
D = 4096
ALPHA = 4.0 ** 0.25
TWO_PI = 2.0 * math.pi


class Buf:
    __slots__ = ("w", "r")

    def __init__(self):
        self.w = None
        self.r = []


class TL:
    __slots__ = ("ap", "b")

    def __init__(self, ap):
        self.ap = ap
        self.b = Buf()


class Op:
    __slots__ = ("eng", "fn", "deps", "dma", "needed", "sig", "dsem", "dval", "idx")

    def __init__(self, eng, fn, dma):
        self.eng = eng
        self.fn = fn
        self.dma = dma
        self.deps = []
        self.needed = False
        self.sig = 0
        self.dsem = None
        self.dval = 0
        self.idx = 0


class Sched:
    def __init__(self, nc):
        self.nc = nc
        self.ops = {e: [] for e in ENGS}
        self.ndma = {e: 0 for e in ENGS}
        self.lastc = {}
        self.dmas = []

    def op(self, eng, fn, reads=(), writes=(), dma=False):
        o = Op(eng, fn, dma)
        deps = []
        for t in reads:
            if t.b.w is not None:
                deps.append(t.b.w)
        for t in writes:
            if t.b.w is not None:
                deps.append(t.b.w)
            deps.extend(t.b.r)
        seen = set()
        for d in deps:
            if id(d) in seen:
                continue
            seen.add(id(d))
            if eng == "pe" and d.eng == "pe" and not d.dma and not dma:
                continue
            o.deps.append(d)
        for t in reads:
            t.b.r.append(o)
        for t in writes:
            t.b.w = o
            t.b.r = []
        if dma:
            o.idx = self.ndma[eng]
            self.ndma[eng] += 1
            self.dmas.append(o)
        else:
            self.lastc[eng] = o
        self.ops[eng].append(o)
        return o

    def dma(self, eng, out, in_, reads=(), writes=(), **kw):
        return self.op(eng, lambda e: e.dma_start(out=out, in_=in_, **kw), reads, writes, dma=True)

    def fence(self):
        deps = list(self.lastc.values()) + self.dmas
        self.dmas = []
        for e in ENGS:
            o = Op(e, None, False)
            o.deps = list(deps)
            self.ops[e].append(o)

    def emit(self):
        nc = self.nc
        self.fence()
        for e in ENGS:
            for o in self.ops[e]:
                for d in o.deps:
                    d.needed = True
        for e in ENGS:
            c = 0
            for o in self.ops[e]:
                if o.fn is not None and not o.dma and o.needed:
                    c += 1
                    o.sig = c
        with ExitStack() as st:
            csem = {e: st.enter_context(nc.semaphore("c_" + e)) for e in ENGS if e != "sp"}
            dsem = {e: [st.enter_context(nc.semaphore("d_%s%d" % (e, i))) for i in range(NDSEM)]
                    for e in ("sp", "act", "pool")}
            for e in ENGS:
                for o in self.ops[e]:
                    if o.dma:
                        o.dsem = dsem[e][o.idx % NDSEM]
                        o.dval = 16 * (o.idx // NDSEM + 1)
            block = st.enter_context(nc.Block())
            engobj = {"pe": block.tensor, "act": block.scalar, "dve": block.vector,
                      "pool": block.gpsimd, "sp": block.sync}

            def mk(e):
                def body(eng):
                    waited = {}

                    def wait(sem, val):
                        if waited.get(id(sem), 0) >= val:
                            return
                        waited[id(sem)] = val
                        eng.wait_ge(sem, val)

                    for o in self.ops[e]:
                        for d in o.deps:
                            if d.dma:
                                wait(d.dsem, d.dval)
                            else:
                                wait(csem[d.eng], d.sig)
                        if o.fn is None:
                            continue
                        if o.dma and o.dval > 16:
                            wait(o.dsem, o.dval - 16)
                        ins = o.fn(eng)
                        if o.dma:
                            ins.then_inc(o.dsem, 16)
                        elif o.needed:
                            ins.then_inc(csem[e], 1)
                return body

            for e in ENGS:
                engobj[e](mk(e))


class Arena:
    def __init__(self, ap):
        self.ap = ap
        self.p = 0
        self.n = ap.shape[1]

    def f32(self, n, parts=128):
        a = self.ap[0:parts, self.p:self.p + n]
        self.p += n
        assert self.p <= self.n, "SBUF arena overflow %d > %d" % (self.p, self.n)
        return TL(a)

    def bf(self, n, parts=128):
        m = (n + 1) // 2
        a = self.ap[0:parts, self.p:self.p + m].bitcast(BF16)[:, 0:n]
        self.p += m
        assert self.p <= self.n, "SBUF arena overflow %d > %d" % (self.p, self.n)
        return TL(a)

    def i32(self, n, parts=128):
        a = self.ap[0:parts, self.p:self.p + n].bitcast(I32)
        self.p += n
        assert self.p <= self.n
        return TL(a)


ONLY = None
CUT = None
WSHAPES = [
    ("ln1_g", (2, D)), ("ln1_b", (2, D)), ("w_in", (2, D, 16400)), ("gla_w_a2", (2, 16, 1024)),
    ("gla_b_a", (2, 1024)), ("gla_norm_g", (2, 512)), ("s5_a_re", (2, 128, 64)), ("s5_a_im", (2, 128, 64)),
    ("s5_log_dt", (2, 128)), ("s5_b_re", (2, 128, 64, 16)), ("s5_b_im", (2, 128, 64, 16)),
    ("s5_c_re", (2, 128, 16, 64)), ("s5_c_im", (2, 128, 16, 64)), ("s5_d", (2, 2048)),
    ("s5_w_glu", (2, 2048, 2048)), ("s5_b_glu", (2, 2048)), ("w_branch", (2, D, D)), ("w_out", (2, D, D)),
    ("ln2_g", (2, D)), ("ln2_b", (2, D)), ("peer_w_q", (2, D, 2048)), ("peer_key1", (2, 128, 128)),
    ("peer_key2", (2, 128, 128)), ("peer_u", (2, 16384, D)), ("peer_v", (2, 16384, D)),
]


def build(NSB, NPT, NLAYER=2, stop_after=None, debug=False, PAIR=False, NCORE=8):
    TPS = NPT + 1
    TSB = TPS * 128
    T = NSB * TSB
    TLc = TSB if PAIR else T
    NS = NSB * 16
    NCP = NPT * 16
    NCB = NCP + 16
    nc = bass.Bass("TRN2", target_bir_lowering=False)
    di = lambda n, s, dt=F32: nc.dram_tensor(n, list(s), dt, kind="ExternalInput").ap()
    do = lambda n, s, dt=F32: nc.dram_tensor(n, list(s), dt, kind="ExternalOutput").ap()
    ds = lambda n, s, dt=F32: nc.dram_tensor(n, list(s), dt, kind=("ExternalOutput" if debug else "Internal")).ap()
    x_in = di("x", [TLc, D])
    pm_in = di("pm", [128, 2]) if PAIR else None
    sgla = di("sgla", [2, NS, 4, 256, 512])
    ss5 = [di("ss5re", [2, NS, 128, 64]), di("ss5im", [2, NS, 128, 64])]
    W = {n: di(n, s) for n, s in WSHAPES if ONLY is None or n in ONLY}
    y_out = do("y", [TLc, D])
    glap = do("glap", [2, 4, 256, 512])
    s5p = [do("s5rep", [2, 128, 64]), do("s5imp", [2, 128, 64])]
    glas = do("glas", [2, NS, 4, 256, 512])
    s5s = [do("s5res", [2, NS, 128, 64]), do("s5ims", [2, NS, 128, 64])]
    X1 = ds("X1", [TLc, D])
    X2 = ds("X2", [TLc, D])
    FF = ds("FF", [TLc, D])
    QKw = ds("QKw", [TLc, 2048])
    Vsw = ds("Vsw", [TLc, 2048], BF16)
    RSw = ds("RSw", [TLc, 2048], BF16)
    ALRw = ds("ALRw", [TLc, 16])
    Usw = ds("Usw", [TLc, 2048], BF16)
    if PAIR:
        QK = ds("QKg", [T, 2048])
        Vs = ds("Vsg", [T, 2048], BF16)
        RS = ds("RSg", [T, 2048], BF16)
        ALR = ds("ALRg", [T, 16])
        Us = ds("Usg", [T, 2048], BF16)
    else:
        QK, Vs, RS, ALR, Us = QKw, Vsw, RSw, ALRw, Usw
    Z = ds("Z", [TLc, 8192], BF16)
    OG = ds("OG", [T, 2048], BF16)
    YG = ds("YG", [T, 2048], BF16)
    MM = ds("MM", [TLc, D], BF16)
    UT = ds("UT", [32, 128, 16384], BF16)
    V16 = ds("V16", [16384, D], BF16)
    SMA = ds("SMA", [128, 128, 256], BF16)
    SMC = ds("SMC", [64, 128, 256], BF16)
    GST = ds("GST", [128, 8, 512])

    S = Sched(nc)
    st = ExitStack()
    NA = 52 * 1024 - 256
    arena_t = st.enter_context(nc.sbuf_tensor("arena", [128, NA], F32))
    ps_t = st.enter_context(nc.psum_tensor("ps", [128, 4096], F32))
    A = Arena(arena_t[:, :])
    PB = [TL(ps_t[:, i * 512:(i + 1) * 512]) for i in range(8)]
    bank_ctr = [0]

    def nbank(lo=0, hi=8):
        i = lo + bank_ctr[0] % (hi - lo)
        bank_ctr[0] += 1
        return PB[i]

    rr = [0]

    def alt(engs=("dve", "act")):
        rr[0] += 1
        return engs[rr[0] % len(engs)]

    def cp(eng, out, in_, reads, writes):
        if eng == "act":
            return S.op("act", lambda e: e.copy(out=out, in_=in_), reads, writes)
        return S.op(eng, lambda e: e.tensor_copy(out=out, in_=in_), reads, writes)

    def tt(eng, out, in0, in1, op, reads, writes):
        return S.op(eng, lambda e: e.tensor_tensor(out=out, in0=in0, in1=in1, op=op), reads, writes)

    def ts(eng, out, in0, s1, s2, op0, op1, reads, writes):
        if s2 is None:
            return S.op(eng, lambda e: e.tensor_scalar(out=out, in0=in0, scalar1=s1, scalar2=None, op0=op0), reads, writes)
        return S.op(eng, lambda e: e.tensor_scalar(out=out, in0=in0, scalar1=s1, scalar2=s2, op0=op0, op1=op1), reads, writes)

    def stt(eng, out, in0, scalar, in1, op0, op1, reads, writes):
        return S.op(eng, lambda e: e.scalar_tensor_tensor(out=out, in0=in0, scalar=scalar, in1=in1, op0=op0, op1=op1), reads, writes)

    def act(out, in_, func, reads, writes, bias=None, scale=None, accum=None):
        kw = {}
        if bias is not None:
            kw["bias"] = bias
        if scale is not None:
            kw["scale"] = scale
        if accum is not None:
            kw["accum_out"] = accum
        return S.op("act", lambda e: e.activation(out=out, in_=in_, func=func, **kw), reads, writes)

    def mm(out, lhsT, rhs, start, stop, reads, writes):
        return S.op("pe", lambda e: e.matmul(out, lhsT=lhsT, rhs=rhs, start=start, stop=stop), reads, writes)

    def trp(out, in_, ident, reads, writes):
        return S.op("pe", lambda e: e.transpose(out=out, in_=in_, identity=ident), reads, writes)

    def memset(eng, ap, val, writes):
        return S.op(eng, lambda e: e.memset(ap, val), (), writes)

    def asel(ap, pattern, op, base, cm, tl):
        return S.op("pool", lambda e: e.affine_select(out=ap, in_=ap, pattern=pattern, compare_op=op, fill=0.0,
                                                      base=base, channel_multiplier=cm), [tl], [tl])

    ident_b = A.bf(128)
    ident_f = A.f32(128)
    triL = A.f32(128)
    triR = A.f32(128)
    triLs = A.f32(128)
    triRs = A.f32(128)
    same = A.f32(128)
    bmask = A.f32(128)
    seqsel = A.f32(16)
    onescol = A.f32(1)
    epscol = A.f32(2)
    colmask = A.bf(16 * 128)
    XST = A.f32(256, parts=64)
    A8 = A.f32(256, parts=64)
    PERS = A.p
    colmask_f = A.f32(16 * 128)
    for t_, v_ in ((ident_f, 1.0), (triL, 1.0), (triR, 1.0), (same, 1.0), (bmask, 1.0), (seqsel, 1.0),
                   (onescol, 1.0), (colmask_f, 1.0)):
        memset("pool", t_.ap, v_, [t_])
    memset("pool", epscol.ap[:, 0:1], 1e-5, [epscol])
    memset("pool", epscol.ap[:, 1:2], 1e-6, [epscol])
    asel(ident_f.ap, [[-1, 128]], ALU.is_equal, 0, 1, ident_f)
    asel(triL.ap, [[1, 128]], ALU.is_ge, 0, -1, triL)
    asel(triR.ap, [[-1, 128]], ALU.is_ge, -1, 1, triR)
    sv = same.ap.rearrange("p (j i) -> p j i", j=16)
    asel(sv, [[-8, 16], [0, 8]], ALU.is_ge, 0, 1, same)
    asel(sv, [[8, 16], [0, 8]], ALU.is_ge, 7, -1, same)
    asel(bmask.ap.rearrange("p (t n) -> p t n", t=8), [[16, 8], [0, 16]], ALU.is_ge, 15, -1, bmask)
    asel(seqsel.ap, [[-8, 16]], ALU.is_ge, 0, 1, seqsel)
    asel(seqsel.ap, [[8, 16]], ALU.is_ge, 7, -1, seqsel)
    cmv = colmask_f.ap.rearrange("p (j t) -> p j t", j=16)
    asel(cmv, [[-8, 16], [1, 128]], ALU.is_ge, 0, 0, colmask_f)
    asel(cmv, [[8, 16], [-1, 128]], ALU.is_ge, 7, 0, colmask_f)
    cp("dve", ident_b.ap, ident_f.ap, [ident_f], [ident_b])
    cp("dve", colmask.ap, colmask_f.ap, [colmask_f], [colmask])
    tt("dve", triLs.ap, triL.ap, same.ap, ALU.mult, [triL, same], [triLs])
    tt("dve", triRs.ap, triR.ap, same.ap, ALU.mult, [triR, same], [triRs])

    def tr_bf(src_tl, src_ap, nch, dst_tl, dst_ap3, rows=128, fixed_bank=None):
        for k0 in range(0, nch, 8):
            nb = min(8, nch - k0)
            bank = fixed_bank if fixed_bank is not None else nbank()
            pb = bank.ap.bitcast(BF16)
            for k in range(nb):
                trp(pb[:, k * 128:k * 128 + rows], src_ap[:, (k0 + k) * 128:(k0 + k + 1) * 128],
                    ident_b.ap[0:rows, 0:rows], [src_tl, ident_b], [bank])
            cp(alt(), dst_ap3[:, k0:k0 + nb, :],
               pb[:, 0:nb * 128].rearrange("p (k t) -> p k t", k=nb)[:, :, 0:rows], [bank], [dst_tl])

    def load_T(src_dram, row0, ntiles, ncols, dstT_ap3, dst_tls, fcast, blend=False):
        bufs = [A.bf(ncols) for _ in range(2)]
        if blend:
            bufs2 = [A.bf(ncols) for _ in range(2)]
            pmt = A.f32(2)
            S.dma("sp", pmt.ap, pm_in[:, :], writes=[pmt])
            for i in range(ntiles):
                b = bufs[i % 2]
                b2 = bufs2[i % 2]
                S.dma("sp", b.ap, src_dram[i * 128:(i + 1) * 128, :], writes=[b])
                S.dma("sp", b2.ap, src_dram[TSB + i * 128:TSB + (i + 1) * 128, :], writes=[b2])
                ts("dve", b.ap, b.ap, pmt.ap[:, 0:1], None, ALU.mult, None, [b, pmt], [b])
                stt("dve", b.ap, b2.ap, pmt.ap[:, 1:2], b.ap, ALU.mult, ALU.add, [b, b2, pmt], [b])
                tr_bf(b, b.ap, ncols // 128, dst_tls[i], dstT_ap3[:, :, i * 128:(i + 1) * 128])
            return
        fb = [A.f32(ncols) for _ in range(2)] if fcast else None
        for i in range(ntiles):
            b = bufs[i % 2]
            rows = slice(row0 + i * 128, row0 + (i + 1) * 128)
            if fcast:
                f = fb[i % 2]
                S.dma("sp", f.ap, src_dram[rows, :], writes=[f])
                cp(alt(), b.ap, f.ap, [f], [b])
            else:
                S.dma("sp", b.ap, src_dram[rows, :], writes=[b])
            tr_bf(b, b.ap, ncols // 128, dst_tls[i], dstT_ap3[:, :, i * 128:(i + 1) * 128])

    def tok_blocks(n):
        return [(t0, min(512, n - t0)) for t0 in range(0, n, 512)]

    def tiles_of(t0, n):
        return list(range(t0 // 128, (t0 + n + 127) // 128))

    def stage_XA(l, sb, xsrc):
        S.fence()
        A.p = PERS
        row0 = sb * TSB
        xT_ap = A.bf(32 * TSB).ap.rearrange("p (c t) -> p c t", c=32)
        xT = [TL(xT_ap[:, :, i * 128:(i + 1) * 128]) for i in range(TPS)]
        mark = A.p
        load_T(xsrc, row0, TPS, D, xT_ap, xT, True)
        S.fence()
        A.p = mark
        wb = [A.bf(32 * 512) for _ in range(2)]
        stg = [A.f32(512) for _ in range(4)]
        segs = [(0, 2048, "qk"), (2048, 2048, "v"), (4096, 2048, "r"), (6144, 16, "alr"), (6160, 2048, "u"),
                (8208, 4096, "zg"), (12304, 4096, "zs")]
        blk = 0
        for (c0, ncs, kind) in segs:
            for n0 in range(0, ncs, 512):
                nb = min(512, ncs - n0)
                w = wb[blk % 2]
                w3 = w.ap.rearrange("p (c n) -> p c n", c=32)
                S.dma("pool", w3[:, :, 0:nb],
                      W["w_in"][l, :, c0 + n0:c0 + n0 + nb].rearrange("(c p) n -> p c n", p=128), writes=[w])
                for i in range(TPS):
                    bank = nbank()
                    for c in range(32):
                        mm(bank.ap[:, 0:nb], xT_ap[:, c, i * 128:(i + 1) * 128], w3[:, c, 0:nb], c == 0, c == 31,
                           [xT[i], w], [bank])
                    sg = stg[(blk * TPS + i) % 4]
                    rows = slice(row0 + i * 128, row0 + (i + 1) * 128)
                    sgb = sg.ap.bitcast(BF16)[:, 0:nb]
                    if kind == "qk":
                        cp("dve", sg.ap[:, 0:nb], bank.ap[:, 0:nb], [bank], [sg])
                        S.dma("sp", QKw[rows, n0:n0 + nb], sg.ap[:, 0:nb], reads=[sg])
                    elif kind == "alr":
                        cp("dve", sg.ap[:, 0:nb], bank.ap[:, 0:nb], [bank], [sg])
                        S.dma("sp", ALRw[rows, 0:16], sg.ap[:, 0:16], reads=[sg])
                    elif kind == "v":
                        cp("dve", sgb, bank.ap[:, 0:nb], [bank], [sg])
                        S.dma("sp", Vsw[rows, n0:n0 + nb], sgb, reads=[sg])
                    elif kind == "u":
                        cp("dve", sgb, bank.ap[:, 0:nb], [bank], [sg])
                        S.dma("sp", Usw[rows, n0:n0 + nb], sgb, reads=[sg])
                    elif kind == "r":
                        act(sgb, bank.ap[:, 0:nb], AF.Silu, [bank], [sg])
                        S.dma("sp", RSw[rows, n0:n0 + nb], sgb, reads=[sg])
                    else:
                        zo = n0 if kind == "zg" else 4096 + n0
                        act(sgb, bank.ap[:, 0:nb], AF.Sigmoid, [bank], [sg])
                        S.dma("sp", Z[rows, zo:zo + nb], sgb, reads=[sg])
                blk += 1

    def stage_G(l, sb):
        S.fence()
        A.p = PERS
        row0 = sb * TSB
        Sf = A.f32(8 * 512)
        Sb = A.bf(8 * 512)
        Sf3 = Sf.ap.rearrange("p (k v) -> p k v", k=8)
        Sb3 = Sb.ap.rearrange("p (k v) -> p k v", k=8)
        Sft = [TL(Sf3[:, k, :]) for k in range(8)]
        Sbt = [TL(Sb3[:, k, :]) for k in range(8)]
        waug = A.f32(1024)
        aug = A.f32(128)
        gt = A.f32(512)
        if sb == 0:
            memset("dve", Sf.ap, 0.0, Sft)
        else:
            S.dma("sp", Sf3, GST[:, :, :], writes=Sft)
        for k in range(8):
            cp("act", Sb3[:, k, :], Sf3[:, k, :], [Sft[k]], [Sbt[k]])
        memset("dve", waug.ap[0:33, :], 0.0, [waug])
        memset("dve", aug.ap[0:33, :], 0.0, [aug])
        memset("dve", aug.ap[32:33, :], 1.0, [aug])
        S.dma("sp", waug.ap[0:16, :], W["gla_w_a2"][l, :, :], writes=[waug])
        S.dma("sp", waug.ap[32:33, :], W["gla_b_a"][l:l + 1, :], writes=[waug])
        S.dma("sp", gt.ap, W["gla_norm_g"][l, :].partition_broadcast(128), writes=[gt])
        qkb = [A.f32(2048) for _ in range(2)]
        vbf = [A.bf(2048) for _ in range(2)]
        rbf = [A.bf(2048) for _ in range(2)]
        alrb = [A.f32(16) for _ in range(2)]
        la = A.f32(1024)
        eb = A.f32(1024)
        enb = A.f32(1024)
        ee = A.f32(1024)
        ebl = A.f32(128)
        qd = A.bf(1024)
        ki = A.bf(1024)
        ke = A.bf(1024)
        qdT = A.bf(1024)
        kiT = A.bf(1024)
        qdT3 = qdT.ap.rearrange("p (k t) -> p k t", k=8)
        kiT3 = kiT.ap.rearrange("p (k t) -> p k t", k=8)
        scm = [A.bf(128) for _ in range(2)]
        qm = [A.bf(2 * 16 * 128) for _ in range(2)]
        sjb = [A.bf(1024) for _ in range(3)]
        kem = [A.bf(1024) for _ in range(2)]
        sjf = [A.f32(1024) for _ in range(2)]
        sjo = [A.f32(1024) for _ in range(2)]
        on = A.f32(512)
        junk = A.f32(512)
        scf = A.f32(128)
        obs = A.f32(512)
        stmp = [A.f32(512) for _ in range(2)]
        ogb = [A.bf(2048) for _ in range(2)]
        ss = A.f32(16)
        for i in range(TPS):
            sample = (i == NPT)
            rows = slice(row0 + i * 128, row0 + (i + 1) * 128)
            qk = qkb[i % 2]
            v = vbf[i % 2]
            r = rbf[i % 2]
            alr = alrb[i % 2]
            og = ogb[i % 2]
            S.dma("sp", qk.ap, QK[rows, :], writes=[qk])
            S.dma("sp", v.ap, Vs[rows, :], writes=[v])
            S.dma("sp", r.ap, RS[rows, :], writes=[r])
            S.dma("sp", alr.ap, ALR[rows, :], writes=[alr])
            bk = nbank()
            trp(bk.ap[0:16, 0:128], alr.ap, ident_f.ap, [alr, ident_f], [bk])
            cp("dve", aug.ap[0:16, :], bk.ap[0:16, 0:128], [bk], [aug])
            b0, b1 = nbank(), nbank()
            mm(b0.ap, aug.ap[0:33, :], waug.ap[0:33, 0:512], True, True, [aug, waug], [b0])
            mm(b1.ap, aug.ap[0:33, :], waug.ap[0:33, 512:1024], True, True, [aug, waug], [b1])
            act(la.ap[:, 0:512], b0.ap, AF.Exp, [b0], [la], scale=-1.0)
            act(la.ap[:, 512:1024], b1.ap, AF.Exp, [b1], [la], scale=-1.0)
            act(la.ap, la.ap, AF.Ln, [la, onescol], [la], bias=onescol.ap)
            ts("dve", la.ap, la.ap, -1.0 / 16.0, None, ALU.mult, None, [la], [la])
            tl_ = triLs if sample else triL
            tr_ = triRs if sample else triR
            nsq = 16 if sample else 1
            sel = seqsel if sample else onescol
            bb = [nbank(), nbank()]
            be = [nbank(), nbank()]
            for hh in range(2):
                mm(bb[hh].ap, tl_.ap, la.ap[:, hh * 512:(hh + 1) * 512], True, True, [tl_, la], [bb[hh]])
                mm(be[hh].ap, tr_.ap, la.ap[:, hh * 512:(hh + 1) * 512], True, True, [tr_, la], [be[hh]])
            bl = nbank()
            for dc in range(8):
                mm(bl.ap[:, dc * 16:dc * 16 + nsq], la.ap[:, dc * 128:(dc + 1) * 128], sel.ap[:, 0:nsq], True, True,
                   [la, sel], [bl])
            for hh in range(2):
                sl = slice(hh * 512, (hh + 1) * 512)
                act(eb.ap[:, sl], bb[hh].ap, AF.Exp, [bb[hh]], [eb])
                act(enb.ap[:, sl], bb[hh].ap, AF.Exp, [bb[hh]], [enb], scale=-1.0)
                act(ee.ap[:, sl], be[hh].ap, AF.Exp, [be[hh]], [ee])
            eblv = ebl.ap.rearrange("p (d j) -> p d j", j=16)
            act(eblv[:, :, 0:nsq], bl.ap[:, 0:128].rearrange("p (d j) -> p d j", j=16)[:, :, 0:nsq], AF.Exp, [bl], [ebl])
            stt("dve", qd.ap, qk.ap[:, 0:1024], 0.0625, eb.ap, ALU.mult, ALU.mult, [qk, eb], [qd])
            tt("pool", ki.ap, qk.ap[:, 1024:2048], enb.ap, ALU.mult, [qk, enb], [ki])
            tt("pool", ke.ap, qk.ap[:, 1024:2048], ee.ap, ALU.mult, [qk, ee], [ke])
            tr_bf(qd, qd.ap, 8, qdT, qdT3)
            tr_bf(ki, ki.ap, 8, kiT, kiT3)
            memset("dve", ss.ap, 0.0, [ss])
            for h in range(4):
                vh = v.ap[:, h * 512:(h + 1) * 512]
                sc = nbank()
                for j in range(2):
                    mm(sc.ap[:, 0:128], kiT3[:, 2 * h + j, :], qdT3[:, 2 * h + j, :], j == 0, j == 1, [kiT, qdT], [sc])
                sm_ = scm[h % 2]
                cp("act", scf.ap, sc.ap[:, 0:128], [sc], [scf])
                tt("dve", sm_.ap, scf.ap, tl_.ap, ALU.mult, [scf, tl_], [sm_])
                ob = nbank()
                mm(ob.ap, sm_.ap, vh, True, False, [sm_, v], [ob])
                if not sample:
                    for j in range(2):
                        mm(ob.ap, qdT3[:, 2 * h + j, :], Sb3[:, 2 * h + j, :], False, j == 1, [qdT, Sbt[2 * h + j]], [ob])
                else:
                    q_ = qm[h % 2]
                    q4 = q_.ap.rearrange("p (j s t) -> p j s t", j=2, s=16)
                    for j in range(2):
                        tt("dve", q4[:, j], qdT3[:, 2 * h + j, :].unsqueeze(1).to_broadcast([128, 16, 128]),
                           colmask.ap.rearrange("p (s t) -> p s t", s=16), ALU.mult, [qdT, colmask], [q_])
                    for sq in range(16):
                        sj = sjb[(h * 16 + sq) % 3]
                        sj3 = sj.ap.rearrange("p (j v) -> p j v", j=2)
                        S.dma("pool", sj3, sgla[l, sb * 16 + sq, h].rearrange("(j p) v -> p j v", p=128), writes=[sj])
                        for j in range(2):
                            mm(ob.ap, q4[:, j, sq, :], sj3[:, j, :], False, (sq == 15 and j == 1), [q_, sj], [ob])
                if not sample:
                    for j in range(2):
                        k = 2 * h + j
                        sbk = nbank()
                        mm(sbk.ap, ke.ap[:, k * 128:(k + 1) * 128], vh, True, True, [ke, v], [sbk])
                        sp_ = stmp[k % 2]
                        cp("act", sp_.ap, sbk.ap, [sbk], [sp_])
                        stt("dve", Sf3[:, k, :], Sf3[:, k, :], ebl.ap[:, k * 16:k * 16 + 1], sp_.ap, ALU.mult, ALU.add,
                            [Sft[k], ebl, sp_], [Sft[k]])
                        cp("act", Sb3[:, k, :], Sf3[:, k, :], [Sft[k]], [Sbt[k]])
                cp("act", obs.ap, ob.ap, [ob], [obs])
                act(junk.ap, obs.ap, AF.Square, [obs], [junk, ss], accum=ss.ap[:, h:h + 1])
                ts("dve", ss.ap[:, 4 + h:5 + h], ss.ap[:, h:h + 1], 1.0 / 512.0, None, ALU.mult, None, [ss], [ss])
                act(ss.ap[:, 8 + h:9 + h], ss.ap[:, 4 + h:5 + h], AF.Ln, [ss, epscol], [ss], bias=epscol.ap[:, 1:2])
                act(ss.ap[:, 12 + h:13 + h], ss.ap[:, 8 + h:9 + h], AF.Exp, [ss], [ss], scale=-0.5)
                stt("dve", on.ap, obs.ap, ss.ap[:, 12 + h:13 + h], gt.ap, ALU.mult, ALU.mult, [obs, ss, gt], [on])
                tt("pool", og.ap[:, h * 512:(h + 1) * 512], on.ap, r.ap[:, h * 512:(h + 1) * 512], ALU.mult, [on, r], [og])
            S.dma("sp", OG[rows, :], og.ap, reads=[og])
            if sample:
                for sq in range(16):
                    km = kem[sq % 2]
                    ts("dve", km.ap, ke.ap, seqsel.ap[:, sq:sq + 1], None, ALU.mult, None, [ke, seqsel], [km])
                    for h in range(4):
                        vh = v.ap[:, h * 512:(h + 1) * 512]
                        f_ = sjf[(sq * 4 + h) % 2]
                        o_ = sjo[(sq * 4 + h) % 2]
                        f3 = f_.ap.rearrange("p (j v) -> p j v", j=2)
                        o3 = o_.ap.rearrange("p (j v) -> p j v", j=2)
                        S.dma("sp", f3, sgla[l, sb * 16 + sq, h].rearrange("(j p) v -> p j v", p=128), writes=[f_])
                        for j in range(2):
                            k = 2 * h + j
                            sbk = nbank()
                            mm(sbk.ap, km.ap[:, k * 128:(k + 1) * 128], vh, True, True, [km, v], [sbk])
                            sp_ = stmp[k % 2]
                            cp("act", sp_.ap, sbk.ap, [sbk], [sp_])
                            stt("dve", o3[:, j, :], f3[:, j, :], ebl.ap[:, k * 16 + sq:k * 16 + sq + 1], sp_.ap,
                                ALU.mult, ALU.add, [f_, ebl, sp_], [o_])
                        S.dma("sp", glas[l, sb * 16 + sq, h].rearrange("(j p) v -> p j v", p=128), o3, reads=[o_])
        if sb == NSB - 1:
            S.dma("sp", glap[l].rearrange("h (j p) v -> p (h j) v", p=128), Sf3, reads=Sft)
        else:
            S.dma("sp", GST[:, :, :], Sf3, reads=Sft)

    def gen_s5(l):
        S.fence()
        A.p = PERS
        araw = A.f32(128)
        are = A.f32(128, parts=64)
        aim = A.f32(128, parts=64)
        dt = A.f32(128, parts=64)
        ldr = A.f32(128, parts=64)
        th = A.f32(128, parts=64)
        P = A.f32(2 * 17 * 128, parts=64)
        Q = A.f32(2 * 16 * 128, parts=64)
        P4 = P.ap.rearrange("p (r k g) -> p r k g", r=2, k=17)
        Q4 = Q.ap.rearrange("p (r k g) -> p r k g", r=2, k=16)
        t1 = A.f32(128, parts=64)
        t2 = A.f32(128, parts=64)
        t3 = A.f32(128, parts=64)
        ti_ = A.i32(128, parts=64)
        mag = A.f32(128, parts=64)
        cr = A.f32(128, parts=64)
        ci = A.f32(128, parts=64)
        big1 = A.f32(16 * 128, parts=64)
        big2 = A.f32(16 * 128, parts=64)
        S.dma("sp", araw.ap[:, 0:64], W["s5_a_re"][l], writes=[araw])
        S.dma("sp", araw.ap[:, 64:128], W["s5_a_im"][l], writes=[araw])
        bk = nbank()
        trp(bk.ap[0:64, 0:128], araw.ap[:, 0:64], ident_f.ap, [araw, ident_f], [bk])
        trp(bk.ap[0:64, 128:256], araw.ap[:, 64:128], ident_f.ap, [araw, ident_f], [bk])
        cp("dve", are.ap, bk.ap[0:64, 0:128], [bk], [are])
        cp("dve", aim.ap, bk.ap[0:64, 128:256], [bk], [aim])
        S.dma("sp", dt.ap, W["s5_log_dt"][l, :].partition_broadcast(64), writes=[dt])
        act(dt.ap, dt.ap, AF.Exp, [dt], [dt])
        tt("dve", ldr.ap, are.ap, dt.ap, ALU.mult, [are, dt], [ldr])
        tt("dve", th.ap, aim.ap, dt.ap, ALU.mult, [aim, dt], [th])
        for kk in range(17):
            k = kk - 8
            act(mag.ap, ldr.ap, AF.Exp, [ldr], [mag], scale=float(k))
            for r_, off in ((0, math.pi / 2), (1, 0.0)):
                ts("dve", t1.ap, th.ap, float(k), off, ALU.mult, ALU.add, [th], [t1])
                ts("dve", ti_.ap, t1.ap, 1.0 / TWO_PI, None, ALU.mult, None, [t1], [ti_])
                cp("dve", t2.ap, ti_.ap, [ti_], [t2])
                stt("dve", t3.ap, t2.ap, -TWO_PI, t1.ap, ALU.mult, ALU.add, [t2, t1], [t3])
                ts("dve", t3.ap, t3.ap, -3.14159, 3.14159, ALU.max, ALU.min, [t3], [t3])
                act(t2.ap, t3.ap, AF.Sin, [t3], [t2])
                tt("dve", P4[:, r_, kk, :], mag.ap, t2.ap, ALU.mult, [mag, t2], [P])
        if CUT == "a":
            return
        ts("dve", t1.ap, P4[:, 0, 9, :], -1.0, None, ALU.add, None, [P], [t1])
        tt("dve", t2.ap, are.ap, are.ap, ALU.mult, [are], [t2])
        tt("dve", t3.ap, aim.ap, aim.ap, ALU.mult, [aim], [t3])
        tt("dve", t2.ap, t2.ap, t3.ap, ALU.add, [t2, t3], [t2])
        S.op("dve", lambda e: e.reciprocal(out=t2.ap, in_=t2.ap), [t2], [t2])
        tt("dve", cr.ap, t1.ap, are.ap, ALU.mult, [t1, are], [cr])
        tt("dve", t3.ap, P4[:, 1, 9, :], aim.ap, ALU.mult, [P, aim], [t3])
        tt("dve", cr.ap, cr.ap, t3.ap, ALU.add, [cr, t3], [cr])
        tt("dve", cr.ap, cr.ap, t2.ap, ALU.mult, [cr, t2], [cr])
        tt("dve", ci.ap, P4[:, 1, 9, :], are.ap, ALU.mult, [P, are], [ci])
        tt("dve", t3.ap, t1.ap, aim.ap, ALU.mult, [t1, aim], [t3])
        tt("dve", ci.ap, ci.ap, t3.ap, ALU.subtract, [ci, t3], [ci])
        tt("dve", ci.ap, ci.ap, t2.ap, ALU.mult, [ci, t2], [ci])
        crb = cr.ap.unsqueeze(1).to_broadcast([64, 16, 128])
        cib = ci.ap.unsqueeze(1).to_broadcast([64, 16, 128])
        b1v = big1.ap.rearrange("p (k g) -> p k g", k=16)
        b2v = big2.ap.rearrange("p (k g) -> p k g", k=16)
        tt("dve", b1v, P4[:, 0, 0:16, :], crb, ALU.mult, [P, cr], [big1])
        tt("dve", b2v, P4[:, 1, 0:16, :], cib, ALU.mult, [P, ci], [big2])
        tt("dve", Q4[:, 0], b1v, b2v, ALU.subtract, [big1, big2], [Q])
        tt("dve", b1v, P4[:, 1, 0:16, :], crb, ALU.mult, [P, cr], [big1])
        tt("dve", b2v, P4[:, 0, 0:16, :], cib, ALU.mult, [P, ci], [big2])
        tt("dve", Q4[:, 1], b1v, b2v, ALU.add, [big1, big2], [Q])
        cp("dve", A8.ap.rearrange("p (r g) -> p r g", r=2), P4[:, :, 16, :], [P], [A8])
        if CUT == "b":
            return
        mark = A.p
        GG = 16
        for gc in range(128 // GG):
            g0 = gc * GG
            S.fence()
            A.p = mark
            BRI = A.f32(2 * GG * 16, parts=64)
            BRI4 = BRI.ap.rearrange("p (r g m) -> p r g m", r=2, g=GG)
            craw = A.f32(2 * 16 * 64, parts=GG)
            craw4 = craw.ap.rearrange("g (r n p) -> g r n p", r=2, n=16)
            CR = A.f32(2 * GG * 16, parts=64)
            CR4 = CR.ap.rearrange("p (r g n) -> p r g n", r=2, g=GG)
            BTW = A.f32(2 * GG * 128, parts=64)
            BTT = A.f32(2 * GG * 128, parts=64)
            CB = A.f32(2 * GG * 128, parts=64)
            BTW5 = BTW.ap.rearrange("p (r g s m) -> p r g s m", r=2, g=GG, s=8)
            BTT5 = BTT.ap.rearrange("p (r g s m) -> p r g s m", r=2, g=GG, s=8)
            CB5 = CB.ap.rearrange("p (r g t n) -> p r g t n", r=2, g=GG, t=8)
            BTW4 = BTW.ap.rearrange("p (r g k) -> p r g k", r=2, g=GG)
            BTT4 = BTT.ap.rearrange("p (r g k) -> p r g k", r=2, g=GG)
            CB4 = CB.ap.rearrange("p (r g k) -> p r g k", r=2, g=GG)
            u1 = A.f32(GG * 16, parts=64)
            u2 = A.f32(GG * 16, parts=64)
            u1v = u1.ap.rearrange("p (g m) -> p g m", g=GG)
            u2v = u2.ap.rearrange("p (g m) -> p g m", g=GG)
            smaS = A.bf(GG * 256)
            smcS = A.bf(GG * 256, parts=64)
            smaS3 = smaS.ap.rearrange("p (g k) -> p g k", g=GG)
            smcS3 = smcS.ap.rearrange("p (g k) -> p g k", g=GG)
            dcol = A.f32(GG)
            tmpT = A.f32(128)
            for r_, nm in ((0, "s5_b_re"), (1, "s5_b_im")):
                S.dma("sp", BRI4[:, r_], W[nm][l, g0:g0 + GG].rearrange("g p m -> p g m"), writes=[BRI])
            for r_, nm in ((0, "s5_c_re"), (1, "s5_c_im")):
                S.dma("sp", craw4[:, r_], W[nm][l, g0:g0 + GG], writes=[craw])
            for s_ in range(8):
                S.dma("sp", dcol.ap[s_ * 16:(s_ + 1) * 16, :],
                      W["s5_d"][l, g0 * 16:(g0 + GG) * 16].rearrange("(g m) -> m g", m=16), writes=[dcol],
                      allow_slow_non_contiguous=True)
            bks = [nbank(), nbank()]
            for r_ in range(2):
                for n_ in range(16):
                    trp(bks[r_].ap[0:64, n_ * GG:(n_ + 1) * GG], craw4[:, r_, n_, :], ident_f.ap[0:GG, 0:GG],
                        [craw, ident_f], [bks[r_]])
                cp("dve", CR4[:, r_].rearrange("p g n -> p n g"),
                   bks[r_].ap[0:64, 0:16 * GG].rearrange("p (n g) -> p n g", n=16), [bks[r_]], [CR])

            def cmul(dst5, Xri, Yr, Yi, neg_im, idx):
                yrb = Yr.unsqueeze(2).to_broadcast([64, GG, 16])
                yib = Yi.unsqueeze(2).to_broadcast([64, GG, 16])
                tt("dve", u1v, Xri[:, 0], yrb, ALU.mult, [BRI, CR, P, Q], [u1])
                tt("pool", u2v, Xri[:, 1], yib, ALU.mult, [BRI, CR, P, Q], [u2])
                tt("dve", dst5[:, 0, :, idx, :], u1v, u2v, ALU.subtract, [u1, u2], [BTW, BTT, CB])
                tt("dve", u1v, Xri[:, 0], yib, ALU.mult, [BRI, CR, P, Q], [u1])
                tt("pool", u2v, Xri[:, 1], yrb, ALU.mult, [BRI, CR, P, Q], [u2])
                if neg_im:
                    stt("dve", dst5[:, 1, :, idx, :], u1v, -1.0, u2v, ALU.mult, ALU.subtract, [u1, u2], [BTW, BTT, CB])
                else:
                    tt("dve", dst5[:, 1, :, idx, :], u1v, u2v, ALU.add, [u1, u2], [BTW, BTT, CB])

            if CUT == "c":
                return
            gs = slice(g0, g0 + GG)
            for s_ in range(8):
                cmul(BTW5, BRI4, Q4[:, 0, 15 - s_, gs], Q4[:, 1, 15 - s_, gs], False, s_)
                cmul(BTT5, BRI4, Q4[:, 0, 7 - s_, gs], Q4[:, 1, 7 - s_, gs], False, s_)
                cmul(CB5, CR4, P4[:, 0, s_ + 9, gs], P4[:, 1, s_ + 9, gs], True, s_)
            if CUT == "d":
                return
            for g in range(GG):
                bk = nbank()
                trp(bk.ap[:, 0:64], BTW4[:, 0, g, :], ident_f.ap[0:64, 0:64], [BTW, ident_f], [bk])
                trp(bk.ap[:, 64:128], BTW4[:, 1, g, :], ident_f.ap[0:64, 0:64], [BTW, ident_f], [bk])
                if CUT == "d1":
                    continue
                mm(bk.ap[:, 128:256], BTT4[:, 0, g, :], CB4[:, 0, g, :], True, False, [BTT, CB], [bk])
                mm(bk.ap[:, 128:256], BTT4[:, 1, g, :], CB4[:, 1, g, :], False, True, [BTT, CB], [bk])
                if CUT == "d2":
                    continue
                cp("act", smaS3[:, g, 0:128], bk.ap[:, 0:128], [bk], [smaS])
                if CUT == "d3":
                    continue
                cp("act", tmpT.ap, bk.ap[:, 128:256], [bk], [tmpT])
                tt("dve", tmpT.ap, tmpT.ap, bmask.ap, ALU.mult, [tmpT, bmask], [tmpT])
                if CUT == "d4":
                    continue
                stt("dve", smaS3[:, g, 128:256], ident_f.ap, dcol.ap[:, g:g + 1], tmpT.ap, ALU.mult, ALU.add,
                    [ident_f, dcol, tmpT], [smaS])
            if CUT in ("d1", "d2", "d3", "d4"):
                return
            if CUT == "e":
                return
            cp("act", smcS3[:, :, 0:128], CB4[:, 0], [CB], [smcS])
            cp("act", smcS3[:, :, 128:256], CB4[:, 1], [CB], [smcS])
            if CUT == "f":
                return
            S.dma("sp", SMA[:, g0:g0 + GG, :], smaS3, reads=[smaS])
            S.dma("sp", SMC[:, g0:g0 + GG, :], smcS3, reads=[smcS])

    def stage_S(l, sb):
        row0 = sb * TSB
        GQ = 32
        if sb == 0:
            S.fence()
            memset("dve", XST.ap, 0.0, [XST])
        XST3 = XST.ap.rearrange("p (r g) -> p r g", r=2)
        A83 = A8.ap.rearrange("p (r g) -> p r g", r=2)
        for gc in range(128 // GQ):
            g0 = gc * GQ
            S.fence()
            A.p = PERS
            sma = A.bf(GQ * 256)
            smc = A.bf(GQ * 256, parts=64)
            sma3 = sma.ap.rearrange("p (g k) -> p g k", g=GQ)
            smc3 = smc.ap.rearrange("p (g k) -> p g k", g=GQ)
            S.dma("sp", sma3, SMA[:, g0:g0 + GQ, :], writes=[sma])
            S.dma("sp", smc3, SMC[:, g0:g0 + GQ, :], writes=[smc])
            ug = A.bf(GQ * NCB)
            ug3 = ug.ap.rearrange("p (g c) -> p g c", g=GQ)
            mark_s = A.p
            ub = A.bf(8 * GQ * 16)
            ubs = A.bf(8 * GQ * 16, parts=16)
            ub4 = ub.ap.rearrange("c (s g m) -> c s g m", s=8, g=GQ)
            ubs4 = ubs.ap.rearrange("c (s g m) -> c s g m", s=8, g=GQ)
            cols = slice(g0 * 16, (g0 + GQ) * 16)
            S.dma("sp", ub.ap[0:NCP, :].rearrange("c (s f) -> c s f", s=8),
                  Us[row0:row0 + NPT * 128, cols].rearrange("(c s) f -> c s f", s=8), writes=[ub])
            S.dma("sp", ubs.ap.rearrange("c (s f) -> c s f", s=8),
                  Us[row0 + NPT * 128:row0 + TSB, cols].rearrange("(c s) f -> c s f", s=8), writes=[ubs])
            ub2 = A.bf(8 * GQ * 16)
            ubs2 = A.bf(8 * GQ * 16, parts=16)
            cp("dve", ub2.ap[0:NCP, :].rearrange("c (g s m) -> c g s m", g=GQ, s=8), ub4[0:NCP].rearrange("c s g m -> c g s m"), [ub], [ub2])
            cp("act", ubs2.ap.rearrange("c (g s m) -> c g s m", g=GQ, s=8), ubs4.rearrange("c s g m -> c g s m"), [ubs], [ubs2])
            ub2v = ub2.ap.rearrange("c (g k) -> c g k", g=GQ)
            ubs2v = ubs2.ap.rearrange("c (g k) -> c g k", g=GQ)
            for gq in range(0, GQ, 4):
                bank = nbank()
                pb = bank.ap.bitcast(BF16)
                for k in range(4):
                    g = gq + k
                    trp(pb[:, k * 256:k * 256 + NCP], ub2v[0:NCP, g, :], ident_b.ap[0:NCP, 0:NCP], [ub2, ident_b], [bank])
                    trp(pb[:, k * 256 + 128:k * 256 + 144], ubs2v[:, g, :], ident_b.ap[0:16, 0:16], [ubs2, ident_b], [bank])
                pbv = pb.rearrange("p (k c) -> p k c", k=4)
                cp(alt(), ug3[:, gq:gq + 4, 0:NCP], pbv[:, :, 0:NCP], [bank], [ug])
                cp(alt(), ug3[:, gq:gq + 4, NCP:NCB], pbv[:, :, 128:144], [bank], [ug])
            S.fence()
            A.p = mark_s
            W2 = A.f32(2 * GQ * NCB, parts=64)
            W4 = W2.ap.rearrange("p (r g c) -> p r g c", r=2, g=GQ)
            for g in range(GQ):
                bank = nbank()
                for r_ in range(2):
                    mm(bank.ap[0:64, r_ * 256:r_ * 256 + NCB], sma3[:, g, r_ * 64:(r_ + 1) * 64], ug3[:, g, :], True, True,
                       [sma, ug], [bank])
                cp(alt(), W4[:, :, g, :], bank.ap[0:64, :].rearrange("p (r c) -> p r c", r=2)[:, :, 0:NCB], [bank], [W2])
            XP = A.f32(2 * GQ * (NCP + 1), parts=64)
            XP4 = XP.ap.rearrange("p (r g c) -> p r g c", r=2, g=GQ)
            XS = A.f32(2 * GQ * 16, parts=64)
            XS4 = XS.ap.rearrange("p (r g j) -> p r g j", r=2, g=GQ)
            XN = A.f32(2 * GQ * 16, parts=64)
            XN4 = XN.ap.rearrange("p (r g j) -> p r g j", r=2, g=GQ)
            XB = A.bf(2 * GQ * NCB, parts=64)
            XB4 = XB.ap.rearrange("p (r g c) -> p r g c", r=2, g=GQ)
            AA = A.f32(2 * GQ, parts=64)
            II = A.f32(2 * GQ, parts=64)
            AA3 = AA.ap.rearrange("p (r g) -> p r g", r=2)
            II3 = II.ap.rearrange("p (r g) -> p r g", r=2)
            tA = A.f32(2 * GQ, parts=64)
            tU = A.f32(2 * GQ, parts=64)
            tA3 = tA.ap.rearrange("p (r g) -> p r g", r=2)
            tU3 = tU.ap.rearrange("p (r g) -> p r g", r=2)
            for r_ in range(2):
                cp("pool", AA3[:, r_, :], A83[:, 0, g0:g0 + GQ], [A8], [AA])
                cp("pool", II3[:, r_, :], A83[:, 1, g0:g0 + GQ], [A8], [II])
            cp("pool", XP4[:, :, :, 0], XST3[:, :, g0:g0 + GQ], [XST], [XP])
            for c in range(NCP):
                xc = XP4[:, :, :, c]
                tt("pool", tA3, AA3, xc, ALU.mult, [AA, XP], [tA])
                tt("pool", tU3, II3, xc, ALU.mult, [II, XP], [tU])
                tt("pool", tA3, tA3, W4[:, :, :, c], ALU.add, [tA, W2], [tA])
                tt("pool", XP4[:, 0, :, c + 1], tA3[:, 0, :], tU3[:, 1, :], ALU.subtract, [tA, tU], [XP])
                tt("pool", XP4[:, 1, :, c + 1], tA3[:, 1, :], tU3[:, 0, :], ALU.add, [tA, tU], [XP])
            cp("pool", XST3[:, :, g0:g0 + GQ], XP4[:, :, :, NCP], [XP], [XST])
            sin = A.f32(16 * 2 * 64, parts=GQ)
            sin4 = sin.ap.rearrange("g (j r p) -> g j r p", j=16, r=2)
            for r_ in range(2):
                S.dma("sp", sin4[:, :, r_, :], ss5[r_][l, sb * 16:(sb + 1) * 16, g0:g0 + GQ, :].rearrange("j g p -> g j p"),
                      writes=[sin])
            bks = [nbank(), nbank()]
            for j in range(16):
                for r_ in range(2):
                    idx = j * 2 + r_
                    trp(bks[idx // 16].ap[0:64, (idx % 16) * GQ:(idx % 16 + 1) * GQ], sin4[:, j, r_, :],
                        ident_f.ap[0:GQ, 0:GQ], [sin, ident_f], [bks[idx // 16]])
            for hb in range(2):
                cp("dve", XS4[:, :, :, hb * 8:(hb + 1) * 8].rearrange("p r g j -> p j r g"),
                   bks[hb].ap[0:64, 0:16 * GQ].rearrange("p (j r g) -> p j r g", j=8, r=2), [bks[hb]], [XS])
            t5 = A.f32(2 * GQ * 16, parts=64)
            u5 = A.f32(2 * GQ * 16, parts=64)
            t54 = t5.ap.rearrange("p (r g j) -> p r g j", r=2, g=GQ)
            u54 = u5.ap.rearrange("p (r g j) -> p r g j", r=2, g=GQ)
            AAb = AA3.unsqueeze(3).to_broadcast([64, 2, GQ, 16])
            IIb = II3.unsqueeze(3).to_broadcast([64, 2, GQ, 16])
            tt("dve", t54, XS4, AAb, ALU.mult, [XS, AA], [t5])
            tt("dve", u54, XS4, IIb, ALU.mult, [XS, II], [u5])
            tt("dve", t54, t54, W4[:, :, :, NCP:NCB], ALU.add, [t5, W2], [t5])
            tt("dve", XN4[:, 0], t54[:, 0], u54[:, 1], ALU.subtract, [t5, u5], [XN])
            tt("dve", XN4[:, 1], t54[:, 1], u54[:, 0], ALU.add, [t5, u5], [XN])
            sout = A.f32(16 * 2 * 64, parts=GQ)
            sout4 = sout.ap.rearrange("g (j r p) -> g j r p", j=16, r=2)
            for q4 in range(4):
                bank = nbank()
                for jj in range(4):
                    for r_ in range(2):
                        j = q4 * 4 + jj
                        o_ = (jj * 2 + r_) * 64
                        trp(bank.ap[0:GQ, o_:o_ + 64], XN4[:, r_, :, j], ident_f.ap[0:64, 0:64], [XN, ident_f], [bank])
                cp(alt(), sout.ap[:, q4 * 512:(q4 + 1) * 512], bank.ap[0:GQ, :], [bank], [sout])
            for r_ in range(2):
                S.dma("sp", s5s[r_][l, sb * 16:(sb + 1) * 16, g0:g0 + GQ, :].rearrange("j g p -> g j p"),
                      sout4[:, :, r_, :], reads=[sout])
            cp("act", XB4[:, :, :, 0:NCP], XP4[:, :, :, 0:NCP], [XP], [XB])
            cp("act", XB4[:, :, :, NCP:NCB], XS4, [XS], [XB])
            yb = A.bf(8 * GQ * 16)
            ybs = A.bf(8 * GQ * 16, parts=16)
            yb4 = yb.ap.rearrange("c (t g n) -> c t g n", t=8, g=GQ)
            ybs4 = ybs.ap.rearrange("c (t g n) -> c t g n", t=8, g=GQ)
            for (c0, m_, ydst, ytl) in ((0, NCP, yb4, yb), (NCP, 16, ybs4, ybs)):
                for gq in range(0, GQ, 4):
                    bank = nbank()
                    for k in range(4):
                        g = gq + k
                        o = bank.ap[0:m_, k * 128:(k + 1) * 128]
                        mm(o, ug3[:, g, c0:c0 + m_], sma3[:, g, 128:256], True, False, [ug, sma], [bank])
                        mm(o, XB4[:, 0, g, c0:c0 + m_], smc3[:, g, 0:128], False, False, [XB, smc], [bank])
                        mm(o, XB4[:, 1, g, c0:c0 + m_], smc3[:, g, 128:256], False, True, [XB, smc], [bank])
                    act(ydst[0:m_, :, gq:gq + 4, :].rearrange("c t g n -> c g t n"),
                        bank.ap[0:m_, :].rearrange("c (g t n) -> c g t n", g=4, t=8), AF.Gelu_apprx_tanh, [bank], [ytl])
            S.dma("sp", YG[row0:row0 + NPT * 128, cols].rearrange("(c t) f -> c t f", t=8),
                  yb.ap[0:NCP, :].rearrange("c (t f) -> c t f", t=8), reads=[yb])
            S.dma("sp", YG[row0 + NPT * 128:row0 + TSB, cols].rearrange("(c t) f -> c t f", t=8),
                  ybs.ap.rearrange("c (t f) -> c t f", t=8), reads=[ybs])
        if sb == NSB - 1:
            S.fence()
            A.p = PERS
            pso = A.f32(128)
            bank = nbank()
            for r_ in range(2):
                trp(bank.ap[:, r_ * 64:(r_ + 1) * 64], XST3[:, r_, :], ident_f.ap[0:64, 0:64], [XST, ident_f], [bank])
            cp("dve", pso.ap, bank.ap[:, 0:128], [bank], [pso])
            for r_ in range(2):
                S.dma("sp", s5p[r_][l], pso.ap[:, r_ * 64:(r_ + 1) * 64], reads=[pso])

    def gen_ut(l):
        S.fence()
        A.p = PERS
        ub_ = [A.bf(D) for _ in range(2)]
        ust = [A.bf(32 * 128) for _ in range(2)]
        vcb = [A.bf(D) for _ in range(2)]
        UTv = UT.rearrange("c d e -> d c e")
        for ec in range(128):
            b = ub_[ec % 2]
            s_ = ust[ec % 2]
            vc = vcb[ec % 2]
            S.dma("pool", vc.ap, W["peer_v"][l, ec * 128:(ec + 1) * 128, :], writes=[vc])
            S.dma("sp", V16[ec * 128:(ec + 1) * 128, :], vc.ap, reads=[vc])
            S.dma("pool", b.ap, W["peer_u"][l, ec * 128:(ec + 1) * 128, :], writes=[b])
            tr_bf(b, b.ap, 32, s_, s_.ap.rearrange("p (c e) -> p c e", c=32))
            S.dma("sp", UTv[:, :, ec * 128:(ec + 1) * 128], s_.ap.rearrange("p (c e) -> p c e", c=32), reads=[s_])

    def stage_B12(l, sb):
        S.fence()
        A.p = PERS
        row0 = sb * TSB
        os5T = A.bf(16 * TSB)
        os5T3 = os5T.ap.rearrange("p (c t) -> p c t", c=16)
        osT = [TL(os5T3[:, c, :]) for c in range(16)]
        mark = A.p
        ygT = A.bf(16 * TSB)
        ygT3 = ygT.ap.rearrange("p (c t) -> p c t", c=16)
        ygt = [TL(ygT3[:, :, i * 128:(i + 1) * 128]) for i in range(TPS)]
        load_T(YG, row0, TPS, 2048, ygT3, ygt, False, blend=PAIR)
        bglu = A.f32(16)
        S.dma("sp", bglu.ap, W["s5_b_glu"][l, :].rearrange("(c p) -> p c", p=128), writes=[bglu],
              allow_slow_non_contiguous=True)
        wg = [A.bf(16 * 128) for _ in range(2)]
        sg = [A.f32(512) for _ in range(2)]
        n_ = 0
        for cc in range(16):
            w = wg[cc % 2]
            w3 = w.ap.rearrange("p (c n) -> p c n", c=16)
            S.dma("pool", w3, W["s5_w_glu"][l, :, cc * 128:(cc + 1) * 128].rearrange("(c p) n -> p c n", p=128), writes=[w])
            for (t0, n) in tok_blocks(TSB):
                bank = nbank()
                rd = [ygt[i] for i in tiles_of(t0, n)]
                for c in range(16):
                    mm(bank.ap[:, 0:n], w3[:, c, :], ygT3[:, c, t0:t0 + n], c == 0, c == 15, [w] + rd, [bank])
                s_ = sg[n_ % 2]
                n_ += 1
                act(s_.ap[:, 0:n], bank.ap[:, 0:n], AF.Sigmoid, [bank, bglu], [s_], bias=bglu.ap[:, cc:cc + 1])
                tt("dve", os5T3[:, cc, t0:t0 + n], s_.ap[:, 0:n], ygT3[:, cc, t0:t0 + n], ALU.mult, [s_] + rd, [osT[cc]])
        S.fence()
        A.p = mark
        ogT = A.bf(16 * TSB)
        ogT3 = ogT.ap.rearrange("p (c t) -> p c t", c=16)
        ogt = [TL(ogT3[:, :, i * 128:(i + 1) * 128]) for i in range(TPS)]
        load_T(OG, row0, TPS, 2048, ogT3, ogt, False, blend=PAIR)
        wbr = [A.bf(32 * 512) for _ in range(2)]
        zt = [A.bf(1024) for _ in range(2)]
        m1 = A.f32(512)
        m2 = A.f32(512)
        mo = [A.bf(512) for _ in range(2)]
        n_ = 0
        for cb in range(8):
            w = wbr[cb % 2]
            w3 = w.ap.rearrange("p (c n) -> p c n", c=32)
            S.dma("pool", w3, W["w_branch"][l, :, cb * 512:(cb + 1) * 512].rearrange("(c p) n -> p c n", p=128), writes=[w])
            for i in range(TPS):
                rows = slice(row0 + i * 128, row0 + (i + 1) * 128)
                ba, bb_ = nbank(), nbank()
                for c in range(16):
                    mm(ba.ap, ogT3[:, c, i * 128:(i + 1) * 128], w3[:, c, :], c == 0, c == 15, [ogt[i], w], [ba])
                for c in range(16):
                    mm(bb_.ap, os5T3[:, c, i * 128:(i + 1) * 128], w3[:, 16 + c, :], c == 0, c == 15, [osT[c], w], [bb_])
                z_ = zt[n_ % 2]
                o_ = mo[n_ % 2]
                n_ += 1
                S.dma("sp", z_.ap[:, 0:512], Z[rows, cb * 512:(cb + 1) * 512], writes=[z_])
                S.dma("sp", z_.ap[:, 512:1024], Z[rows, 4096 + cb * 512:4096 + (cb + 1) * 512], writes=[z_])
                cp("act", m1.ap, ba.ap, [ba], [m1])
                cp("act", m2.ap, bb_.ap, [bb_], [m2])
                tt("dve", m1.ap, m1.ap, z_.ap[:, 0:512], ALU.mult, [m1, z_], [m1])
                tt("dve", m2.ap, m2.ap, z_.ap[:, 512:1024], ALU.mult, [m2, z_], [m2])
                tt("pool", o_.ap, m1.ap, m2.ap, ALU.add, [m1, m2], [o_])
                S.dma("sp", MM[rows, cb * 512:(cb + 1) * 512], o_.ap, reads=[o_])

    def stage_B3(l, sb):
        S.fence()
        A.p = PERS
        row0 = sb * TSB
        mT = A.bf(32 * TSB)
        mT3 = mT.ap.rearrange("p (c t) -> p c t", c=32)
        mt = [TL(mT3[:, :, i * 128:(i + 1) * 128]) for i in range(TPS)]
        mark = A.p
        load_T(MM, row0, TPS, D, mT3, mt, False)
        S.fence()
        A.p = mark
        wo = [A.bf(32 * 512) for _ in range(2)]
        hs = [A.f32(512) for _ in range(3)]
        n_ = 0
        for cb in range(8):
            w = wo[cb % 2]
            w3 = w.ap.rearrange("p (c n) -> p c n", c=32)
            S.dma("pool", w3, W["w_out"][l, :, cb * 512:(cb + 1) * 512].rearrange("(c p) n -> p c n", p=128), writes=[w])
            for i in range(TPS):
                rows = slice(row0 + i * 128, row0 + (i + 1) * 128)
                bank = nbank()
                for c in range(32):
                    mm(bank.ap, mT3[:, c, i * 128:(i + 1) * 128], w3[:, c, :], c == 0, c == 31, [mt[i], w], [bank])
                h_ = hs[n_ % 3]
                n_ += 1
                cp(alt(), h_.ap, bank.ap, [bank], [h_])
                S.dma("sp", FF[rows, cb * 512:(cb + 1) * 512], h_.ap, reads=[h_])

    def stage_LN(l, sb, resid, gname, bname, dst):
        S.fence()
        A.p = PERS
        row0 = sb * TSB
        g_ = A.f32(D)
        b_ = A.f32(D)
        S.dma("sp", g_.ap, W[gname][l, :].partition_broadcast(128), writes=[g_])
        S.dma("sp", b_.ap, W[bname][l, :].partition_broadcast(128), writes=[b_])
        ft = [A.f32(D) for _ in range(2)]
        xt = [A.f32(D) for _ in range(2)]
        ot = [A.f32(D) for _ in range(2)]
        stt_ = [A.f32(8) for _ in range(2)]
        for i in range(TPS):
            rows = slice(row0 + i * 128, row0 + (i + 1) * 128)
            f = ft[i % 2]
            x = xt[i % 2]
            o = ot[i % 2]
            s_ = stt_[i % 2]
            S.dma("sp", f.ap, FF[rows, :], writes=[f])
            S.dma("sp", x.ap, resid[rows, :], writes=[x])
            memset("dve", s_.ap, 0.0, [s_])
            stt("dve", f.ap, x.ap, ALPHA, f.ap, ALU.mult, ALU.add, [x, f], [f])
            act(o.ap, f.ap, AF.Identity, [f], [o, s_], accum=s_.ap[:, 0:1])
            ts("dve", s_.ap[:, 1:2], s_.ap[:, 0:1], -1.0 / D, None, ALU.mult, None, [s_], [s_])
            act(o.ap, f.ap, AF.Square, [f, s_], [o, s_], bias=s_.ap[:, 1:2], accum=s_.ap[:, 2:3])
            ts("dve", s_.ap[:, 3:4], s_.ap[:, 2:3], 1.0 / D, None, ALU.mult, None, [s_], [s_])
            act(s_.ap[:, 4:5], s_.ap[:, 3:4], AF.Ln, [s_, epscol], [s_], bias=epscol.ap[:, 0:1])
            act(s_.ap[:, 5:6], s_.ap[:, 4:5], AF.Exp, [s_], [s_], scale=-0.5)
            ts("dve", o.ap, f.ap, s_.ap[:, 1:2], s_.ap[:, 5:6], ALU.add, ALU.mult, [f, s_], [o])
            tt("pool", o.ap, o.ap, g_.ap, ALU.mult, [o, g_], [o])
            tt("pool", o.ap, o.ap, b_.ap, ALU.add, [o, b_], [o])
            S.dma("sp", dst[rows, :], o.ap, reads=[o])

    def stage_B4(l, sb):
        row0 = sb * TSB
        UTv = UT.rearrange("c d e -> d c e")
        GT = 3
        for gt0 in range(0, TPS, GT):
            ng = min(GT, TPS - gt0)
            ntok = ng * 128
            S.fence()
            A.p = PERS
            x2T = A.bf(32 * ntok)
            x2T3 = x2T.ap.rearrange("p (c t) -> p c t", c=32)
            x2t = [TL(x2T3[:, :, i * 128:(i + 1) * 128]) for i in range(ng)]
            s1p = [A.f32(1024) for _ in range(ng)]
            s2c = [A.f32(1024) for _ in range(ng)]
            thp = [A.f32(8) for _ in range(ng)]
            mark = A.p
            load_T(X2, row0 + gt0 * 128, ng, D, x2T3, x2t, True)
            kraw = A.f32(256)
            kbf = A.bf(256)
            kT = A.bf(256)
            S.dma("sp", kraw.ap[:, 0:128], W["peer_key1"][l], writes=[kraw])
            S.dma("sp", kraw.ap[:, 128:256], W["peer_key2"][l], writes=[kraw])
            cp("dve", kbf.ap, kraw.ap, [kraw], [kbf])
            tr_bf(kbf, kbf.ap, 2, kT, kT.ap.rearrange("p (k t) -> p k t", k=2))
            qT = A.bf(16 * ntok)
            qT3 = qT.ap.rearrange("p (c t) -> p c t", c=16)
            qTt = [TL(qT3[:, c, :]) for c in range(16)]
            wq = [A.bf(32 * 128) for _ in range(2)]
            for cc in range(16):
                w = wq[cc % 2]
                w3 = w.ap.rearrange("p (c n) -> p c n", c=32)
                S.dma("pool", w3, W["peer_w_q"][l, :, cc * 128:(cc + 1) * 128].rearrange("(c p) n -> p c n", p=128), writes=[w])
                bank = nbank()
                for c in range(32):
                    mm(bank.ap[:, 0:ntok], w3[:, c, :], x2T3[:, c, :], c == 0, c == 31, [w] + x2t, [bank])
                cp(alt(), qT3[:, cc, :], bank.ap[:, 0:ntok], [bank], [qTt[cc]])
            sc = A.f32(2048)
            sc3 = sc.ap.rearrange("p (a k) -> p a k", a=16)
            sc4 = sc.ap.rearrange("p (h f k) -> p h f k", h=8, f=2)
            m16 = A.f32(256)
            m163 = m16.ap.rearrange("p (a k) -> p a k", a=16)
            m164 = m16.ap.rearrange("p (h f k) -> p h f k", h=8, f=2)
            wk = [A.f32(128) for _ in range(2)]
            cand = [A.f32(256) for _ in range(2)]
            wk2 = [A.f32(256) for _ in range(2)]
            c16 = A.f32(128)
            c163 = c16.ap.rearrange("p (h k) -> p h k", h=8)
            ex = A.f32(128)
            ex3 = ex.ap.rearrange("p (h k) -> p h k", h=8)
            sm8 = A.f32(32)
            for ti in range(ng):
                bks = [nbank(), nbank(), nbank(), nbank()]
                for a in range(16):
                    mm(bks[a // 4].ap[:, (a % 4) * 128:(a % 4 + 1) * 128], qT3[:, a, ti * 128:(ti + 1) * 128],
                       kT.ap[:, (a % 2) * 128:(a % 2 + 1) * 128], True, True, [qTt[a], kT], [bks[a // 4]])
                for q_ in range(4):
                    cp(alt(), sc.ap[:, q_ * 512:(q_ + 1) * 512], bks[q_].ap, [bks[q_]], [sc])
                for a in range(16):
                    w_ = wk[a % 2]
                    S.op("dve", lambda e, a=a: e.max(out=m163[:, a, 0:8], in_=sc3[:, a, :]), [sc], [m16])
                    S.op("dve", lambda e, a=a, w_=w_: e.match_replace(out=w_.ap, in_to_replace=m163[:, a, 0:8],
                                                                     in_values=sc3[:, a, :], imm_value=-1e30), [sc, m16], [w_])
                    S.op("dve", lambda e, a=a, w_=w_: e.max(out=m163[:, a, 8:16], in_=w_.ap), [w_], [m16])
                for h in range(8):
                    cd = cand[h % 2]
                    w2 = wk2[h % 2]
                    tt("pool", cd.ap.rearrange("p (i j) -> p i j", i=16),
                       m164[:, h, 0, :].unsqueeze(2).to_broadcast([128, 16, 16]),
                       m164[:, h, 1, :].unsqueeze(1).to_broadcast([128, 16, 16]), ALU.add, [m16], [cd])
                    S.op("dve", lambda e, h=h, cd=cd: e.max(out=c163[:, h, 0:8], in_=cd.ap), [cd], [c16])
                    S.op("dve", lambda e, h=h, cd=cd, w2=w2: e.match_replace(out=w2.ap, in_to_replace=c163[:, h, 0:8],
                                                                            in_values=cd.ap, imm_value=-1e30), [cd, c16], [w2])
                    S.op("dve", lambda e, h=h, w2=w2: e.max(out=c163[:, h, 8:16], in_=w2.ap), [w2], [c16])
                tt("dve", ex3, c163, c163[:, :, 0:1].to_broadcast([128, 8, 16]), ALU.subtract, [c16], [ex])
                act(ex.ap, ex.ap, AF.Exp, [ex], [ex])
                S.op("dve", lambda e: e.tensor_reduce(out=sm8.ap[:, 0:8], in_=ex3, axis=AX.X, op=ALU.add), [ex], [sm8])
                act(sm8.ap[:, 8:16], sm8.ap[:, 0:8], AF.Ln, [sm8], [sm8])
                tt("dve", sm8.ap[:, 16:24], sm8.ap[:, 8:16], c163[:, :, 0], ALU.add, [sm8, c16], [sm8])
                tt("dve", thp[ti].ap, c163[:, :, 15], sm8.ap[:, 16:24], ALU.subtract, [c16, sm8], [thp[ti]])
                tt("dve", s1p[ti].ap.rearrange("p (h k) -> p h k", h=8), sc4[:, :, 0, :],
                   sm8.ap[:, 16:24].unsqueeze(2).to_broadcast([128, 8, 128]), ALU.subtract, [sc, sm8], [s1p[ti]])
                cp("pool", s2c[ti].ap.rearrange("p (h k) -> p h k", h=8), sc4[:, :, 1, :], [sc], [s2c[ti]])
            S.fence()
            A.p = mark
            utb = A.bf(32 * 512)
            utb3 = utb.ap.rearrange("p (c e) -> p c e", c=32)
            vb = A.bf(4 * D)
            vb3 = vb.ap.rearrange("p (c d) -> p c d", c=4)
            acc = [A.f32(D) for _ in range(ng)]
            Ap = A.f32(2048)
            Eb = A.bf(2048)
            Ap4 = Ap.ap.rearrange("p (h a k) -> p h a k", h=4, a=4)
            Eb4 = Eb.ap.rearrange("p (h a k) -> p h a k", h=4, a=4)
            Ebt = [TL(Eb4[:, h4]) for h4 in range(4)]
            wtA = A.f32(512)
            wtB = A.f32(512)
            wt = [[A.bf(512) for _ in range(ng)] for _ in range(2)]
            gl = A.f32(512)
            wgb = A.bf(512)
            wgT = [A.bf(512) for _ in range(ng)]
            vtmp = [A.f32(1024) for _ in range(2)]

            def gate_round(eb_, ti, hh):
                s1v = s1p[ti].ap.rearrange("p (h k) -> p h k", h=8)
                s2v = s2c[ti].ap.rearrange("p (h k) -> p h k", h=8)
                hs_ = slice(hh * 4, hh * 4 + 4)
                tt("pool", Ap4, s1v[:, hs_, eb_ * 4:eb_ * 4 + 4].unsqueeze(3).to_broadcast([128, 4, 4, 128]),
                   s2v[:, hs_, :].unsqueeze(2).to_broadcast([128, 4, 4, 128]), ALU.add, [s1p[ti], s2c[ti]], [Ap])
                act(Eb.ap, Ap.ap, AF.Exp, [Ap], Ebt)
                for h4 in range(4):
                    stt("dve", Eb4[:, h4], Ap4[:, h4], thp[ti].ap[:, hh * 4 + h4:hh * 4 + h4 + 1], Eb4[:, h4],
                        ALU.is_ge, ALU.mult, [Ap, Ebt[h4], thp[ti]], [Ebt[h4]])
                dst_ = wtA if hh == 0 else wtB
                S.op("dve", lambda e, dst_=dst_: e.tensor_reduce(out=dst_.ap, in_=Eb.ap.rearrange("p (h k) -> p k h", h=4),
                                                                axis=AX.X, op=ALU.add), Ebt, [dst_])
                if hh == 1:
                    tt("pool", wt[eb_ % 2][ti].ap, wtA.ap, wtB.ap, ALU.add, [wtA, wtB], [wt[eb_ % 2][ti]])

            rounds0 = [(ti, hh) for ti in range(ng) for hh in range(2)]
            S.dma("sp", utb3, UTv[:, :, 0:512], writes=[utb])
            for (ti, hh) in rounds0:
                gate_round(0, ti, hh)
            for eb_ in range(32):
                par = eb_ % 2
                S.dma("sp", vb3, V16[eb_ * 512:(eb_ + 1) * 512, :].rearrange("(c p) d -> p c d", p=128), writes=[vb])
                for ti in range(ng):
                    hb = PB[ti]
                    for c in range(32):
                        mm(hb.ap, x2T3[:, c, ti * 128:(ti + 1) * 128], utb3[:, c, :], c == 0, c == 31, [x2t[ti], utb], [hb])
                pend = list(rounds0) if eb_ < 31 else []
                if eb_ < 31:
                    S.dma("sp", utb3, UTv[:, :, (eb_ + 1) * 512:(eb_ + 2) * 512], writes=[utb])
                    for _ in range(2):
                        if pend:
                            gate_round(eb_ + 1, *pend.pop(0))
                for ti in range(ng):
                    act(gl.ap, PB[ti].ap, AF.Gelu_apprx_tanh, [PB[ti]], [gl])
                    tt("dve", wgb.ap, gl.ap, wt[par][ti].ap, ALU.mult, [gl, wt[par][ti]], [wgb])
                    tr_bf(wgb, wgb.ap, 4, wgT[ti], wgT[ti].ap.rearrange("p (c t) -> p c t", c=4), fixed_bank=PB[3])
                for half in range(2):
                    for ti in range(ng):
                        wT3 = wgT[ti].ap.rearrange("p (c t) -> p c t", c=4)
                        for pr in range(2):
                            bks = [PB[4 + 2 * pr], PB[5 + 2 * pr]]
                            for d2 in range(2):
                                db = 2 * pr + d2
                                for ch in range(4):
                                    mm(bks[d2].ap, wT3[:, ch, :], vb3[:, ch, half * 2048 + db * 512:half * 2048 + (db + 1) * 512],
                                       ch == 0, ch == 3, [wgT[ti], vb], [bks[d2]])
                            a_ = acc[ti].ap[:, half * 2048 + pr * 1024:half * 2048 + (pr + 1) * 1024]
                            src_ = ps_t[:, 2048 + pr * 1024:2048 + (pr + 1) * 1024]
                            if eb_ == 0:
                                cp("act", a_, src_, bks, [acc[ti]])
                            else:
                                cp("act", vtmp[pr].ap, src_, bks, [vtmp[pr]])
                                tt("dve", a_, a_, vtmp[pr].ap, ALU.add, [vtmp[pr], acc[ti]], [acc[ti]])
                        if pend:
                            gate_round(eb_ + 1, *pend.pop(0))
                while pend:
                    gate_round(eb_ + 1, *pend.pop(0))
            for ti in range(ng):
                rows = slice(row0 + (gt0 + ti) * 128, row0 + (gt0 + ti + 1) * 128)
                S.dma("sp", FF[rows, :], acc[ti].ap, reads=[acc[ti]])

    nst = [0]

    def run_stage(f, *a):
        if stop_after is not None and nst[0] >= stop_after:
            return
        nst[0] += 1
        f(*a)

    def exchange(l):
        S.fence()
        RG = [[2 * i, 2 * i + 1] for i in range(NCORE // 2)]
        for src, dst in ((QKw, QK), (Vsw, Vs), (RSw, RS), (ALRw, ALR), (Usw, Us)):
            S.op("pool", lambda e, src=src, dst=dst: e.collective_compute(
                "AllGather", ALU.bypass, replica_groups=RG, ins=[src[:, :]], outs=[dst[:, :]]), dma=True)

    for l in range(NLAYER):
        xsrc = x_in if l == 0 else X1
        dst = X1 if l < NLAYER - 1 else y_out
        run_stage(gen_s5, l)
        run_stage(gen_ut, l)
        if PAIR:
            run_stage(stage_XA, l, 0, xsrc)
            run_stage(exchange, l)
            for sb in range(NSB):
                run_stage(stage_G, l, sb)
            for sb in range(NSB):
                run_stage(stage_S, l, sb)
            run_stage(stage_B12, l, 0)
            run_stage(stage_B3, l, 0)
            run_stage(stage_LN, l, 0, xsrc, "ln1_g", "ln1_b", X2)
            run_stage(stage_B4, l, 0)
            run_stage(stage_LN, l, 0, X2, "ln2_g", "ln2_b", dst)
            continue
        for sb in range(NSB):
            run_stage(stage_XA, l, sb, xsrc)
            run_stage(stage_G, l, sb)
            run_stage(stage_S, l, sb)
            run_stage(stage_B12, l, sb)
            run_stage(stage_B3, l, sb)
            run_stage(stage_LN, l, sb, xsrc, "ln1_g", "ln1_b", X2)
            run_stage(stage_B4, l, sb)
            run_stage(stage_LN, l, sb, X2, "ln2_g", "ln2_b", dst)
    S.emit()
    st.close()
    return nc


_NC_CACHE = {}


def run_cfg(inp, NSB, NPT, NCORE, NLAYER=2, stop_after=None, debug=False):
    key = (NSB, NPT, NLAYER, stop_after, debug)
    if key not in _NC_CACHE:
        _NC_CACHE[key] = build(NSB, NPT, NLAYER, stop_after, debug)
    nc = _NC_CACHE[key]
    TPS = NPT + 1
    NS = NSB * 16
    PL = NPT * 128
    xp = np.asarray(inp["x_prompt"], np.float32)
    xs = np.asarray(inp["x_sample"], np.float32)
    in_maps = []
    for i in range(NCORE):
        parts = []
        for sb in range(NSB):
            parts.append(xp[i, sb * PL:(sb + 1) * PL])
            parts.append(xs[i * NS + sb * 16:i * NS + (sb + 1) * 16].reshape(128, D))
        m = {"x": np.ascontiguousarray(np.concatenate(parts, 0)),
             "sgla": np.ascontiguousarray(inp["state_gla"][:, i * NS:(i + 1) * NS]),
             "ss5re": np.ascontiguousarray(inp["state_s5_re"][:, i * NS:(i + 1) * NS]),
             "ss5im": np.ascontiguousarray(inp["state_s5_im"][:, i * NS:(i + 1) * NS])}
        for n, _ in WSHAPES:
            if ONLY is None or n in ONLY:
                m[n] = np.asarray(inp[n], np.float32)
        in_maps.append(m)
    res = run_bass_kernel_spmd(nc, in_maps, core_ids=list(range(NCORE)))
    R = res.results
    if debug:
        return R
    yp = np.zeros((NCORE, NSB * PL, D), np.float32)
    ys = np.zeros((NCORE * NS, 8, D), np.float32)
    for i in range(NCORE):
        y = np.asarray(R[i]["y"]).reshape(NSB, TPS * 128, D)
        for sb in range(NSB):
            yp[i, sb * PL:(sb + 1) * PL] = y[sb, :PL]
            ys[i * NS + sb * 16:i * NS + (sb + 1) * 16] = y[sb, PL:].reshape(16, 8, D)
    st1 = lambda k: np.stack([np.asarray(R[i][k]) for i in range(NCORE)], 1)
    cat1 = lambda k: np.concatenate([np.asarray(R[i][k]) for i in range(NCORE)], 1)
    return (yp, ys, st1("glap"), st1("s5rep"), st1("s5imp"), cat1("glas"), cat1("s5res"), cat1("s5ims"))


def kernel(**inputs):
    return run_cfg(inputs, 2, 8, 4)


def run_pair(inp, NPT, NPAIR, NLAYER=2):
    NCORE = 2 * NPAIR
    key = ("pair", NPT, NLAYER, NCORE)
    if key not in _NC_CACHE:
        _NC_CACHE[key] = build(2, NPT, NLAYER, None, False, True, NCORE)
    nc = _NC_CACHE[key]
    PL = NPT * 128
    xp = np.asarray(inp["x_prompt"], np.float32)
    xs = np.asarray(inp["x_sample"], np.float32)
    in_maps = []
    for c in range(NCORE):
        i, p = c // 2, c % 2
        x = np.concatenate([xp[i, p * PL:(p + 1) * PL], xs[i * 32 + p * 16:i * 32 + (p + 1) * 16].reshape(128, D)], 0)
        pm = np.zeros((128, 2), np.float32)
        pm[:, p] = 1.0
        m = {"x": np.ascontiguousarray(x), "pm": pm,
             "sgla": np.ascontiguousarray(inp["state_gla"][:, i * 32:(i + 1) * 32]),
             "ss5re": np.ascontiguousarray(inp["state_s5_re"][:, i * 32:(i + 1) * 32]),
             "ss5im": np.ascontiguousarray(inp["state_s5_im"][:, i * 32:(i + 1) * 32])}
        for n, _ in WSHAPES:
            m[n] = np.asarray(inp[n], np.float32)
        in_maps.append(m)
    res = run_bass_kernel_spmd(nc, in_maps, core_ids=list(range(NCORE)))
    R = res.results
    yp = np.zeros((NPAIR, 2 * PL, D), np.float32)
    ys = np.zeros((NPAIR * 32, 8, D), np.float32)
    for c in range(NCORE):
        i, p = c // 2, c % 2
        y = np.asarray(R[c]["y"])
        yp[i, p * PL:(p + 1) * PL] = y[:PL]
        ys[i * 32 + p * 16:i * 32 + (p + 1) * 16] = y[PL:].reshape(16, 8, D)
    st1 = lambda k: np.stack([np.asarray(R[2 * i][k]) for i in range(NPAIR)], 1)
    cat1 = lambda k: np.concatenate([np.asarray(R[2 * i][k]) for i in range(NPAIR)], 1)
    return (yp, ys, st1("glap"), st1("s5rep"), st1("s5imp"), cat1("glas"), cat1("s5res"), cat1("s5ims"))
```
